# Optimizing a Trainium2 kernel written in Bass

```python
import math
import jax, jax.numpy as jnp
from jax import lax
import numpy as np

D_MODEL = 2048
BATCH = 16
SEQ = 256
DEPTH = 2
DEC_BATCH = 8
DEC_SEQ = 1024
PAST_LEN = 512

GRID_W = 64
N_EVEN = (DEPTH + 1) // 2
N_ODD = DEPTH // 2
EPS = 1e-6
N_MOD = 9
D_FF = 5632
POOL_WINDOWS = (2, 4, 8, 16)
N_POOL_GROUPS = 4
POOL_WIDTH = D_MODEL // 2
POOL_GROUP_DIM = POOL_WIDTH // N_POOL_GROUPS
HEAD_DIM = 128
N_Q_HEADS = (D_MODEL // 2) // HEAD_DIM
N_KV_HEADS = 2
Q_PER_KV = N_Q_HEADS // N_KV_HEADS
ATTN_WIDTH = N_Q_HEADS * HEAD_DIM
KV_WIDTH = N_KV_HEADS * HEAD_DIM
MIX_IN = POOL_WIDTH + ATTN_WIDTH + 2 * KV_WIDTH
MIX_OUT = POOL_WIDTH + ATTN_WIDTH
Q_BLOCK = 128
ROPE_THETA = 10000.0
D_INNER = 2 * D_MODEL
SSM_HEADDIM = 64
SSM_HEADS = D_INNER // SSM_HEADDIM
SSM_GROUPS = 8
HEADS_PER_GROUP = SSM_HEADS // SSM_GROUPS
D_STATE = 128
D_CONV = 3
SSM_CHUNK = 128
CONV_DIM = D_INNER + 2 * SSM_GROUPS * D_STATE
SSM_IN = D_INNER + CONV_DIM + 2 * SSM_HEADS

kernel_name = "hybrid_pool_gqa_ssd_diffusion_step"

F32 = jnp.float32


def rmsnorm(x, g):
    x32 = x.astype(F32)
    r = x32 * lax.rsqrt(jnp.mean(x32 * x32, axis=-1, keepdims=True) + EPS)
    return (r * g.astype(F32)).astype(x.dtype)


def modulate_in(h, g_pre, shift, scale):
    return rmsnorm(h, g_pre) * (1 + scale[:, None]) + shift[:, None]


def residual_add(h, y, g_post, gate, w):
    return h + w * gate[:, None] * rmsnorm(y, g_post)


def swiglu(u, w_in, w_out):
    a, b = jnp.split(u @ w_in, 2, axis=-1)
    return (jax.nn.silu(a) * b) @ w_out


def pool_mixer(xp, pool_w_l, pool_scale_l):
    b, L, _ = xp.shape
    x32 = xp.astype(F32)
    cs = jnp.concatenate([jnp.zeros((b, 1, POOL_WIDTH), F32), jnp.cumsum(x32, axis=1)], axis=1)
    cs = cs.reshape(b, L + 1, N_POOL_GROUPS, POOL_GROUP_DIM)
    xg = x32.reshape(b, L, N_POOL_GROUPS, POOL_GROUP_DIM)
    t = jnp.arange(L)
    outs = []
    for gi, w in enumerate(POOL_WINDOWS):
        lo = jnp.clip(t - w // 2, 0, L)
        hi = jnp.clip(t - w // 2 + w, 0, L)
        s = cs[:, hi, gi] - cs[:, lo, gi]
        mean = s / (hi - lo).astype(F32)[None, :, None]
        outs.append(mean - xg[:, :, gi])
    d = jnp.stack(outs, axis=2).astype(xp.dtype)
    y = jnp.einsum('blgc,gcd->blgd', d, pool_w_l).reshape(b, L, POOL_WIDTH)
    return y * pool_scale_l


def rope_2d(x):
    L = x.shape[1]
    rows = L // GRID_W
    row = jnp.repeat(jnp.arange(rows), GRID_W).astype(F32)
    col = jnp.tile(jnp.arange(GRID_W), rows).astype(F32)
    half = HEAD_DIM // 2
    quarter = half // 2
    inv_freq = ROPE_THETA ** (-jnp.arange(quarter, dtype=F32) / quarter)

    def rot(xh, pos):
        ang = pos[:, None] * inv_freq[None]
        cos = jnp.cos(ang)[None, :, None]
        sin = jnp.sin(ang)[None, :, None]
        x1, x2 = xh[..., :quarter], xh[..., quarter:]
        return jnp.concatenate([x1 * cos - x2 * sin, x2 * cos + x1 * sin], axis=-1)

    x32 = x.astype(F32)
    return jnp.concatenate([rot(x32[..., :half], row), rot(x32[..., half:], col)], axis=-1).astype(x.dtype)


def block_attention(q, k, v):
    b, Lq, H, Dh = q.shape
    nb = Lq // Q_BLOCK
    qb = q.reshape(b, nb, Q_BLOCK, N_KV_HEADS, Q_PER_KV, Dh).transpose(1, 0, 2, 3, 4, 5)
    scale = HEAD_DIM ** -0.5

    def one(qblk):
        s = jnp.einsum('bqkgd,bskd->bkgqs', qblk, k, preferred_element_type=F32) * scale
        p = jax.nn.softmax(s, axis=-1)
        o = jnp.einsum('bkgqs,bskd->bqkgd', p.astype(v.dtype), v, preferred_element_type=F32)
        return o.astype(q.dtype)

    o = lax.map(one, qb)
    return o.transpose(1, 0, 2, 3, 4, 5).reshape(b, Lq, H * Dh)


def even_mixer(u, w_in, pool_w_l, pool_scale_l, qk_g, w_out, ctx_kv):
    b, L, _ = u.shape
    p = u @ w_in
    xp, q, k, v = jnp.split(p, [POOL_WIDTH, POOL_WIDTH + ATTN_WIDTH, POOL_WIDTH + ATTN_WIDTH + KV_WIDTH], axis=-1)
    q = rmsnorm(q.reshape(b, L, N_Q_HEADS, HEAD_DIM), qk_g[0])
    k = rmsnorm(k.reshape(b, L, N_KV_HEADS, HEAD_DIM), qk_g[1])
    v = v.reshape(b, L, N_KV_HEADS, HEAD_DIM)
    if ctx_kv is None:
        attn = block_attention(q, k, v)
    else:
        ck, cv = ctx_kv
        k_all = jnp.concatenate([ck.astype(k.dtype), rope_2d(k)], axis=1)
        v_all = jnp.concatenate([cv.astype(v.dtype), v], axis=1)
        attn = block_attention(rope_2d(q), k_all, v_all)
    y = jnp.concatenate([pool_mixer(xp, pool_w_l, pool_scale_l), attn], axis=-1) @ w_out
    return y, k, v


def centred_conv(x, w, bias):
    L = x.shape[1]
    pad = D_CONV // 2
    xpad = jnp.pad(x, ((0, 0), (pad, D_CONV - 1 - pad), (0, 0)))
    out = xpad[:, 0:L] * w[:, 0]
    for j in range(1, D_CONV):
        out = out + xpad[:, j:j + L] * w[:, j]
    return out + bias


def ssd_scan(x, dt, A, Bm, Cm, h0):
    b, L, H, P = x.shape
    nc = L // SSM_CHUNK
    Q = SSM_CHUNK

    def to_chunks(a):
        return jnp.moveaxis(a.reshape(b, nc, Q, *a.shape[2:]), 1, 0)

    xs = to_chunks(x.astype(F32))
    dts = to_chunks(dt)
    Bs = to_chunks(Bm.astype(F32))
    Cs = to_chunks(Cm.astype(F32))
    mask = jnp.tril(jnp.ones((Q, Q), bool))

    def step(h, inp):
        xc, dtc, Bc, Cc = inp
        a = jnp.cumsum(dtc * A, axis=1)
        diff = a[:, :, None, :] - a[:, None, :, :]
        decay = jnp.exp(jnp.where(mask[None, :, :, None], diff, -jnp.inf))
        decay = decay.transpose(0, 3, 1, 2).reshape(b, SSM_GROUPS, HEADS_PER_GROUP, Q, Q)
        cb = jnp.einsum('bign,bjgn->bgij', Cc, Bc)
        wmat = cb[:, :, None] * decay
        xdt = (xc * dtc[..., None]).reshape(b, Q, SSM_GROUPS, HEADS_PER_GROUP, P)
        y_intra = jnp.einsum('bghij,bjghp->bighp', wmat, xdt)
        hg = h.reshape(b, SSM_GROUPS, HEADS_PER_GROUP, P, D_STATE)
        y_state = jnp.einsum('bign,bghpn->bighp', Cc, hg) * jnp.exp(a).reshape(b, Q, SSM_GROUPS, HEADS_PER_GROUP)[..., None]
        dend = jnp.exp(a[:, -1:, :] - a).reshape(b, Q, SSM_GROUPS, HEADS_PER_GROUP)[..., None]
        h_new = h * jnp.exp(a[:, -1])[:, :, None, None] + jnp.einsum('bjgn,bjghp->bghpn', Bc, xdt * dend).reshape(b, H, P, D_STATE)
        return h_new, (y_intra + y_state).reshape(b, Q, H, P)

    hT, ys = lax.scan(step, h0.astype(F32), (xs, dts, Bs, Cs))
    return jnp.moveaxis(ys, 0, 1).reshape(b, L, H, P), hT


def odd_mixer(u, w_in, conv_w, conv_b, dt_bias, A_log, D_skip, norm_g, w_out, h0):
    b, L, _ = u.shape
    z, xbc, dt = jnp.split(u @ w_in, [D_INNER, D_INNER + CONV_DIM], axis=-1)
    xbc = jax.nn.silu(centred_conv(xbc, conv_w, conv_b))
    xs, Bm, Cm = jnp.split(xbc, [D_INNER, D_INNER + SSM_GROUPS * D_STATE], axis=-1)
    xs = xs.reshape(b, L, SSM_HEADS, SSM_HEADDIM)
    Bm = Bm.reshape(b, L, SSM_GROUPS, D_STATE)
    Cm = Cm.reshape(b, L, SSM_GROUPS, D_STATE)
    dt_dir = jax.nn.softplus(dt.astype(F32).reshape(b, L, 2, SSM_HEADS) + dt_bias.astype(F32))
    A = -jnp.exp(A_log.astype(F32))
    Dk = D_skip.astype(F32)
    y_f, h_f = ssd_scan(xs, dt_dir[:, :, 0], A[0], Bm, Cm, h0[:, 0])
    fl = lambda arr: jnp.flip(arr, axis=1)
    y_b, h_b = ssd_scan(fl(xs), fl(dt_dir[:, :, 1]), A[1], fl(Bm), fl(Cm), h0[:, 1])
    x32 = xs.astype(F32)
    y = (y_f + Dk[0][:, None] * x32) + (fl(y_b) + Dk[1][:, None] * x32)
    y = y.reshape(b, L, D_INNER).astype(u.dtype) * jax.nn.silu(z)
    y = rmsnorm(y, norm_g) @ w_out
    return y, jnp.stack([h_f, h_b], axis=1)


def run_trunk(h, cond, ctx_k, ctx_v, ctx_state, ada_w, ada_b, norm_g, ffn_w_in, ffn_w_out,
              mix_w_in, pool_w, pool_scale, qk_norm_g, mix_w_out,
              ssm_w_in, ssm_conv_w, ssm_conv_b, ssm_dt_bias, ssm_A_log, ssm_D, ssm_norm_g, ssm_w_out):
    is_context = ctx_k is None
    ks, vs, ss = [], [], []
    for l in range(DEPTH):
        mods = jnp.split(jax.nn.silu(cond) @ ada_w[l] + ada_b[l], N_MOD, axis=-1)
        g = norm_g[l]
        u = modulate_in(h, g[0], mods[0], mods[1])
        h = residual_add(h, swiglu(u, ffn_w_in[l, 0], ffn_w_out[l, 0]), g[1], mods[2], 0.5)
        u = modulate_in(h, g[2], mods[3], mods[4])
        if l % 2 == 0:
            e = l // 2
            kv = None if is_context else (ctx_k[:, e], ctx_v[:, e])
            y, k, v = even_mixer(u, mix_w_in[e], pool_w[e], pool_scale[e], qk_norm_g[e], mix_w_out[e], kv)
            ks.append(k)
            vs.append(v)
        else:
            o = l // 2
            if is_context:
                h0 = jnp.zeros((h.shape[0], 2, SSM_HEADS, SSM_HEADDIM, D_STATE), F32)
            else:
                h0 = ctx_state[:, o]
            y, hs = odd_mixer(u, ssm_w_in[o], ssm_conv_w[o], ssm_conv_b[o], ssm_dt_bias[o], ssm_A_log[o],
                              ssm_D[o], ssm_norm_g[o], ssm_w_out[o], h0)
            ss.append(hs)
        h = residual_add(h, y, g[3], mods[5], 1.0)
        u = modulate_in(h, g[4], mods[6], mods[7])
        h = residual_add(h, swiglu(u, ffn_w_in[l, 1], ffn_w_out[l, 1]), g[5], mods[8], 0.5)
    return h, ks, vs, ss


def setup_inputs(seed: int = 0) -> dict:
    key = jax.random.key(seed)
    k = jax.random.split(key, 28)
    nrm = lambda kk, shape, s: jax.random.normal(kk, shape, F32) * s
    u_dt = jax.random.uniform(k[25], (N_ODD, 2, SSM_HEADS), F32)
    dt0 = jnp.exp(u_dt * (math.log(0.1) - math.log(0.001)) + math.log(0.001))
    dt_bias = dt0 + jnp.log(-jnp.expm1(-dt0))
    return {
        "x_prompt": nrm(k[0], (BATCH, SEQ, D_MODEL), 1.0),
        "x_sample": nrm(k[1], (DEC_BATCH, DEC_SEQ, D_MODEL), 1.0),
        "cache_k": nrm(k[2], (DEC_BATCH, N_EVEN, PAST_LEN, N_KV_HEADS, HEAD_DIM), 1.0),
        "cache_v": nrm(k[3], (DEC_BATCH, N_EVEN, PAST_LEN, N_KV_HEADS, HEAD_DIM), 1.0),
        "state_ssm": nrm(k[4], (DEC_BATCH, N_ODD, 2, SSM_HEADS, SSM_HEADDIM, D_STATE), 0.1),
        "c": nrm(k[5], (DEC_BATCH, D_MODEL), 1.0),
        "c_ctx": nrm(k[6], (D_MODEL,), 1.0),
        "ada_w": nrm(k[7], (DEPTH, D_MODEL, N_MOD * D_MODEL), 0.5 * D_MODEL ** -0.5),
        "ada_b": nrm(k[8], (DEPTH, N_MOD * D_MODEL), 0.01),
        "norm_g": 1.0 + nrm(k[9], (DEPTH, 6, D_MODEL), 0.05),
        "ffn_w_in": nrm(k[10], (DEPTH, 2, D_MODEL, 2 * D_FF), D_MODEL ** -0.5),
        "ffn_w_out": nrm(k[11], (DEPTH, 2, D_FF, D_MODEL), D_FF ** -0.5),
        "mix_w_in": nrm(k[12], (N_EVEN, D_MODEL, MIX_IN), D_MODEL ** -0.5),
        "pool_w": nrm(k[13], (N_EVEN, N_POOL_GROUPS, POOL_GROUP_DIM, POOL_GROUP_DIM), POOL_GROUP_DIM ** -0.5),
        "pool_scale": 1.0 + nrm(k[14], (N_EVEN, POOL_WIDTH), 0.1),
        "qk_norm_g": 1.0 + nrm(k[15], (N_EVEN, 2, HEAD_DIM), 0.05),
        "mix_w_out": nrm(k[16], (N_EVEN, MIX_OUT, D_MODEL), MIX_OUT ** -0.5),
        "ssm_w_in": nrm(k[17], (N_ODD, D_MODEL, SSM_IN), D_MODEL ** -0.5),
        "ssm_conv_w": nrm(k[18], (N_ODD, CONV_DIM, D_CONV), D_CONV ** -0.5),
        "ssm_conv_b": nrm(k[19], (N_ODD, CONV_DIM), 0.01),
        "ssm_dt_bias": dt_bias,
        "ssm_A_log": jnp.log(jax.random.uniform(k[20], (N_ODD, 2, SSM_HEADS), F32, 1.0, 16.0)),
        "ssm_D": 1.0 + nrm(k[21], (N_ODD, 2, SSM_HEADS), 0.1),
        "ssm_norm_g": 1.0 + nrm(k[22], (N_ODD, D_INNER), 0.05),
        "ssm_w_out": nrm(k[23], (N_ODD, D_INNER, D_MODEL), D_INNER ** -0.5),
    }


def reference(x_prompt, x_sample, cache_k, cache_v, state_ssm, c, c_ctx,
              ada_w, ada_b, norm_g, ffn_w_in, ffn_w_out,
              mix_w_in, pool_w, pool_scale, qk_norm_g, mix_w_out,
              ssm_w_in, ssm_conv_w, ssm_conv_b, ssm_dt_bias, ssm_A_log, ssm_D, ssm_norm_g, ssm_w_out):
    y_prompt, ks, vs, ss = run_trunk(
        x_prompt, c_ctx[None], None, None, None, ada_w, ada_b, norm_g, ffn_w_in, ffn_w_out,
        mix_w_in, pool_w, pool_scale, qk_norm_g, mix_w_out,
        ssm_w_in, ssm_conv_w, ssm_conv_b, ssm_dt_bias, ssm_A_log, ssm_D, ssm_norm_g, ssm_w_out)
    new_k = jnp.stack(ks, axis=1)
    new_v = jnp.stack(vs, axis=1)
    new_ssm = jnp.stack(ss, axis=1)
    y_sample, _, _, _ = run_trunk(
        x_sample, c, cache_k, cache_v, state_ssm, ada_w, ada_b, norm_g, ffn_w_in, ffn_w_out,
        mix_w_in, pool_w, pool_scale, qk_norm_g, mix_w_out,
        ssm_w_in, ssm_conv_w, ssm_conv_b, ssm_dt_bias, ssm_A_log, ssm_D, ssm_norm_g, ssm_w_out)
    return (y_prompt, y_sample, new_k, new_v, new_ssm)
```

```python
import math
from contextlib import ExitStack

import numpy as np
import concourse.bass as bass
import concourse.mybir as mybir
from concourse.bass_utils import run_bass_kernel_spmd

F32 = mybir.dt.float32
BF16 = mybir.dt.bfloat16
AF = mybir.ActivationFunctionType
ALU = mybir.AluOpType
AX = mybir.AxisListType

N_CORES = 8
D = 2048
KC = D // 128
DFF = 5632
JH = DFF // 128
NMOD = 9
EPS = 1e-6
MIX_IN = 2560
D_INNER = 4096
CONV_DIM = 6144
SSM_IN = 10368
NH = 64
HP = 64
NG = 8
NS = 128
PAST = 512
L_CTX = 256
L_SMP = 1024


class TL:
    __slots__ = ("sem", "n", "name")

    def __init__(self, sem, name):
        self.sem = sem
        self.n = 0
        self.name = name


class Buf:
    __slots__ = ("w", "r", "excl")

    def __init__(self):
        self.w = {}
        self.r = {}
        self.excl = False


def bufs(*shape):
    a = np.empty(shape, dtype=object)
    for idx in np.ndindex(*shape):
        a[idx] = Buf()
    return a


def flat(*items):
    out = []
    for it in items:
        if it is None:
            continue
        if isinstance(it, Buf):
            out.append(it)
        elif isinstance(it, np.ndarray):
            out.extend(it.ravel().tolist())
        else:
            for x in it:
                out.extend(flat(x))
    return out


class Prog:
    def __init__(self, nc, es):
        self.nc = nc
        self.es = es
        self.eng = {"pe": nc.tensor, "dve": nc.vector, "act": nc.scalar, "pool": nc.gpsimd, "sp": nc.sync}
        self.tl = {k: TL(es.enter_context(nc.semaphore("s_" + k)), k) for k in self.eng}
        self.seen = {k: {} for k in self.eng}
        self.nsem = 5
        self.out_sems = []
        self.dsems = []
        self.rings = {}
        self.last = {}
        self.ring_pos = {}
        self.fsem = TL(es.enter_context(nc.semaphore("s_fence")), "fence")

    def dsem(self, name):
        self.nsem += 1
        tl = TL(self.es.enter_context(self.nc.semaphore("d_" + name)), name)
        self.dsems.append(tl)
        return tl

    def flush(self, e):
        ins = self.last.get(e)
        if ins is not None:
            tl = self.tl[e]
            ins.then_inc(tl.sem, 1)
            tl.n += 1
            self.last[e] = None

    def _waits(self, e, r, w):
        need = {}
        own = self.tl[e]
        for b in r:
            for tl, v in b.w.items():
                if tl is own and e == "pe":
                    continue
                if need.get(tl, 0) < v:
                    need[tl] = v
            if b.excl:
                for tl, v in b.r.items():
                    if tl is own:
                        continue
                    if need.get(tl, 0) < v:
                        need[tl] = v
        for b in w:
            for tl, v in b.w.items():
                if tl is own and e == "pe":
                    continue
                if need.get(tl, 0) < v:
                    need[tl] = v
            for tl, v in b.r.items():
                if tl is own and e == "pe":
                    continue
                if need.get(tl, 0) < v:
                    need[tl] = v
        seen = self.seen[e]
        h = self.eng[e]
        for tl, v in need.items():
            if seen.get(tl, 0) >= v:
                continue
            if v > tl.n:
                assert tl.name in self.eng and v == tl.n + 1, (tl.name, v, tl.n)
                self.flush(tl.name)
            h.wait_ge(tl.sem, v)
            seen[tl] = v

    def op(self, e, fn, r=(), w=(), inc=True):
        r = flat(r)
        w = flat(w)
        self._waits(e, r, w)
        ins = fn()
        tl = self.tl[e]
        self.last[e] = ins
        v = tl.n + 1
        for b in w:
            b.w[tl] = v
        for b in r:
            if b.r.get(tl, 0) < v:
                b.r[tl] = v
        return ins

    def ring_next(self, e):
        ring = self.rings.setdefault(e, [])
        if len(ring) < 12:
            tl = self.dsem("ring_%s%d" % (e, len(ring)))
            ring.append(tl)
            self.ring_pos[e] = len(ring) - 1
            return tl
        i = (self.ring_pos[e] + 1) % len(ring)
        self.ring_pos[e] = i
        tl = ring[i]
        if self.seen[e].get(tl, 0) < tl.n:
            self.eng[e].wait_ge(tl.sem, tl.n)
            self.seen[e][tl] = tl.n
        return tl

    def dma(self, e, out, in_, ds=None, r=(), w=(), **kw):
        r = flat(r)
        w = flat(w)
        self._waits(e, r, w)
        if ds is None:
            ds = self.ring_next(e)
        ins = self.eng[e].dma_start(out=out, in_=in_, **kw)
        ins.then_inc(ds.sem, 16)
        ds.n += 16
        for b in w:
            b.w[ds] = ds.n
        for b in r:
            b.r[ds] = ds.n
        return ins

    def wait_all(self, e, tls):
        h = self.eng[e]
        for tl in tls:
            if tl.n > 0:
                h.wait_ge(tl.sem, tl.n)


def _host_consts():
    k = np.arange(128)
    ident = np.eye(128, dtype=np.float32)
    U = (k[:, None] <= k[None, :]).astype(np.float32)
    Li = (k[:, None] >= k[None, :]).astype(np.float32)
    Ls = (k[:, None] > k[None, :]).astype(np.float32)
    Us = (k[:, None] < k[None, :]).astype(np.float32)
    R = np.zeros((128, 128), np.float32)
    for p in range(128):
        if (p % 64) < 32:
            R[p, p + 32] = -1.0
        else:
            R[p, p - 32] = 1.0
    Rt = np.ascontiguousarray(R.T)
    cst = np.concatenate([ident, U, Li, Ls, Us, Rt], axis=1)
    t = np.arange(L_SMP)
    row = (t // 64).astype(np.float32)
    col = (t % 64).astype(np.float32)
    inv_freq = (10000.0 ** (-np.arange(32, dtype=np.float32) / 32)).astype(np.float32)
    rope = np.zeros((128, 2, L_SMP), np.float32)
    for p in range(128):
        pos = row if p < 64 else col
        ang = (pos * inv_freq[p % 32]).astype(np.float32)
        rope[p, 0] = np.cos(ang)
        rope[p, 1] = np.sin(ang)
    def cnt(L):
        o = np.zeros((4, L), np.float32)
        tt = np.arange(L)
        for gi, w in enumerate((2, 4, 8, 16)):
            lo = np.clip(tt - w // 2, 0, L)
            hi = np.clip(tt - w // 2 + w, 0, L)
            o[gi] = 1.0 / (hi - lo).astype(np.float32)
        return np.ascontiguousarray(np.broadcast_to(o[None], (128, 4, L)))
    return cst, rope, cnt(L_CTX), cnt(L_SMP)


C_ID, C_U, C_LI, C_LS, C_US, C_RT = range(6)


class Builder:
    def __init__(self, cfg):
        self.cfg = cfg
        self.nc = bass.Bass("TRN2", target_bir_lowering=False)

    def din(self, name, shape, dt=F32):
        return self.nc.dram_tensor(name, list(shape), dt, kind="ExternalInput").ap()

    def dout(self, name, shape, dt=F32):
        return self.nc.dram_tensor(name, list(shape), dt, kind="ExternalOutput").ap()

    def sb(self, es, name, shape, dt):
        return es.enter_context(self.nc.sbuf_tensor("t_" + name, list(shape), dt))

    def build(self):
        nc = self.nc
        cfg = self.cfg
        I = {}
        I["x_ctx"] = self.din("x_ctx", [2 * L_CTX, D])
        I["x_smp"] = self.din("x_smp", [L_SMP, D])
        I["cache_k"] = self.din("cache_k", [PAST, 256])
        I["cache_v"] = self.din("cache_v", [PAST, 256])
        I["state"] = self.din("state", [2, NH * HP, NS])
        I["cond"] = self.din("cond", [2, D])
        I["ada_w"] = self.din("ada_w", [2, D, NMOD * D])
        I["ada_b"] = self.din("ada_b", [2, NMOD * D])
        I["norm_g"] = self.din("norm_g", [2, 6, D])
        I["ffn_w_in"] = self.din("ffn_w_in", [2, 2, D, 2 * DFF])
        I["ffn_w_out"] = self.din("ffn_w_out", [2, 2, DFF, D])
        I["mix_w_in"] = self.din("mix_w_in", [1, D, MIX_IN])
        I["pool_w"] = self.din("pool_w", [1, 4, 256, 256])
        I["pool_scale"] = self.din("pool_scale", [1, 1024])
        I["qk_norm_g"] = self.din("qk_norm_g", [1, 2, 128])
        I["mix_w_out"] = self.din("mix_w_out", [1, D, D])
        I["ssm_w_in"] = self.din("ssm_w_in", [1, D, SSM_IN])
        I["ssm_conv_w"] = self.din("ssm_conv_w", [1, CONV_DIM, 3])
        I["ssm_conv_b"] = self.din("ssm_conv_b", [1, CONV_DIM])
        I["ssm_dt_bias"] = self.din("ssm_dt_bias", [1, 2, NH])
        I["ssm_A_log"] = self.din("ssm_A_log", [1, 2, NH])
        I["ssm_D"] = self.din("ssm_D", [1, 2, NH])
        I["ssm_norm_g"] = self.din("ssm_norm_g", [1, D_INNER])
        I["ssm_w_out"] = self.din("ssm_w_out", [1, D_INNER, D])
        I["cst"] = self.din("cst", [128, 768])
        I["rope"] = self.din("rope", [128, 2, L_SMP])
        I["pcnt_ctx"] = self.din("pcnt_ctx", [128, 4, L_CTX])
        I["pcnt_smp"] = self.din("pcnt_smp", [128, 4, L_SMP])
        O = {}
        O["y_ctx"] = self.dout("y_ctx", [2 * L_CTX, D])
        O["y_smp"] = self.dout("y_smp", [L_SMP, D])
        O["new_k"] = self.dout("new_k", [2 * L_CTX, 256])
        O["new_v"] = self.dout("new_v", [2 * L_CTX, 256])
        O["new_ssm"] = self.dout("new_ssm", [2, 2, NH * HP, NS])
        self.I, self.O = I, O
        self.hscr = nc.dram_tensor("hscr", [128, KC, L_SMP], F32, kind="Internal").ap()
        self.ygscr = nc.dram_tensor("ygscr", [128, 32, L_SMP], BF16, kind="Internal").ap()

        with ExitStack() as es:
            P = Prog(nc, es)
            self.P = P
            self.ps = [es.enter_context(nc.psum_tensor("ps%d" % i, [128, 512], F32)) for i in range(8)]
            self.psb = bufs(8)
            for b_ in self.psb:
                b_.excl = True
            self.ps_rr = 0
            self.held = set()
            self.setup(es)
            for r in cfg.get("passes", (0, 1)):
                self.run_pass(r)
            self.fence()
        return nc

    def bank(self):
        while True:
            i = self.ps_rr
            self.ps_rr = (self.ps_rr + 1) % 7
            if i not in self.held:
                return i

    def setup(self, es):
        nc, P, I = self.nc, self.P, self.I
        sb = self.sb
        self.cst = sb(es, "cst", [128, 768], F32)
        self.cstb = Buf()
        self.ones_f = sb(es, "ones_f", [128, 128], F32)
        self.ones_b = sb(es, "ones_b", [128, 128], BF16)
        self.onesb = Buf()
        self.mods = sb(es, "mods", [128, 2, 2, NMOD * KC], F32)
        self.modsb = Buf()
        self.normgT = sb(es, "normgT", [128, 192], F32)
        self.vecT = sb(es, "vecT", [128, 90], F32)
        self.convw = sb(es, "convw", [128, 48, 3], F32)
        self.gs = sb(es, "gs", [128, 12, KC], F32)
        self.gg = sb(es, "gg", [128, 12, KC], F32)
        self.dtb = sb(es, "dtb", [128, 128], F32)
        self.Aneg = sb(es, "Aneg", [128, 128], F32)
        self.Dsum = sb(es, "Dsum", [128, 64], F32)
        self.smallb = Buf()
        self.epsc = sb(es, "epsc", [128, 1], F32)
        self.onec = sb(es, "onec", [128, 1], F32)
        self.tmp = [sb(es, "tmp%d" % i, [128, 512], F32) for i in range(6)]
        self.tmpb = bufs(6)
        self.rstd = sb(es, "rstd", [128, 512], F32)
        self.rstdb = Buf()
        self.tmp_rr = 0

        P.dma("sp", self.cst[:], I["cst"], None, w=[self.cstb])
        P.op("dve", lambda: nc.vector.memset(self.ones_f[:], 1.0), w=[self.onesb])
        P.op("dve", lambda: nc.vector.memset(self.ones_b[:], 1.0), w=[self.onesb])
        P.op("dve", lambda: nc.vector.memset(self.epsc[:], EPS), w=[self.smallb])
        P.op("dve", lambda: nc.vector.memset(self.onec[:], 1.0), w=[self.smallb])

        with ExitStack() as s2:
            stage = [sb(s2, "stg%d" % i, [128, 128], F32) for i in range(2)]
            stageb = bufs(2)
            adabT = sb(s2, "adabT", [128, 288], F32)
            adabTb = Buf()
            condT = sb(s2, "condT", [128, 32], F32)
            scT = sb(s2, "scT", [128, KC, 2], BF16)
            condb = Buf()
            si = [0]

            def load_T(dst_ap, rows_ap, R, wb):
                k = si[0] % 2
                si[0] += 1
                P.dma("sp", stage[k][0:R, :], rows_ap, None, w=[stageb[k]])
                bk = self.bank()
                P.op("pe", lambda: nc.tensor.transpose(self.ps[bk][:, 0:R], stage[k][0:R, :],
                                                       self.cst[0:R, C_ID * 128:C_ID * 128 + R]),
                     r=[stageb[k], self.cstb], w=[self.psb[bk]])
                P.op("dve", lambda: nc.vector.tensor_copy(out=dst_ap, in_=self.ps[bk][:, 0:R]),
                     r=[self.psb[bk]], w=[wb])

            ab = I["ada_b"].rearrange("l (c p) -> (l c) p", p=128)
            for i in range(3):
                load_T(adabT[:, i * 96:(i + 1) * 96], ab[i * 96:(i + 1) * 96, :], 96, adabTb)
            ng = I["norm_g"].rearrange("l i (c p) -> (l i c) p", p=128)
            for i in range(2):
                load_T(self.normgT[:, i * 96:(i + 1) * 96], ng[i * 96:(i + 1) * 96, :], 96, self.smallb)
            load_T(condT[:, :], I["cond"].rearrange("r (c p) -> (r c) p", p=128), 32, condb)
            k = si[0] % 2
            si[0] += 1
            P.dma("sp", stage[k][0:8, :], I["pool_scale"].rearrange("o (c p) -> (o c) p", p=128), None, w=[stageb[k]])
            P.dma("sp", stage[k][8:10, :], I["qk_norm_g"].rearrange("o i p -> (o i) p"), None, w=[stageb[k]])
            P.dma("sp", stage[k][10:58, :], I["ssm_conv_b"].rearrange("o (c p) -> (o c) p", p=128), None, w=[stageb[k]])
            P.dma("sp", stage[k][58:90, :], I["ssm_norm_g"].rearrange("o (c p) -> (o c) p", p=128), None, w=[stageb[k]])
            bk = self.bank()
            P.op("pe", lambda: nc.tensor.transpose(self.ps[bk][:, 0:90], stage[k][0:90, :],
                                                   self.cst[0:90, C_ID * 128:C_ID * 128 + 90]),
                 r=[stageb[k], self.cstb], w=[self.psb[bk]])
            P.op("dve", lambda: nc.vector.tensor_copy(out=self.vecT[:, :], in_=self.ps[bk][:, 0:90]),
                 r=[self.psb[bk]], w=[self.smallb])
            cw = I["ssm_conv_w"].rearrange("o (c p) j -> p (o c) j", p=128)
            for i in range(4):
                P.dma("sp", self.convw[:, i * 12:(i + 1) * 12, :], cw[:, i * 12:(i + 1) * 12, :], None, w=[self.smallb])
            P.dma("sp", self.dtb[:, :], I["ssm_dt_bias"].rearrange("o a h -> o (a h)").partition_broadcast(128).squeeze(1),
                  None, w=[self.smallb])
            P.dma("sp", self.Aneg[:, :], I["ssm_A_log"].rearrange("o a h -> o (a h)").partition_broadcast(128).squeeze(1),
                  None, w=[self.smallb])
            dtmp = self.tmp[0]
            P.dma("sp", dtmp[:, 0:128], I["ssm_D"].rearrange("o a h -> o (a h)").partition_broadcast(128).squeeze(1),
                  None, w=[self.tmpb[0]])
            P.op("act", lambda: nc.scalar.activation(out=self.Aneg[:, :], in_=self.Aneg[:, :], func=AF.Exp),
                 r=[self.smallb], w=[self.smallb])
            P.op("dve", lambda: nc.vector.tensor_scalar(out=self.Aneg[:, :], in0=self.Aneg[:, :], scalar1=-1.0,
                                                        scalar2=None, op0=ALU.mult),
                 r=[self.smallb], w=[self.smallb])
            P.op("dve", lambda: nc.vector.tensor_tensor(out=self.Dsum[:, :], in0=dtmp[:, 0:64], in1=dtmp[:, 64:128],
                                                        op=ALU.add),
                 r=[self.tmpb[0]], w=[self.smallb])
            P.op("act", lambda: nc.scalar.activation(out=scT[:, :, :].rearrange("p c r -> p r c"),
                                                     in_=condT[:, :].rearrange("p (r c) -> p r c", r=2),
                                                     func=AF.Silu),
                 r=[condb], w=[condb])
            NB = 1024
            aslot = [sb(s2, "aslot%d" % i, [128, KC, NB], BF16) for i in range(2)]
            aslotb = bufs(2)
            asem = [P.dsem("aslot%d" % i) for i in range(2)]
            blocks = [(l, b) for l in range(2) for b in range(NMOD * D // NB)]

            def aload(i):
                l, b = blocks[i]
                src = I["ada_w"][l, :, b * NB:(b + 1) * NB].rearrange("(k p) n -> p k n", p=128)
                P.dma("pool", aslot[i % 2][:, :, :], src, asem[i % 2], w=[aslotb[i % 2]])

            aload(0)
            for i, (l, b) in enumerate(blocks):
                if i + 1 < len(blocks):
                    aload(i + 1)
                sl = aslot[i % 2]
                bk = self.bank()
                pv = self.ps[bk][:, 0:16].rearrange("p (c r) -> p c r", r=2)
                for cc in range(8):
                    for kc in range(KC):
                        last = (cc == 7 and kc == KC - 1)
                        P.op("pe", lambda: nc.tensor.matmul(pv[:, cc, :], lhsT=sl[:, kc, cc * 128:(cc + 1) * 128],
                                                            rhs=scT[:, kc, :], start=(kc == 0), stop=(kc == KC - 1)),
                             r=[aslotb[i % 2], condb], w=[self.psb[bk]], inc=last)
                for r in range(2):
                    P.op("dve", lambda: nc.vector.tensor_tensor(
                        out=self.mods[:, l, r, b * 8:(b + 1) * 8], in0=pv[:, :, r],
                        in1=adabT[:, l * 144 + b * 8:l * 144 + (b + 1) * 8], op=ALU.add),
                        r=[self.psb[bk], adabTb], w=[self.modsb])
            for l in range(2):
                for s in range(3):
                    for r in range(2):
                        idx = (l * 3 + s) * 2 + r
                        wgt = 1.0 if s == 1 else 0.5
                        P.op("dve", lambda: nc.vector.scalar_tensor_tensor(
                            out=self.gs[:, idx, :], in0=self.mods[:, l, r, (3 * s + 1) * KC:(3 * s + 2) * KC],
                            scalar=1.0, in1=self.normgT[:, (l * 6 + 2 * s) * KC:(l * 6 + 2 * s + 1) * KC],
                            op0=ALU.add, op1=ALU.mult), r=[self.modsb, self.smallb], w=[self.modsb])
                        P.op("dve", lambda: nc.vector.scalar_tensor_tensor(
                            out=self.gg[:, idx, :], in0=self.mods[:, l, r, (3 * s + 2) * KC:(3 * s + 3) * KC],
                            scalar=wgt, in1=self.normgT[:, (l * 6 + 2 * s + 1) * KC:(l * 6 + 2 * s + 2) * KC],
                            op0=ALU.mult, op1=ALU.mult), r=[self.modsb, self.smallb], w=[self.modsb])
            self.fence()

    def fence(self):
        P = self.P
        es = ["pe", "dve", "act", "pool"]
        sp = P.eng["sp"]
        for e in es:
            P.flush(e)
        for tl in [P.tl[e] for e in es] + P.dsems:
            if tl.n > P.seen["sp"].get(tl, 0):
                sp.wait_ge(tl.sem, tl.n)
                P.seen["sp"][tl] = tl.n
        sp.sem_inc(P.fsem.sem, 1)
        P.fsem.n += 1
        for e in es:
            P.eng[e].wait_ge(P.fsem.sem, P.fsem.n)
            for tl in [P.tl[o] for o in es + ["sp"]] + P.dsems:
                P.seen[e][tl] = tl.n

    def sh_ap(self, l, s, r, c):
        return self.mods[:, l, r, 3 * s * KC + c:3 * s * KC + c + 1]

    def gs_ap(self, l, s, r, c):
        idx = (l * 3 + s) * 2 + r
        return self.gs[:, idx, c:c + 1]

    def gg_ap(self, l, s, r, c):
        idx = (l * 3 + s) * 2 + r
        return self.gg[:, idx, c:c + 1]

    def newtmp(self):
        i = self.tmp_rr
        self.tmp_rr = (self.tmp_rr + 1) % 6
        return i

    def rstd_from_stats(self, sbk, dim):
        nc, P = self.nc, self.P
        P.op("act", lambda: nc.scalar.activation(out=self.rstd[:, :], in_=self.ps[sbk][:, :], func=AF.Sqrt,
                                                 scale=1.0 / dim, bias=self.epsc[:, 0:1]),
             r=[self.psb[sbk], self.smallb], w=[self.rstdb])
        P.op("dve", lambda: nc.vector.reciprocal(out=self.rstd[:, :], in_=self.rstd[:, :]),
             r=[self.rstdb], w=[self.rstdb])

    def modulate(self, h, hb, tt, l, s, r, u_ap, ub):
        nc, P = self.nc, self.P
        sbk = 7
        for c in range(KC):
            ti = self.newtmp()
            P.op("act", lambda: nc.scalar.activation(out=self.tmp[ti][:, :], in_=h[:, c, tt * 512:(tt + 1) * 512],
                                                     func=AF.Square), r=[hb[c, tt]], w=[self.tmpb[ti]])
            P.op("pe", lambda: nc.tensor.matmul(self.ps[sbk][:, :], lhsT=self.ones_f[:, :], rhs=self.tmp[ti][:, :],
                                                start=(c == 0), stop=(c == KC - 1)),
                 r=[self.tmpb[ti], self.onesb], w=[self.psb[sbk]])
        self.rstd_from_stats(sbk, D)
        for c in range(KC):
            ti = self.newtmp()
            P.op("dve", lambda: nc.vector.tensor_tensor(out=self.tmp[ti][:, :], in0=h[:, c, tt * 512:(tt + 1) * 512],
                                                        in1=self.rstd[:, :], op=ALU.mult),
                 r=[hb[c, tt], self.rstdb], w=[self.tmpb[ti]])
            P.op("act", lambda: nc.scalar.activation(out=u_ap(c), in_=self.tmp[ti][:, :], func=AF.Identity,
                                                     scale=self.gs_ap(l, s, r, c), bias=self.sh_ap(l, s, r, c)),
                 r=[self.tmpb[ti], self.modsb], w=[ub(c)])

    def residual(self, h, hb, tt, l, s, r, y_ap, yb, sbk, dim=D):
        nc, P = self.nc, self.P
        self.rstd_from_stats(sbk, dim)
        for c in range(KC):
            ti = self.newtmp()
            P.op("dve", lambda: nc.vector.tensor_tensor(out=self.tmp[ti][:, :], in0=y_ap(c), in1=self.rstd[:, :],
                                                        op=ALU.mult), r=[yb(c), self.rstdb], w=[self.tmpb[ti]])
            hv = h[:, c, tt * 512:(tt + 1) * 512]
            P.op("dve", lambda: nc.vector.scalar_tensor_tensor(out=hv, in0=self.tmp[ti][:, :],
                                                               scalar=self.gg_ap(l, s, r, c), in1=hv,
                                                               op0=ALU.mult, op1=ALU.add),
                 r=[self.tmpb[ti], hb[c, tt], self.modsb], w=[hb[c, tt]])

    def out_proj_residual(self, w_ap, KCin, rhs_ap, rhsb, slots, slotb, ssem, h, hb, tt, l, s, r, uy, uyb, post_scale=None):
        nc, P = self.nc, self.P
        sbk = 7
        ns = len(slots)

        def wload(oc):
            src = w_ap[:, oc * 128:(oc + 1) * 128].rearrange("(k p) n -> p k n", p=128)
            P.dma("pool", slots[oc % ns][:, 0:KCin, :], src, ssem[oc % ns], w=[slotb[oc % ns]])

        pend = None
        wload(0)
        for oc in range(KC):
            if oc + 1 < KC:
                wload(oc + 1)
            bk = self.bank()
            sl = slots[oc % ns]
            for kc in range(KCin):
                P.op("pe", lambda: nc.tensor.matmul(self.ps[bk][:, :], lhsT=sl[:, kc, :], rhs=rhs_ap(kc),
                                                    start=(kc == 0), stop=(kc == KCin - 1)),
                     r=[slotb[oc % ns], rhsb(kc)], w=[self.psb[bk]], inc=(kc == KCin - 1))
            if pend is not None:
                pend()
            yv = uy[:, oc * 512:(oc + 1) * 512]
            ti = self.newtmp()
            if post_scale is None:
                P.op("act", lambda: nc.scalar.copy(out=yv, in_=self.ps[bk][:, :]), r=[self.psb[bk]], w=[uyb[oc]])
                P.op("act", lambda: nc.scalar.activation(out=self.tmp[ti][:, :], in_=self.ps[bk][:, :], func=AF.Square),
                     r=[self.psb[bk]], w=[self.tmpb[ti]])
            else:
                ps_ap, ps_b = post_scale
                P.op("dve", lambda: nc.vector.tensor_tensor(out=yv, in0=self.ps[bk][:, :], in1=ps_ap, op=ALU.mult),
                     r=[self.psb[bk], ps_b], w=[uyb[oc]])
                P.op("act", lambda: nc.scalar.activation(out=self.tmp[ti][:, :], in_=yv, func=AF.Square),
                     r=[uyb[oc]], w=[self.tmpb[ti]])

            def mk(ti=ti, oc=oc):
                def f():
                    P.op("pe", lambda: nc.tensor.matmul(self.ps[sbk][:, :], lhsT=self.ones_f[:, :],
                                                        rhs=self.tmp[ti][:, :], start=(oc == 0), stop=(oc == KC - 1)),
                         r=[self.tmpb[ti], self.onesb], w=[self.psb[sbk]])
                return f
            pend = mk()
        pend()
        self.residual(h, hb, tt, l, s, r, lambda c: uy[:, c * 512:(c + 1) * 512], lambda c: uyb[c], sbk)

    def load_x(self, es, r, T, h, hb):
        nc, P, I = self.nc, self.P, self.I
        x = I["x_ctx"] if r == 0 else I["x_smp"]
        xs = [self.sb(es, "xs%d_%d" % (i, r), [128, 1024], F32) for i in range(2)]
        xsb = bufs(2)
        n = 0
        for tk in range(T // 128):
            for hf in range(2):
                k = n % 2
                n += 1
                P.dma("sp", xs[k][:, :], x[tk * 128:(tk + 1) * 128, hf * 1024:(hf + 1) * 1024], None, w=[xsb[k]])
                for c4 in range(2):
                    bk = self.bank()
                    for q in range(4):
                        c = c4 * 4 + q
                        P.op("pe", lambda: nc.tensor.transpose(self.ps[bk][:, q * 128:(q + 1) * 128],
                                                               xs[k][:, c * 128:(c + 1) * 128],
                                                               self.cst[:, C_ID * 128:(C_ID + 1) * 128]),
                             r=[xsb[k], self.cstb], w=[self.psb[bk]], inc=(q == 3))
                    c0 = hf * 8 + c4 * 4
                    P.op("dve", lambda: nc.vector.tensor_copy(
                        out=h[:, c0:c0 + 4, tk * 128:(tk + 1) * 128],
                        in_=self.ps[bk][:, :].rearrange("p (q t) -> p q t", q=4)),
                        r=[self.psb[bk]], w=[hb[c0:c0 + 4, tk // 4]])

    def store_y(self, es, r, T, h, hb):
        nc, P, O = self.nc, self.P, self.O
        y = O["y_ctx"] if r == 0 else O["y_smp"]
        ost = [self.sb(es, "os%d_%d" % (i, r), [128, 1024], F32) for i in range(2)]
        ostb = bufs(2)
        n = 0
        for tk in range(T // 128):
            for hf in range(2):
                k = n % 2
                n += 1
                for c4 in range(2):
                    bk = self.bank()
                    for q in range(4):
                        c = hf * 8 + c4 * 4 + q
                        P.op("pe", lambda: nc.tensor.transpose(self.ps[bk][:, q * 128:(q + 1) * 128],
                                                               h[:, c, tk * 128:(tk + 1) * 128],
                                                               self.cst[:, C_ID * 128:(C_ID + 1) * 128]),
                             r=[hb[c, tk // 4], self.cstb], w=[self.psb[bk]], inc=(q == 3))
                    P.op("dve", lambda: nc.vector.tensor_copy(out=ost[k][:, c4 * 512:(c4 + 1) * 512],
                                                              in_=self.ps[bk][:, :]),
                         r=[self.psb[bk]], w=[ostb[k]])
                P.dma("sp", y[tk * 128:(tk + 1) * 128, hf * 1024:(hf + 1) * 1024], ost[k][:, :], None, r=[ostb[k]])

    def h_load(self, h, hb, T, sem):
        P = self.P
        for tt in range(T // 512):
            P.dma("sp", h[:, :, tt * 512:(tt + 1) * 512], self.hscr[:, :, tt * 512:(tt + 1) * 512], None, w=[hb[:, tt]])

    def h_store(self, h, hb, T, sem):
        P = self.P
        for tt in range(T // 512):
            P.dma("sp", self.hscr[:, :, tt * 512:(tt + 1) * 512], h[:, :, tt * 512:(tt + 1) * 512], None, r=[hb[:, tt]])

    def ffn(self, T, l, f, r, uy, uyb, win, winb, wsem, src, dst):
        nc, P, I = self.nc, self.P, self.I
        s = 0 if f == 0 else 2
        w_in = I["ffn_w_in"][l, f]
        w_out = I["ffn_w_out"][l, f]
        uyh = uy[:, :].bitcast(BF16)
        tag = "%d%d%d" % (r, l, f)
        with ExitStack() as es:
            h = self.sb(es, "h_" + tag, [128, KC, T], F32)
            hb = bufs(KC, T // 512)
            if src == "x":
                with ExitStack() as e2:
                    self.load_x(e2, r, T, h, hb)
                    self.fence()
            else:
                self.h_load(h, hb, T, None)
            e3 = ExitStack()
            g = self.sb(e3, "g_" + tag, [128, JH, 512], BF16)
            gb = bufs(JH)
            wout = [self.sb(e3, "wo%d_%s" % (i, tag), [128, JH, 128], BF16) for i in range(2)]
            woutb = bufs(2)
            wosem = [P.dsem("wo%d_%s" % (i, tag)) for i in range(2)]
            for tt in range(0 if self.cfg.get("ffn_io_only") else T // 512):
                self.modulate(h, hb, tt, l, s, r, lambda c: uyh[:, c * 512:(c + 1) * 512], lambda c: uyb[c // 2])

                def wload(j):
                    for half in range(2):
                        srcw = w_in[:, half * DFF + j * 128: half * DFF + (j + 1) * 128].rearrange("(k p) n -> p k n", p=128)
                        P.dma("pool", win[j % 2][:, :, half * 128:(half + 1) * 128], srcw, wsem[j % 2], w=[winb[j % 2]])

                wload(0)
                for j in range(JH):
                    if j + 1 < JH:
                        wload(j + 1)
                    sl = win[j % 2]
                    ba = self.bank()
                    bb = self.bank()
                    for half, bk in ((0, ba), (1, bb)):
                        for kc in range(KC):
                            P.op("pe", lambda: nc.tensor.matmul(self.ps[bk][:, :], lhsT=sl[:, kc, half * 128:(half + 1) * 128],
                                                                rhs=uyh[:, kc * 512:(kc + 1) * 512],
                                                                start=(kc == 0), stop=(kc == KC - 1)),
                                 r=[winb[j % 2], uyb[kc // 2]], w=[self.psb[bk]], inc=(kc == KC - 1))
                    ti = self.newtmp()
                    P.op("act", lambda: nc.scalar.activation(out=self.tmp[ti][:, :], in_=self.ps[ba][:, :], func=AF.Silu),
                         r=[self.psb[ba]], w=[self.tmpb[ti]])
                    P.op("dve", lambda: nc.vector.tensor_tensor(out=g[:, j, :], in0=self.tmp[ti][:, :], in1=self.ps[bb][:, :],
                                                                op=ALU.mult),
                         r=[self.tmpb[ti], self.psb[bb]], w=[gb[j]])
                self.out_proj_residual(w_out, JH, lambda kc: g[:, kc, :], lambda kc: gb[kc], wout, woutb, wosem,
                                       h, hb, tt, l, s, r, uy, uyb)
            self.fence()
            e3.close()
            if dst == "y":
                with ExitStack() as e2:
                    self.store_y(e2, r, T, h, hb)
                    self.fence()
            else:
                self.h_store(h, hb, T, None)
            self.fence()

    def run_pass(self, r):
        nc, P = self.nc, self.P
        cfg = self.cfg
        T = 512 if r == 0 else 1024
        with ExitStack() as es:
            uy = self.sb(es, "uy%d" % r, [128, 8192], F32)
            uyb = bufs(KC)
            win = [self.sb(es, "win%d_%d" % (i, r), [128, KC, 256], BF16) for i in range(2)]
            winb = bufs(2)
            wsem = [P.dsem("win%d_%d" % (i, r)) for i in range(2)]
            phases = []
            for l in range(2):
                if cfg.get("ffn", True):
                    phases.append(("ffn", l, 0))
                if cfg.get("mixer", True) and l in cfg.get("mix_layers", (0, 1)):
                    phases.append(("mix", l, 0))
                if cfg.get("ffn", True):
                    phases.append(("ffn", l, 1))
            assert phases[0][0] == "ffn" and phases[-1][0] == "ffn"
            for i, (kind, l, f) in enumerate(phases):
                if kind == "ffn":
                    self.ffn(T, l, f, r, uy, uyb, win, winb, wsem,
                             "x" if i == 0 else "scr", "y" if i == len(phases) - 1 else "scr")
                elif l == 0:
                    self.even_mixer(T, r, uy, uyb, win, winb, wsem)
                else:
                    self.odd_mixer(T, r, uy, uyb, win, winb, wsem)
            self.fence()

    def bank_hold(self):
        i = self.bank()
        self.held.add(i)
        return i

    def bank_release(self, i):
        self.held.discard(i)

    def mix_out_residual(self, w_ap, KCin, rhs_of_tile, rhsb_of_tile, T, l, r, uy, uyb, win, winb, wsem, tag,
                         post_scale=None):
        nc, P = self.nc, self.P
        with ExitStack() as es:
            ht = [self.sb(es, "ht%d_%s" % (i, tag), [128, KC, 512], F32) for i in range(1)]
            slots = [win[0][:, :, 0:128], win[1][:, :, 0:128]] if KCin <= KC else None
            if slots is None:
                wo = [self.sb(es, "mwo%d_%s" % (i, tag), [128, KCin, 128], BF16) for i in range(2)]
                slots = [wo[0][:, :, :], wo[1][:, :, :]]
                slb = bufs(2)
                slsem = [P.dsem("mwo%d_%s" % (i, tag)) for i in range(2)]
            else:
                slb, slsem = winb, wsem
            for tt in range(T // 512):
                htb = bufs(KC, 1)
                P.dma("sp", ht[0][:, :, :], self.hscr[:, :, tt * 512:(tt + 1) * 512], None, w=[htb])
                self.out_proj_residual(w_ap, KCin, rhs_of_tile(tt), rhsb_of_tile(tt), slots, slb, slsem,
                                       ht[0], htb, 0, l, 1, r, uy, uyb, post_scale=(post_scale(tt) if post_scale else None))
                P.dma("sp", self.hscr[:, :, tt * 512:(tt + 1) * 512], ht[0][:, :, :], None, r=[htb])
                self.fence()

    def even_mixer(self, T, r, uy, uyb, win, winb, wsem):
        nc, P, I, O = self.nc, self.P, self.I, self.O
        l, s = 0, 1
        nseq = 2 if r == 0 else 1
        L = T // nseq
        NT = T // 512
        uyh = uy[:, :].bitcast(BF16)
        w_in = I["mix_w_in"][0]
        ident = self.cst[:, C_ID * 128:(C_ID + 1) * 128]
        tag = "e%d" % r
        NK = L + (PAST if r == 1 else 0)
        NKC = NK // 128
        with ExitStack() as es:
            cat = self.sb(es, "cat" + tag, [128, KC, T], BF16)
            catb = bufs(KC, NT)
            qT = self.sb(es, "qT" + tag, [128, 8, T], BF16)
            qTb = bufs(8, NT)
            kT = self.sb(es, "kT" + tag, [128, 2, nseq * NK], BF16)
            kTb = bufs(2)
            vtok = self.sb(es, "vtok" + tag, [128, nseq * NKC, 256], BF16)
            vtokb = Buf()
            gq = self.sb(es, "gq" + tag, [128, 2], F32)
            gqb = Buf()
            ub = bufs(KC, NT)
            P.op("dve", lambda: nc.vector.tensor_scalar(out=gq[:, 0:1], in0=self.vecT[:, 8:9], scalar1=128.0 ** -0.5,
                                                        scalar2=None, op0=ALU.mult), r=[self.smallb], w=[gqb])
            P.op("dve", lambda: nc.vector.tensor_copy(out=gq[:, 1:2], in_=self.vecT[:, 9:10]), r=[self.smallb], w=[gqb])
            with ExitStack() as e1:
                h = self.sb(e1, "h_" + tag, [128, KC, T], F32)
                hb = bufs(KC, NT)
                self.h_load(h, hb, T, None)
                for tt in range(NT):
                    self.modulate(h, hb, tt, l, s, r, lambda c: uyh[:, c * T + tt * 512:c * T + (tt + 1) * 512],
                                  lambda c: ub[c, tt])
                self.fence()
            with ExitStack() as e1:
                xpp = self.sb(e1, "xpp" + tag, [128, nseq, L + 16], F32)
                xppb = Buf()
                sA = self.sb(e1, "sA" + tag, [128, nseq, L + 16], F32)
                sB = self.sb(e1, "sB" + tag, [128, nseq, L + 16], F32)
                sb_ = Buf()
                dbf = self.sb(e1, "dbf" + tag, [128, 2, T], BF16)
                dbfb = bufs(2)
                pcnt = self.sb(e1, "pcnt" + tag, [128, 4, L], F32)
                pcntb = Buf()
                pw = self.sb(e1, "pw" + tag, [128, 4, 2, 256], BF16)
                pwb = Buf()
                P.dma("sp", pcnt[:, :, :], I["pcnt_ctx"] if r == 0 else I["pcnt_smp"], None, w=[pcntb])
                for gi_ in range(4):
                    P.dma("pool", pw[:, gi_, :, :], I["pool_w"][0, gi_].rearrange("(c p) d -> p c d", p=128), None, w=[pwb])
                P.op("dve", lambda: nc.vector.memset(xpp[:, :, :], 0.0), w=[xppb])
                if r == 1:
                    rope = self.sb(e1, "rope" + tag, [128, 2, T], F32)
                    ropeb = Buf()
                    P.dma("sp", rope[:, :, :], I["rope"], None, w=[ropeb])
                    ck = self.sb(e1, "ck" + tag, [128, 4, 256], F32)
                    ckb = Buf()
                    P.dma("sp", ck[:, :, :], I["cache_k"].rearrange("(b p) f -> p b f", p=128), None, w=[ckb])
                    P.dma("pool", vtok[:, 0:4, :], I["cache_v"].rearrange("(b p) f -> p b f", p=128), None, w=[vtokb])
                    for kh in range(2):
                        bk = self.bank()
                        for blk in range(4):
                            P.op("pe", lambda: nc.tensor.transpose(self.ps[bk][:, blk * 128:(blk + 1) * 128],
                                                                   ck[:, blk, kh * 128:(kh + 1) * 128], ident),
                                 r=[ckb, self.cstb], w=[self.psb[bk]], inc=(blk == 3))
                        P.op("dve", lambda: nc.vector.tensor_copy(out=kT[:, kh, 0:512], in_=self.ps[bk][:, :]),
                             r=[self.psb[bk]], w=[kTb[kh]])
                else:
                    nkst = self.sb(e1, "nkst" + tag, [128, 4, 256], F32)
                    nkstb = bufs(4)
                    nvst = self.sb(e1, "nvst" + tag, [128, 4, 256], F32)
                    nvstb = bufs(4)

                def u_ap(kc, tt):
                    return uyh[:, kc * T + tt * 512:kc * T + (tt + 1) * 512]

                def wload(i):
                    srcw = w_in[:, i * 256:(i + 1) * 256].rearrange("(k p) n -> p k n", p=128)
                    P.dma("pool", win[i % 2][:, :, :], srcw, wsem[i % 2], w=[winb[i % 2]])

                NI = self.cfg.get("even_ni", 10)
                if NI > 0:
                    wload(0)
                for i in range(NI):
                    if i + 1 < NI:
                        wload(i + 1)
                    sl = win[i % 2]
                    if i == 9:
                        for tb in range(T // 128):
                            bk = self.bank()
                            for kc in range(KC):
                                P.op("pe", lambda: nc.tensor.matmul(
                                    self.ps[bk][:, 0:256], lhsT=uyh[:, kc * T + tb * 128:kc * T + (tb + 1) * 128],
                                    rhs=sl[:, kc, :], start=(kc == 0), stop=(kc == KC - 1)),
                                    r=[winb[i % 2], ub[kc, tb // 4]], w=[self.psb[bk]], inc=(kc == KC - 1))
                            if r == 0:
                                kcx = tb
                            else:
                                kcx = 4 + tb
                            if self.cfg.get("v_dbg", 3) >= 2:
                                P.op("dve", lambda: nc.vector.tensor_copy(out=vtok[:, kcx, :], in_=self.ps[bk][:, 0:256]),
                                     r=[self.psb[bk]], w=[vtokb])
                            if r == 0 and self.cfg.get("v_dbg", 3) >= 3:
                                P.op("act", lambda: nc.scalar.copy(out=nvst[:, tb, :], in_=self.ps[bk][:, 0:256]),
                                     r=[self.psb[bk]], w=[nvstb[tb]])
                                P.dma("sp", O["new_v"][tb * 128:(tb + 1) * 128, :], nvst[:, tb, :], None, r=[nvstb[tb]])
                        continue
                    for sub in range(2):
                        cc = i * 2 + sub
                        for tt in range(NT):
                            bk = self.bank()
                            for kc in range(KC):
                                P.op("pe", lambda: nc.tensor.matmul(
                                    self.ps[bk][:, :], lhsT=sl[:, kc, sub * 128:(sub + 1) * 128], rhs=u_ap(kc, tt),
                                    start=(kc == 0), stop=(kc == KC - 1)),
                                    r=[winb[i % 2], ub[kc, tt]], w=[self.psb[bk]], inc=(kc == KC - 1))
                            if cc < 8:
                                if r == 0:
                                    P.op("act", lambda: nc.scalar.copy(
                                        out=xpp[:, :, 8:8 + L], in_=self.ps[bk][:, :].rearrange("p (s t) -> p s t", s=2)),
                                        r=[self.psb[bk]], w=[xppb])
                                else:
                                    P.op("act", lambda: nc.scalar.copy(
                                        out=xpp[:, 0, 8 + tt * 512:8 + (tt + 1) * 512], in_=self.ps[bk][:, :]),
                                        r=[self.psb[bk]], w=[xppb])
                            else:
                                isq = cc < 16
                                hd = cc - 8 if isq else cc - 16
                                t1 = self.newtmp()
                                P.op("act", lambda: nc.scalar.activation(out=self.tmp[t1][:, :], in_=self.ps[bk][:, :],
                                                                         func=AF.Square), r=[self.psb[bk]], w=[self.tmpb[t1]])
                                sbk = self.bank()
                                P.op("pe", lambda: nc.tensor.matmul(self.ps[sbk][:, :], lhsT=self.ones_f[:, :],
                                                                    rhs=self.tmp[t1][:, :], start=True, stop=True),
                                     r=[self.tmpb[t1], self.onesb], w=[self.psb[sbk]])
                                t2 = self.newtmp()
                                P.op("act", lambda: nc.scalar.activation(out=self.tmp[t2][:, :], in_=self.ps[sbk][:, :],
                                                                         func=AF.Sqrt, scale=1.0 / 128, bias=self.epsc[:, 0:1]),
                                     r=[self.psb[sbk], self.smallb], w=[self.tmpb[t2]])
                                P.op("dve", lambda: nc.vector.reciprocal(out=self.tmp[t2][:, :], in_=self.tmp[t2][:, :]),
                                     r=[self.tmpb[t2]], w=[self.tmpb[t2]])
                                t3 = self.newtmp()
                                P.op("dve", lambda: nc.vector.tensor_tensor(out=self.tmp[t3][:, :], in0=self.ps[bk][:, :],
                                                                            in1=self.tmp[t2][:, :], op=ALU.mult),
                                     r=[self.psb[bk], self.tmpb[t2]], w=[self.tmpb[t3]])
                                gcol = gq[:, 0:1] if isq else gq[:, 1:2]
                                if r == 0:
                                    if isq:
                                        P.op("act", lambda: nc.scalar.activation(
                                            out=qT[:, hd, tt * 512:(tt + 1) * 512], in_=self.tmp[t3][:, :],
                                            func=AF.Copy, scale=gcol), r=[self.tmpb[t3], gqb], w=[qTb[hd, tt]])
                                    else:
                                        t4 = self.newtmp()
                                        P.op("act", lambda: nc.scalar.activation(
                                            out=self.tmp[t4][:, :], in_=self.tmp[t3][:, :], func=AF.Copy, scale=gcol),
                                            r=[self.tmpb[t3], gqb], w=[self.tmpb[t4]])
                                        P.op("dve", lambda: nc.vector.tensor_copy(out=kT[:, hd, 0:512], in_=self.tmp[t4][:, :]),
                                             r=[self.tmpb[t4]], w=[kTb[hd]])
                                        tbk = self.bank()
                                        for blk in range(4):
                                            P.op("pe", lambda: nc.tensor.transpose(
                                                self.ps[tbk][:, blk * 128:(blk + 1) * 128],
                                                self.tmp[t4][:, blk * 128:(blk + 1) * 128], ident),
                                                r=[self.tmpb[t4], self.cstb], w=[self.psb[tbk]], inc=(blk == 3))
                                        P.op("dve", lambda: nc.vector.tensor_copy(
                                            out=nkst[:, :, hd * 128:(hd + 1) * 128],
                                            in_=self.ps[tbk][:, :].rearrange("p (b d) -> p b d", b=4)),
                                            r=[self.psb[tbk]], w=[nkstb])
                                        if hd == 1:
                                            for blk in range(4):
                                                P.dma("sp", O["new_k"][blk * 128:(blk + 1) * 128, :], nkst[:, blk, :], None,
                                                      r=[nkstb[blk]])
                                else:
                                    t4 = self.newtmp()
                                    P.op("act", lambda: nc.scalar.activation(
                                        out=self.tmp[t4][:, :], in_=self.tmp[t3][:, :], func=AF.Copy, scale=gcol),
                                        r=[self.tmpb[t3], gqb], w=[self.tmpb[t4]])
                                    rbk = self.bank()
                                    P.op("pe", lambda: nc.tensor.matmul(self.ps[rbk][:, :],
                                                                        lhsT=self.cst[:, C_RT * 128:(C_RT + 1) * 128],
                                                                        rhs=self.tmp[t4][:, :], start=True, stop=True),
                                         r=[self.tmpb[t4], self.cstb], w=[self.psb[rbk]])
                                    t5 = self.newtmp()
                                    P.op("dve", lambda: nc.vector.tensor_tensor(
                                        out=self.tmp[t5][:, :], in0=self.tmp[t4][:, :], in1=rope[:, 0, tt * 512:(tt + 1) * 512],
                                        op=ALU.mult), r=[self.tmpb[t4], ropeb], w=[self.tmpb[t5]])
                                    t6 = self.newtmp()
                                    P.op("dve", lambda: nc.vector.tensor_tensor(
                                        out=self.tmp[t6][:, :], in0=self.ps[rbk][:, :], in1=rope[:, 1, tt * 512:(tt + 1) * 512],
                                        op=ALU.mult), r=[self.psb[rbk], ropeb], w=[self.tmpb[t6]])
                                    if isq:
                                        dst, dstb = qT[:, hd, tt * 512:(tt + 1) * 512], qTb[hd, tt]
                                    else:
                                        dst, dstb = kT[:, hd, PAST + tt * 512:PAST + (tt + 1) * 512], kTb[hd]
                                    P.op("dve", lambda: nc.vector.tensor_tensor(out=dst, in0=self.tmp[t5][:, :],
                                                                                in1=self.tmp[t6][:, :], op=ALU.add),
                                         r=[self.tmpb[t5], self.tmpb[t6]], w=[dstb])
                        if cc < 8:
                            gi = cc // 2
                            wv = (2, 4, 8, 16)[gi]
                            hw = wv // 2
                            W = L + 16
                            cur = xpp
                            step = 1
                            bufs_ = [sA, sB]
                            bi = 0
                            while step < wv:
                                nxt = bufs_[bi]
                                bi ^= 1
                                n_valid = W - 2 * step + 1 if step > 1 else W - 1
                                n_valid = W - (2 * step - 1)
                                P.op("dve", lambda: nc.vector.tensor_tensor(
                                    out=nxt[:, :, 0:n_valid], in0=cur[:, :, 0:n_valid], in1=cur[:, :, step:step + n_valid],
                                    op=ALU.add), r=[xppb, sb_], w=[sb_])
                                cur = nxt
                                step *= 2
                            off = 8 - hw
                            other = bufs_[bi]
                            P.op("dve", lambda: nc.vector.tensor_tensor(
                                out=other[:, :, 0:L], in0=cur[:, :, off:off + L],
                                in1=pcnt[:, gi, :].unsqueeze(1).broadcast_to([128, nseq, L]) if nseq > 1 else pcnt[:, gi:gi + 1, :],
                                op=ALU.mult), r=[sb_, pcntb], w=[sb_])
                            P.op("dve", lambda: nc.vector.tensor_tensor(
                                out=dbf[:, cc % 2, :].rearrange("p (s t) -> p s t", s=nseq), in0=other[:, :, 0:L],
                                in1=xpp[:, :, 8:8 + L], op=ALU.subtract), r=[sb_, xppb], w=[dbfb[cc % 2]])
                            if cc % 2 == 1:
                                for do in range(2):
                                    for tt in range(NT):
                                        bk = self.bank()
                                        for c in range(2):
                                            P.op("pe", lambda: nc.tensor.matmul(
                                                self.ps[bk][:, :], lhsT=pw[:, gi, c, do * 128:(do + 1) * 128],
                                                rhs=dbf[:, c, tt * 512:(tt + 1) * 512], start=(c == 0), stop=(c == 1)),
                                                r=[pwb, dbfb[c]], w=[self.psb[bk]], inc=(c == 1))
                                        oc = gi * 2 + do
                                        P.op("act", lambda: nc.scalar.activation(
                                            out=cat[:, oc, tt * 512:(tt + 1) * 512], in_=self.ps[bk][:, :], func=AF.Copy,
                                            scale=self.vecT[:, oc:oc + 1]), r=[self.psb[bk], self.smallb], w=[catb[oc, tt]])
                self.fence()
            if self.cfg.get("even_stop", 9) < 2:
                return
            with ExitStack() as e2:
                pT = [self.sb(e2, "pT%d%s" % (i, tag), [128, 512], BF16) for i in range(3)]
                pTb = bufs(3)
                pi = 0
                NQ = L if r == 0 else 512
                for sq in range(nseq):
                    for qt in range(L // NQ):
                        q0 = sq * L + qt * NQ
                        for hd in range(8):
                            kh = hd // 4
                            ob = self.bank_hold()
                            db = self.bank_hold()

                            def s_mm(kcx):
                                bk = self.bank()
                                P.op("pe", lambda: nc.tensor.matmul(
                                    self.ps[bk][:, 0:NQ], lhsT=kT[:, kh, sq * NK + kcx * 128:sq * NK + (kcx + 1) * 128],
                                    rhs=qT[:, hd, q0:q0 + NQ], start=True, stop=True),
                                    r=[kTb[kh], qTb[hd, q0 // 512]], w=[self.psb[bk]])
                                return bk
                            sbk_next = s_mm(0)
                            for kcx in range(NKC):
                                sbk = sbk_next
                                k_ = pi % 3
                                pi += 1
                                P.op("act", lambda: nc.scalar.activation(out=pT[k_][:, 0:NQ], in_=self.ps[sbk][:, 0:NQ],
                                                                         func=AF.Exp), r=[self.psb[sbk]], w=[pTb[k_]])
                                if kcx + 1 < NKC:
                                    sbk_next = s_mm(kcx + 1)
                                P.op("pe", lambda: nc.tensor.matmul(
                                    self.ps[ob][:, 0:NQ], lhsT=vtok[:, sq * NKC + kcx, kh * 128:(kh + 1) * 128],
                                    rhs=pT[k_][:, 0:NQ], start=(kcx == 0), stop=(kcx == NKC - 1)),
                                    r=[vtokb, pTb[k_]], w=[self.psb[ob]], inc=False)
                                P.op("pe", lambda: nc.tensor.matmul(
                                    self.ps[db][:, 0:NQ], lhsT=self.ones_b[:, :], rhs=pT[k_][:, 0:NQ],
                                    start=(kcx == 0), stop=(kcx == NKC - 1)),
                                    r=[self.onesb, pTb[k_]], w=[self.psb[db], self.psb[ob]])
                            t1 = self.newtmp()
                            P.op("dve", lambda: nc.vector.reciprocal(out=self.tmp[t1][:, 0:NQ], in_=self.ps[db][:, 0:NQ]),
                                 r=[self.psb[db]], w=[self.tmpb[t1]])
                            P.op("dve", lambda: nc.vector.tensor_tensor(
                                out=cat[:, 8 + hd, q0:q0 + NQ], in0=self.ps[ob][:, 0:NQ], in1=self.tmp[t1][:, 0:NQ],
                                op=ALU.mult), r=[self.psb[ob], self.tmpb[t1]], w=[catb[8 + hd, q0 // 512]])
                            self.bank_release(ob)
                            self.bank_release(db)
                self.fence()
            if self.cfg.get("even_stop", 9) < 3:
                return
            self.mix_out_residual(I["mix_w_out"][0], KC, lambda tt: (lambda kc: cat[:, kc, tt * 512:(tt + 1) * 512]),
                                  lambda tt: (lambda kc: catb[kc, tt]), T, l, r, uy, uyb, win, winb, wsem, tag)
            self.fence()

    def odd_mixer(self, T, r, uy, uyb, win, winb, wsem):
        nc, P, I, O = self.nc, self.P, self.I, self.O
        l, s = 1, 1
        nseq = 2 if r == 0 else 1
        L = T // nseq
        NT = T // 512
        NTB = T // 128
        NCH = L // 128
        uyh = uy[:, :].bitcast(BF16)
        w_in = I["ssm_w_in"][0]
        ident = self.cst[:, C_ID * 128:(C_ID + 1) * 128]
        cU = self.cst[:, C_U * 128:(C_U + 1) * 128]
        cLi = self.cst[:, C_LI * 128:(C_LI + 1) * 128]
        cLs = self.cst[:, C_LS * 128:(C_LS + 1) * 128]
        cUs = self.cst[:, C_US * 128:(C_US + 1) * 128]
        tag = "o%d" % r
        ygscr = self.ygscr
        with ExitStack() as es:
            ub = bufs(KC, NT)
            with ExitStack() as e1:
                h = self.sb(e1, "h_" + tag, [128, KC, T], F32)
                hb = bufs(KC, NT)
                self.h_load(h, hb, T, None)
                for tt in range(NT):
                    self.modulate(h, hb, tt, l, s, r, lambda c: uyh[:, c * T + tt * 512:c * T + (tt + 1) * 512],
                                  lambda c: ub[c, tt])
                self.fence()
            dt = self.sb(es, "dt" + tag, [128, NTB, 128], F32)
            dtA = self.sb(es, "dtA" + tag, [128, NTB, 128], F32)
            ea = self.sb(es, "ea" + tag, [128, NTB, 128], F32)
            dtd = self.sb(es, "dtd" + tag, [128, NTB, 128], F32)
            etot = self.sb(es, "etot" + tag, [128, NTB, 128], F32)
            dtb_ = Buf()
            ssq = self.sb(es, "ssq" + tag, [128, NTB, 8], F32)
            ssqb = Buf()
            rrow = self.sb(es, "rrow" + tag, [128, T], F32)
            rrowb = Buf()

            def u_blk(kc, tb):
                return uyh[:, kc * T + tb * 128:kc * T + (tb + 1) * 128]

            srcw = w_in[:, 10240:10368].rearrange("(k p) n -> p k n", p=128)
            P.dma("pool", win[0][:, :, 0:128], srcw, wsem[0], w=[winb[0]])
            for tb in range(NTB):
                bk = self.bank()
                for kc in range(KC):
                    P.op("pe", lambda: nc.tensor.matmul(self.ps[bk][:, 0:128], lhsT=u_blk(kc, tb), rhs=win[0][:, kc, 0:128],
                                                        start=(kc == 0), stop=(kc == KC - 1)),
                         r=[winb[0], ub[kc, tb // 4]], w=[self.psb[bk]], inc=(kc == KC - 1))
                t1 = self.newtmp()
                xb = self.tmp[t1][:, 0:128]
                P.op("dve", lambda: nc.vector.tensor_tensor(out=xb, in0=self.ps[bk][:, 0:128], in1=self.dtb[:, :], op=ALU.add),
                     r=[self.psb[bk], self.smallb], w=[self.tmpb[t1]])
                t2 = self.newtmp()
                ab = self.tmp[t2][:, 0:128]
                P.op("act", lambda: nc.scalar.activation(out=ab, in_=xb, func=AF.Abs),
                     r=[self.tmpb[t1]], w=[self.tmpb[t2]])
                P.op("act", lambda: nc.scalar.activation(out=ab, in_=ab, func=AF.Exp, scale=-1.0),
                     r=[self.tmpb[t2]], w=[self.tmpb[t2]])
                P.op("act", lambda: nc.scalar.activation(out=ab, in_=ab, func=AF.Ln, bias=self.onec[:, 0:1]),
                     r=[self.tmpb[t2], self.smallb], w=[self.tmpb[t2]])
                P.op("dve", lambda: nc.vector.scalar_tensor_tensor(out=dt[:, tb, :], in0=xb, scalar=0.0, in1=ab,
                                                                   op0=ALU.max, op1=ALU.add),
                     r=[self.tmpb[t1], self.tmpb[t2]], w=[dtb_])
                P.op("dve", lambda: nc.vector.tensor_tensor(out=dtA[:, tb, :], in0=dt[:, tb, :], in1=self.Aneg[:, :], op=ALU.mult),
                     r=[dtb_, self.smallb], w=[dtb_])
                bk2 = self.bank()
                P.op("pe", lambda: nc.tensor.matmul(self.ps[bk2][:, 0:64], lhsT=cU, rhs=dtA[:, tb, 0:64], start=True, stop=True),
                     r=[dtb_, self.cstb], w=[self.psb[bk2]], inc=False)
                P.op("pe", lambda: nc.tensor.matmul(self.ps[bk2][:, 64:128], lhsT=cLi, rhs=dtA[:, tb, 64:128], start=True, stop=True),
                     r=[dtb_, self.cstb], w=[self.psb[bk2]], inc=False)
                P.op("pe", lambda: nc.tensor.matmul(self.ps[bk2][:, 128:256], lhsT=self.ones_f[:, :], rhs=dtA[:, tb, :],
                                                    start=True, stop=True),
                     r=[dtb_, self.onesb], w=[self.psb[bk2]])
                P.op("act", lambda: nc.scalar.activation(out=ea[:, tb, :], in_=self.ps[bk2][:, 0:128], func=AF.Exp),
                     r=[self.psb[bk2]], w=[dtb_])
                P.op("act", lambda: nc.scalar.activation(out=etot[:, tb, :], in_=self.ps[bk2][:, 128:256], func=AF.Exp),
                     r=[self.psb[bk2]], w=[dtb_])
                t3 = self.newtmp()
                ac = self.tmp[t3][:, 0:128]
                P.op("act", lambda: nc.scalar.copy(out=ac, in_=self.ps[bk2][:, 0:128]), r=[self.psb[bk2]], w=[self.tmpb[t3]])
                P.op("dve", lambda: nc.vector.tensor_tensor(out=ac, in0=self.ps[bk2][:, 128:256], in1=ac, op=ALU.subtract),
                     r=[self.psb[bk2], self.tmpb[t3]], w=[self.tmpb[t3]])
                P.op("act", lambda: nc.scalar.activation(out=ac, in_=ac, func=AF.Exp), r=[self.tmpb[t3]], w=[self.tmpb[t3]])
                P.op("dve", lambda: nc.vector.tensor_tensor(out=dtd[:, tb, :], in0=ac, in1=dt[:, tb, :], op=ALU.mult),
                     r=[self.tmpb[t3], dtb_], w=[dtb_])
            with ExitStack() as e2:
                sb = self.sb
                x_tok = sb(e2, "xtok" + tag, [128, NTB, 512], F32)
                xtokb = Buf()
                zs = sb(e2, "zs" + tag, [128, NTB, 512], BF16)
                zsb = Buf()
                yacc = sb(e2, "yacc" + tag, [128, NTB, 512], F32)
                yaccb = bufs(NTB)
                BT = sb(e2, "BT" + tag, [128, T], BF16)
                CT = sb(e2, "CT" + tag, [128, T], BF16)
                bcb = bufs(2)
                Btok = sb(e2, "Btok" + tag, [128, NTB, 128], BF16)
                Btokb = Buf()
                xc = sb(e2, "xc" + tag, [128, nseq, L + 2], F32)
                xcb = Buf()
                cva = sb(e2, "cva" + tag, [128, nseq, L], F32)
                cvb = sb(e2, "cvb" + tag, [128, nseq, L], F32)
                cvab = Buf()
                cvbb = Buf()
                S = [sb(e2, "S%d%s" % (d, tag), [128, 512], F32) for d in range(2)]
                Sb16 = [sb(e2, "Sb%d%s" % (d, tag), [128, 512], BF16) for d in range(2)]
                Sbuf = bufs(2)
                rb = [sb(e2, "rb%d%s" % (i, tag), [128, 8, 128], F32) for i in range(2)]
                rbb = bufs(2)
                dec = [sb(e2, "dec%d%s" % (i, tag), [128, 4, 128], F32) for i in range(2)]
                decb = bufs(2)
                wm = [sb(e2, "wm%d%s" % (i, tag), [128, 8, 128], BF16) for i in range(2)]
                wmb = bufs(2)
                cbm = [sb(e2, "cbm%d%s" % (i, tag), [128, 128], F32) for i in range(2)]
                cbmb = bufs(2)
                xdt = [sb(e2, "xdt%d%s" % (i, tag), [128, 512], BF16) for i in range(2)]
                xdtb = bufs(2)
                xdd = [sb(e2, "xdd%d%s" % (i, tag), [128, 512], BF16) for i in range(2)]
                xddb = bufs(2)
                ygst = sb(e2, "ygst" + tag, [128, 4, T], BF16)
                ygstb = Buf()
                stst = sb(e2, "stst" + tag, [128, 4, 128], F32)
                ststb = Buf()
                P.op("dve", lambda: nc.vector.memset(xc[:, :, :], 0.0), w=[xcb])
                it = 0
                nload = [0]

                def wload(col0, ncol, dcol=0, k=None):
                    if k is None:
                        k = nload[0] % 2
                        nload[0] += 1
                    srcw_ = w_in[:, col0:col0 + ncol].rearrange("(k p) n -> p k n", p=128)
                    P.dma("pool", win[k][:, :, dcol:dcol + ncol], srcw_, wsem[k], w=[winb[k]])
                    return k

                def fm_chunk(k, sub, cidx, kind, dst, dstb):
                    for tt in range(NT):
                        bk = self.bank()
                        for kc in range(KC):
                            P.op("pe", lambda: nc.tensor.matmul(
                                self.ps[bk][:, :], lhsT=win[k][:, kc, sub * 128:(sub + 1) * 128],
                                rhs=uyh[:, kc * T + tt * 512:kc * T + (tt + 1) * 512], start=(kc == 0), stop=(kc == KC - 1)),
                                r=[winb[k], ub[kc, tt]], w=[self.psb[bk]], inc=(kc == KC - 1))
                        if r == 0:
                            P.op("act", lambda: nc.scalar.copy(out=xc[:, :, 1:1 + L],
                                                               in_=self.ps[bk][:, :].rearrange("p (s t) -> p s t", s=2)),
                                 r=[self.psb[bk]], w=[xcb])
                        else:
                            P.op("act", lambda: nc.scalar.copy(out=xc[:, 0, 1 + tt * 512:1 + (tt + 1) * 512], in_=self.ps[bk][:, :]),
                                 r=[self.psb[bk]], w=[xcb])
                    w0 = self.convw[:, cidx, 0:1]
                    w1 = self.convw[:, cidx, 1:2]
                    w2 = self.convw[:, cidx, 2:3]
                    bcol = self.vecT[:, 10 + cidx:11 + cidx]
                    P.op("dve", lambda: nc.vector.tensor_scalar(out=cva[:, :, :], in0=xc[:, :, 0:L], scalar1=w0, scalar2=None,
                                                                op0=ALU.mult), r=[xcb, self.smallb], w=[cvab])
                    P.op("dve", lambda: nc.vector.scalar_tensor_tensor(out=cvb[:, :, :], in0=xc[:, :, 1:L + 1], scalar=w1,
                                                                       in1=cva[:, :, :], op0=ALU.mult, op1=ALU.add),
                         r=[xcb, cvab, self.smallb], w=[cvbb])
                    P.op("dve", lambda: nc.vector.scalar_tensor_tensor(out=cva[:, :, :], in0=xc[:, :, 2:L + 2], scalar=w2,
                                                                       in1=cvb[:, :, :], op0=ALU.mult, op1=ALU.add),
                         r=[xcb, cvbb, self.smallb], w=[cvab])
                    cflat = cva[:, :, :].rearrange("p s t -> p (s t)")
                    vflat = cvb[:, :, :].rearrange("p s t -> p (s t)")
                    P.op("act", lambda: nc.scalar.activation(out=vflat, in_=cflat, func=AF.Silu, bias=bcol),
                         r=[cvab, self.smallb], w=[cvbb])
                    if kind in ("B", "C"):
                        P.op("act", lambda: nc.scalar.copy(out=dst[:, :], in_=vflat), r=[cvbb], w=[dstb])
                    if kind in ("x", "B"):
                        for t4 in range(NTB // 4):
                            bk = self.bank()
                            for q in range(4):
                                tb = t4 * 4 + q
                                P.op("pe", lambda: nc.tensor.transpose(self.ps[bk][:, q * 128:(q + 1) * 128],
                                                                       vflat[:, tb * 128:(tb + 1) * 128], ident),
                                     r=[cvbb, self.cstb], w=[self.psb[bk]], inc=(q == 3))
                            pv = self.ps[bk][:, :].rearrange("p (q f) -> p q f", q=4)
                            if kind == "x":
                                P.op("dve", lambda: nc.vector.tensor_copy(out=x_tok[:, t4 * 4:(t4 + 1) * 4, dst * 128:(dst + 1) * 128],
                                                                          in_=pv), r=[self.psb[bk]], w=[xtokb])
                            else:
                                P.op("dve", lambda: nc.vector.tensor_copy(out=Btok[:, t4 * 4:(t4 + 1) * 4, :], in_=pv),
                                     r=[self.psb[bk]], w=[Btokb])

                for g in range(NG):
                    for half in range(2):
                        k = wload(g * 512 + half * 256, 256)
                        for tb in range(NTB):
                            bk = self.bank()
                            for kc in range(KC):
                                P.op("pe", lambda: nc.tensor.matmul(self.ps[bk][:, 0:256], lhsT=u_blk(kc, tb), rhs=win[k][:, kc, :],
                                                                    start=(kc == 0), stop=(kc == KC - 1)),
                                     r=[winb[k], ub[kc, tb // 4]], w=[self.psb[bk]], inc=(kc == KC - 1))
                            P.op("act", lambda: nc.scalar.activation(out=zs[:, tb, half * 256:(half + 1) * 256],
                                                                     in_=self.ps[bk][:, 0:256], func=AF.Silu),
                                 r=[self.psb[bk]], w=[zsb])
                    for half in range(2):
                        k = wload(4096 + g * 512 + half * 256, 256)
                        for sub in range(2):
                            ch = half * 2 + sub
                            fm_chunk(k, sub, g * 4 + ch, "x", ch, None)
                    k = wload(8192 + g * 128, 128, 0)
                    wload(9216 + g * 128, 128, 128, k=k)
                    fm_chunk(k, 0, 32 + g, "B", BT, bcb[0])
                    fm_chunk(k, 1, 40 + g, "C", CT, bcb[1])
                    for tb in range(NTB):
                        P.op("dve", lambda: nc.vector.tensor_tensor(
                            out=yacc[:, tb, :].rearrange("p (h q) -> p h q", h=8),
                            in0=x_tok[:, tb, :].rearrange("p (h q) -> p h q", h=8),
                            in1=self.Dsum[:, g * 8:(g + 1) * 8].unsqueeze(2).broadcast_to([128, 8, 64]), op=ALU.mult),
                            r=[xtokb, self.smallb], w=[yaccb[tb]])
                    for sq in range(nseq):
                        for d in range(2):
                            Mk = cU if d == 0 else cLi
                            Ml = cLs if d == 0 else cUs
                            hcol = d * 64 + g * 8
                            if r == 1:
                                P.dma("sp", stst[:, :, :], I["state"][d, g * 512:(g + 1) * 512, :].rearrange("(q p) n -> p q n", p=128),
                                      None, w=[ststb])
                                bk = self.bank()
                                for q in range(4):
                                    P.op("pe", lambda: nc.tensor.transpose(self.ps[bk][:, q * 128:(q + 1) * 128], stst[:, q, :], ident),
                                         r=[ststb, self.cstb], w=[self.psb[bk]], inc=(q == 3))
                                P.op("dve", lambda: nc.vector.tensor_copy(out=S[d][:, :], in_=self.ps[bk][:, :]),
                                     r=[self.psb[bk]], w=[Sbuf[d]])
                                P.op("act", lambda: nc.scalar.copy(out=Sb16[d][:, :], in_=self.ps[bk][:, :]),
                                     r=[self.psb[bk]], w=[Sbuf[d]])
                            order = range(NCH) if d == 0 else range(NCH - 1, -1, -1)
                            for ci, c in enumerate(order):
                                tb = sq * NCH + c
                                first = (ci == 0 and r == 0)
                                tok = slice(tb * 128, (tb + 1) * 128)
                                i2 = it % 2
                                it += 1
                                bk = self.bank()
                                P.op("pe", lambda: nc.tensor.matmul(self.ps[bk][:, 0:128], lhsT=BT[:, tok], rhs=CT[:, tok],
                                                                    start=True, stop=True), r=[bcb], w=[self.psb[bk]])
                                P.op("dve", lambda: nc.vector.tensor_tensor(out=cbm[i2][:, :], in0=self.ps[bk][:, 0:128], in1=Mk,
                                                                            op=ALU.mult), r=[self.psb[bk], self.cstb], w=[cbmb[i2]])
                                P.op("dve", lambda: nc.vector.tensor_tensor(
                                    out=rb[i2][:, :, :], in0=Mk.unsqueeze(1).broadcast_to([128, 8, 128]),
                                    in1=dtA[:, tb, hcol:hcol + 8].unsqueeze(2).broadcast_to([128, 8, 128]), op=ALU.mult),
                                    r=[self.cstb, dtb_], w=[rbb[i2]])
                                for hf in range(2):
                                    bk = self.bank()
                                    P.op("pe", lambda: nc.tensor.matmul(
                                        self.ps[bk][:, :], lhsT=Ml, rhs=rb[i2][:, hf * 4:(hf + 1) * 4, :].rearrange("p h i -> p (h i)"),
                                        start=True, stop=True), r=[rbb[i2], self.cstb], w=[self.psb[bk]])
                                    P.op("act", lambda: nc.scalar.activation(out=dec[hf][:, :, :].rearrange("p h i -> p (h i)"),
                                                                             in_=self.ps[bk][:, :], func=AF.Exp),
                                         r=[self.psb[bk]], w=[decb[hf]])
                                    P.op("dve", lambda: nc.vector.tensor_tensor(
                                        out=wm[i2][:, hf * 4:(hf + 1) * 4, :], in0=dec[hf][:, :, :],
                                        in1=cbm[i2][:, :].unsqueeze(1).broadcast_to([128, 4, 128]), op=ALU.mult),
                                        r=[decb[hf], cbmb[i2]], w=[wmb[i2]])
                                xv = x_tok[:, tb, :].rearrange("p (h q) -> p h q", h=8)
                                P.op("dve", lambda: nc.vector.tensor_tensor(
                                    out=xdt[i2][:, :].rearrange("p (h q) -> p h q", h=8), in0=xv,
                                    in1=dt[:, tb, hcol:hcol + 8].unsqueeze(2).broadcast_to([128, 8, 64]), op=ALU.mult),
                                    r=[xtokb, dtb_], w=[xdtb[i2]])
                                P.op("dve", lambda: nc.vector.tensor_tensor(
                                    out=xdd[i2][:, :].rearrange("p (h q) -> p h q", h=8), in0=xv,
                                    in1=dtd[:, tb, hcol:hcol + 8].unsqueeze(2).broadcast_to([128, 8, 64]), op=ALU.mult),
                                    r=[xtokb, dtb_], w=[xddb[i2]])
                                yi = self.bank()
                                for hh in range(8):
                                    P.op("pe", lambda: nc.tensor.matmul(self.ps[yi][:, hh * 64:(hh + 1) * 64], lhsT=wm[i2][:, hh, :],
                                                                        rhs=xdt[i2][:, hh * 64:(hh + 1) * 64], start=True, stop=True),
                                         r=[wmb[i2], xdtb[i2]], w=[self.psb[yi]], inc=(hh == 7))
                                if not first:
                                    ysb = self.bank()
                                    P.op("pe", lambda: nc.tensor.matmul(self.ps[ysb][:, :], lhsT=CT[:, tok], rhs=Sb16[d][:, :],
                                                                        start=True, stop=True), r=[bcb[1], Sbuf[d]], w=[self.psb[ysb]])
                                    t1 = self.newtmp()
                                    P.op("dve", lambda: nc.vector.tensor_tensor(
                                        out=self.tmp[t1][:, :].rearrange("p (h q) -> p h q", h=8),
                                        in0=self.ps[ysb][:, :].rearrange("p (h q) -> p h q", h=8),
                                        in1=ea[:, tb, hcol:hcol + 8].unsqueeze(2).broadcast_to([128, 8, 64]), op=ALU.mult),
                                        r=[self.psb[ysb], dtb_], w=[self.tmpb[t1]])
                                    P.op("dve", lambda: nc.vector.tensor_tensor(out=self.tmp[t1][:, :], in0=self.tmp[t1][:, :],
                                                                                in1=self.ps[yi][:, :], op=ALU.add),
                                         r=[self.tmpb[t1], self.psb[yi]], w=[self.tmpb[t1]])
                                    P.op("dve", lambda: nc.vector.tensor_tensor(out=yacc[:, tb, :], in0=yacc[:, tb, :],
                                                                                in1=self.tmp[t1][:, :], op=ALU.add),
                                         r=[self.tmpb[t1], yaccb[tb]], w=[yaccb[tb]])
                                else:
                                    P.op("dve", lambda: nc.vector.tensor_tensor(out=yacc[:, tb, :], in0=yacc[:, tb, :],
                                                                                in1=self.ps[yi][:, :], op=ALU.add),
                                         r=[self.psb[yi], yaccb[tb]], w=[yaccb[tb]])
                                su = self.bank()
                                P.op("pe", lambda: nc.tensor.matmul(self.ps[su][:, :], lhsT=Btok[:, tb, :], rhs=xdd[i2][:, :],
                                                                    start=True, stop=True), r=[Btokb, xddb[i2]], w=[self.psb[su]])
                                if first:
                                    P.op("dve", lambda: nc.vector.tensor_copy(out=S[d][:, :], in_=self.ps[su][:, :]),
                                         r=[self.psb[su]], w=[Sbuf[d]])
                                else:
                                    P.op("dve", lambda: nc.vector.tensor_tensor(
                                        out=S[d][:, :].rearrange("p (h q) -> p h q", h=8),
                                        in0=S[d][:, :].rearrange("p (h q) -> p h q", h=8),
                                        in1=etot[:, tb, hcol:hcol + 8].unsqueeze(2).broadcast_to([128, 8, 64]), op=ALU.mult),
                                        r=[Sbuf[d], dtb_], w=[Sbuf[d]])
                                    P.op("dve", lambda: nc.vector.tensor_tensor(out=S[d][:, :], in0=S[d][:, :], in1=self.ps[su][:, :],
                                                                                op=ALU.add), r=[Sbuf[d], self.psb[su]], w=[Sbuf[d]])
                                P.op("act", lambda: nc.scalar.copy(out=Sb16[d][:, :], in_=S[d][:, :]), r=[Sbuf[d]], w=[Sbuf[d]])
                            if r == 0:
                                bk = self.bank()
                                for q in range(4):
                                    P.op("pe", lambda: nc.tensor.transpose(self.ps[bk][:, q * 128:(q + 1) * 128],
                                                                           S[d][:, q * 128:(q + 1) * 128], ident),
                                         r=[Sbuf[d], self.cstb], w=[self.psb[bk]], inc=(q == 3))
                                P.op("dve", lambda: nc.vector.tensor_copy(out=stst[:, :, :],
                                                                          in_=self.ps[bk][:, :].rearrange("p (q n) -> p q n", q=4)),
                                     r=[self.psb[bk]], w=[ststb])
                                P.dma("sp", O["new_ssm"][sq, d, g * 512:(g + 1) * 512, :].rearrange("(q p) n -> p q n", p=128),
                                      stst[:, :, :], None, r=[ststb])
                    for tb in range(NTB):
                        P.op("dve", lambda: nc.vector.tensor_tensor(out=yacc[:, tb, :], in0=yacc[:, tb, :], in1=zs[:, tb, :],
                                                                    op=ALU.mult), r=[yaccb[tb], zsb], w=[yaccb[tb]])
                        t1 = self.newtmp()
                        P.op("act", lambda: nc.scalar.activation(out=self.tmp[t1][:, :], in_=yacc[:, tb, :], func=AF.Square,
                                                                 accum_out=ssq[:, tb, g:g + 1]),
                             r=[yaccb[tb]], w=[self.tmpb[t1], ssqb])
                    for ch in range(4):
                        for t4 in range(NTB // 4):
                            bk = self.bank()
                            for q in range(4):
                                tb = t4 * 4 + q
                                P.op("pe", lambda: nc.tensor.transpose(self.ps[bk][:, q * 128:(q + 1) * 128],
                                                                       yacc[:, tb, ch * 128:(ch + 1) * 128], ident),
                                     r=[yaccb[tb], self.cstb], w=[self.psb[bk]], inc=(q == 3))
                            P.op("act", lambda: nc.scalar.activation(out=ygst[:, ch, t4 * 512:(t4 + 1) * 512], in_=self.ps[bk][:, :],
                                                                     func=AF.Copy, scale=self.vecT[:, 58 + g * 4 + ch:59 + g * 4 + ch]),
                                 r=[self.psb[bk], self.smallb], w=[ygstb])
                    P.dma("sp", ygscr[:, g * 4:(g + 1) * 4, 0:T], ygst[:, :, :], None, r=[ygstb])
                self.fence()
            t1 = self.newtmp()
            rt = self.tmp[t1][:, 0:NTB]
            P.op("dve", lambda: nc.vector.tensor_reduce(out=rt, in_=ssq[:, :, :], axis=AX.X, op=ALU.add),
                 r=[ssqb], w=[self.tmpb[t1]])
            P.op("act", lambda: nc.scalar.activation(out=rt, in_=rt, func=AF.Sqrt, scale=1.0 / D_INNER, bias=self.epsc[:, 0:1]),
                 r=[self.tmpb[t1], self.smallb], w=[self.tmpb[t1]])
            P.op("dve", lambda: nc.vector.reciprocal(out=rt, in_=rt), r=[self.tmpb[t1]], w=[self.tmpb[t1]])
            for t4 in range(NTB // 4):
                t2 = self.newtmp()
                for q in range(4):
                    tb = t4 * 4 + q
                    P.op("dve", lambda: nc.vector.tensor_scalar(out=self.tmp[t2][:, q * 128:(q + 1) * 128], in0=ident,
                                                                scalar1=rt[:, tb:tb + 1], scalar2=None, op0=ALU.mult),
                         r=[self.tmpb[t1], self.cstb], w=[self.tmpb[t2]])
                bk = self.bank()
                P.op("pe", lambda: nc.tensor.matmul(self.ps[bk][:, :], lhsT=self.ones_f[:, :], rhs=self.tmp[t2][:, :],
                                                    start=True, stop=True), r=[self.tmpb[t2], self.onesb], w=[self.psb[bk]])
                P.op("act", lambda: nc.scalar.copy(out=rrow[:, t4 * 512:(t4 + 1) * 512], in_=self.ps[bk][:, :]),
                     r=[self.psb[bk]], w=[rrowb])
            self.fence()
            with ExitStack() as e3:
                ygt = self.sb(e3, "ygt" + tag, [128, 32, 512], BF16)
                ygtb_holder = [None]

                def rhs_of_tile(tt):
                    b_ = Buf()
                    ygtb_holder[0] = b_
                    P.dma("sp", ygt[:, :, :], ygscr[:, :, tt * 512:(tt + 1) * 512], None, w=[b_])
                    return lambda kc: ygt[:, kc, :]

                def rhsb_of_tile(tt):
                    return lambda kc: ygtb_holder[0]

                self.mix_out_residual(I["ssm_w_out"][0], 32, rhs_of_tile, rhsb_of_tile, T, l, r, uy, uyb, win, winb, wsem, tag,
                                      post_scale=lambda tt: (rrow[:, tt * 512:(tt + 1) * 512], rrowb))
                self.fence()


_CONSTS = None


def make_in_maps(inp):
    global _CONSTS
    if _CONSTS is None:
        _CONSTS = _host_consts()
    cst, rope, pc_ctx, pc_smp = _CONSTS
    f = lambda a: np.ascontiguousarray(np.asarray(a, dtype=np.float32))
    shared = {k: f(inp[k]) for k in ("ada_w", "ada_b", "norm_g", "ffn_w_in", "ffn_w_out", "mix_w_in", "pool_w",
                                     "pool_scale", "qk_norm_g", "mix_w_out", "ssm_w_in", "ssm_conv_w", "ssm_conv_b",
                                     "ssm_dt_bias", "ssm_A_log", "ssm_D", "ssm_norm_g", "ssm_w_out")}
    shared.update(cst=cst, rope=rope, pcnt_ctx=pc_ctx, pcnt_smp=pc_smp)
    xp = f(inp["x_prompt"])
    xs = f(inp["x_sample"])
    ck = f(inp["cache_k"])
    cv = f(inp["cache_v"])
    st = f(inp["state_ssm"])
    c = f(inp["c"])
    cc = f(inp["c_ctx"])
    maps = []
    for i in range(N_CORES):
        m = dict(shared)
        m["x_ctx"] = xp[2 * i:2 * i + 2].reshape(2 * L_CTX, D)
        m["x_smp"] = xs[i]
        m["cache_k"] = ck[i, 0].reshape(PAST, 256)
        m["cache_v"] = cv[i, 0].reshape(PAST, 256)
        m["state"] = st[i, 0].reshape(2, NH * HP, NS)
        m["cond"] = np.stack([cc, c[i]], axis=0)
        maps.append(m)
    return maps


def run(inp, cfg=None, n_cores=N_CORES):
    cfg = cfg or {}
    b = Builder(cfg)
    nc = b.build()
    maps = make_in_maps(inp)[:n_cores]
    res = run_bass_kernel_spmd(nc, maps, core_ids=list(range(n_cores)))
    return res.results


def kernel(**inputs):
    rs = run(inputs)
    y_prompt = np.stack([r["y_ctx"] for r in rs]).reshape(16, L_CTX, D)
    y_sample = np.stack([r["y_smp"] for r in rs]).reshape(8, L_SMP, D)
    new_k = np.stack([r["new_k"] for r in rs]).reshape(16, 1, L_CTX, 2, 128)
    new_v = np.stack([r["new_v"] for r in rs]).reshape(16, 1, L_CTX, 2, 128)
    new_ssm = np.stack([r["new_ssm"] for r in rs]).reshape(16, 1, 2, NH, HP, NS)
    return (y_prompt.astype(np.float32), y_sample.astype(np.float32), new_k.astype(np.float32),
            new_v.astype(np.float32), new_ssm.astype(np.float32))
```

```python
import math
from contextlib import ExitStack

import numpy as np
import concourse.bass as bass
import concourse.mybir as mybir
from concourse.bass_utils import run_bass_kernel_spmd

F32 = mybir.dt.float32
BF16 = mybir.dt.bfloat16
AF = mybir.ActivationFunctionType
ALU = mybir.AluOpType
AX = mybir.AxisListType

N_CORES = 8
D = 2048
KC = D // 128
DFF = 5632
JH = DFF // 128
NMOD = 9
EPS = 1e-6
MIX_IN = 2560
D_INNER = 4096
CONV_DIM = 6144
SSM_IN = 10368
NH = 64
HP = 64
NG = 8
NS = 128
PAST = 512
L_CTX = 256
L_SMP = 1024


class TL:
    __slots__ = ("sem", "n", "name")

    def __init__(self, sem, name):
        self.sem = sem
        self.n = 0
        self.name = name


class Buf:
    __slots__ = ("w", "r", "excl")

    def __init__(self):
        self.w = {}
        self.r = {}
        self.excl = False


def bufs(*shape):
    a = np.empty(shape, dtype=object)
    for idx in np.ndindex(*shape):
        a[idx] = Buf()
    return a


def flat(*items):
    out = []
    for it in items:
        if it is None:
            continue
        if isinstance(it, Buf):
            out.append(it)
        elif isinstance(it, np.ndarray):
            out.extend(it.ravel().tolist())
        else:
            for x in it:
                out.extend(flat(x))
    return out


class Prog:
    def __init__(self, nc, es):
        self.nc = nc
        self.es = es
        self.eng = {"pe": nc.tensor, "dve": nc.vector, "act": nc.scalar, "pool": nc.gpsimd, "sp": nc.sync}
        self.tl = {k: TL(es.enter_context(nc.semaphore("s_" + k)), k) for k in self.eng}
        self.seen = {k: {} for k in self.eng}
        self.nsem = 5
        self.out_sems = []
        self.dsems = []
        self.rings = {}
        self.last = {}
        self.ring_pos = {}
        self.fsem = TL(es.enter_context(nc.semaphore("s_fence")), "fence")

    def dsem(self, name):
        self.nsem += 1
        tl = TL(self.es.enter_context(self.nc.semaphore("d_" + name)), name)
        self.dsems.append(tl)
        return tl

    def flush(self, e):
        ins = self.last.get(e)
        if ins is not None:
            tl = self.tl[e]
            ins.then_inc(tl.sem, 1)
            tl.n += 1
            self.last[e] = None

    def _waits(self, e, r, w):
        need = {}
        own = self.tl[e]
        for b in r:
            for tl, v in b.w.items():
                if tl is own and e == "pe":
                    continue
                if need.get(tl, 0) < v:
                    need[tl] = v
            if b.excl:
                for tl, v in b.r.items():
                    if tl is own:
                        continue
                    if need.get(tl, 0) < v:
                        need[tl] = v
        for b in w:
            for tl, v in b.w.items():
                if tl is own and e == "pe":
                    continue
                if need.get(tl, 0) < v:
                    need[tl] = v
            for tl, v in b.r.items():
                if tl is own and e == "pe":
                    continue
                if need.get(tl, 0) < v:
                    need[tl] = v
        seen = self.seen[e]
        h = self.eng[e]
        for tl, v in need.items():
            if seen.get(tl, 0) >= v:
                continue
            if v > tl.n:
                assert tl.name in self.eng and v == tl.n + 1, (tl.name, v, tl.n)
                self.flush(tl.name)
            h.wait_ge(tl.sem, v)
            seen[tl] = v

    def op(self, e, fn, r=(), w=(), inc=True):
        r = flat(r)
        w = flat(w)
        self._waits(e, r, w)
        ins = fn()
        tl = self.tl[e]
        self.last[e] = ins
        v = tl.n + 1
        for b in w:
            b.w[tl] = v
        for b in r:
            if b.r.get(tl, 0) < v:
                b.r[tl] = v
        return ins

    def ring_next(self, e):
        ring = self.rings.setdefault(e, [])
        if len(ring) < 12:
            tl = self.dsem("ring_%s%d" % (e, len(ring)))
            ring.append(tl)
            self.ring_pos[e] = len(ring) - 1
            return tl
        i = (self.ring_pos[e] + 1) % len(ring)
        self.ring_pos[e] = i
        tl = ring[i]
        if self.seen[e].get(tl, 0) < tl.n:
            self.eng[e].wait_ge(tl.sem, tl.n)
            self.seen[e][tl] = tl.n
        return tl

    def dma(self, e, out, in_, ds=None, r=(), w=(), **kw):
        r = flat(r)
        w = flat(w)
        self._waits(e, r, w)
        if ds is None:
            ds = self.ring_next(e)
        ins = self.eng[e].dma_start(out=out, in_=in_, **kw)
        ins.then_inc(ds.sem, 16)
        ds.n += 16
        for b in w:
            b.w[ds] = ds.n
        for b in r:
            b.r[ds] = ds.n
        return ins

    def wait_all(self, e, tls):
        h = self.eng[e]
        for tl in tls:
            if tl.n > 0:
                h.wait_ge(tl.sem, tl.n)


def _host_consts():
    k = np.arange(128)
    ident = np.eye(128, dtype=np.float32)
    U = (k[:, None] <= k[None, :]).astype(np.float32)
    Li = (k[:, None] >= k[None, :]).astype(np.float32)
    Ls = (k[:, None] > k[None, :]).astype(np.float32)
    Us = (k[:, None] < k[None, :]).astype(np.float32)
    R = np.zeros((128, 128), np.float32)
    for p in range(128):
        if (p % 64) < 32:
            R[p, p + 32] = -1.0
        else:
            R[p, p - 32] = 1.0
    Rt = np.ascontiguousarray(R.T)
    cst = np.concatenate([ident, U, Li, Ls, Us, Rt], axis=1)
    t = np.arange(L_SMP)
    row = (t // 64).astype(np.float32)
    col = (t % 64).astype(np.float32)
    inv_freq = (10000.0 ** (-np.arange(32, dtype=np.float32) / 32)).astype(np.float32)
    rope = np.zeros((128, 2, L_SMP), np.float32)
    for p in range(128):
        pos = row if p < 64 else col
        ang = (pos * inv_freq[p % 32]).astype(np.float32)
        rope[p, 0] = np.cos(ang)
        rope[p, 1] = np.sin(ang)
    def cnt(L):
        o = np.zeros((4, L), np.float32)
        tt = np.arange(L)
        for gi, w in enumerate((2, 4, 8, 16)):
            lo = np.clip(tt - w // 2, 0, L)
            hi = np.clip(tt - w // 2 + w, 0, L)
            o[gi] = 1.0 / (hi - lo).astype(np.float32)
        return np.ascontiguousarray(np.broadcast_to(o[None], (128, 4, L)))
    return cst, rope, cnt(L_CTX), cnt(L_SMP)


C_ID, C_U, C_LI, C_LS, C_US, C_RT = range(6)


class Builder:
    def __init__(self, cfg):
        self.cfg = cfg
        self.nc = bass.Bass("TRN2", target_bir_lowering=False)

    def din(self, name, shape, dt=F32):
        return self.nc.dram_tensor(name, list(shape), dt, kind="ExternalInput").ap()

    def dout(self, name, shape, dt=F32):
        return self.nc.dram_tensor(name, list(shape), dt, kind="ExternalOutput").ap()

    def sb(self, es, name, shape, dt):
        return es.enter_context(self.nc.sbuf_tensor("t_" + name, list(shape), dt))

    def build(self):
        nc = self.nc
        cfg = self.cfg
        I = {}
        I["x_ctx"] = self.din("x_ctx", [2 * L_CTX, D])
        I["x_smp"] = self.din("x_smp", [L_SMP, D])
        I["cache_k"] = self.din("cache_k", [PAST, 256])
        I["cache_v"] = self.din("cache_v", [PAST, 256])
        I["state"] = self.din("state", [2, NH * HP, NS])
        I["cond"] = self.din("cond", [2, D])
        I["ada_w"] = self.din("ada_w", [2, D, NMOD * D])
        I["ada_b"] = self.din("ada_b", [2, NMOD * D])
        I["norm_g"] = self.din("norm_g", [2, 6, D])
        I["ffn_w_in"] = self.din("ffn_w_in", [2, 2, D, 2 * DFF])
        I["ffn_w_out"] = self.din("ffn_w_out", [2, 2, DFF, D])
        I["mix_w_in"] = self.din("mix_w_in", [1, D, MIX_IN])
        I["pool_w"] = self.din("pool_w", [1, 4, 256, 256])
        I["pool_scale"] = self.din("pool_scale", [1, 1024])
        I["qk_norm_g"] = self.din("qk_norm_g", [1, 2, 128])
        I["mix_w_out"] = self.din("mix_w_out", [1, D, D])
        I["ssm_w_in"] = self.din("ssm_w_in", [1, D, SSM_IN])
        I["ssm_conv_w"] = self.din("ssm_conv_w", [1, CONV_DIM, 3])
        I["ssm_conv_b"] = self.din("ssm_conv_b", [1, CONV_DIM])
        I["ssm_dt_bias"] = self.din("ssm_dt_bias", [1, 2, NH])
        I["ssm_A_log"] = self.din("ssm_A_log", [1, 2, NH])
        I["ssm_D"] = self.din("ssm_D", [1, 2, NH])
        I["ssm_norm_g"] = self.din("ssm_norm_g", [1, D_INNER])
        I["ssm_w_out"] = self.din("ssm_w_out", [1, D_INNER, D])
        I["cst"] = self.din("cst", [128, 768])
        I["rope"] = self.din("rope", [128, 2, L_SMP])
        I["pcnt_ctx"] = self.din("pcnt_ctx", [128, 4, L_CTX])
        I["pcnt_smp"] = self.din("pcnt_smp", [128, 4, L_SMP])
        O = {}
        O["y_ctx"] = self.dout("y_ctx", [2 * L_CTX, D])
        O["y_smp"] = self.dout("y_smp", [L_SMP, D])
        O["new_k"] = self.dout("new_k", [2 * L_CTX, 256])
        O["new_v"] = self.dout("new_v", [2 * L_CTX, 256])
        O["new_ssm"] = self.dout("new_ssm", [2, 2, NH * HP, NS])
        self.I, self.O = I, O
        self.hscr = nc.dram_tensor("hscr", [128, KC, L_SMP], F32, kind="Internal").ap()
        self.ygscr = nc.dram_tensor("ygscr", [128, 32, L_SMP], BF16, kind="Internal").ap()

        with ExitStack() as es:
            P = Prog(nc, es)
            self.P = P
            self.ps = [es.enter_context(nc.psum_tensor("ps%d" % i, [128, 512], F32)) for i in range(8)]
            self.psb = bufs(8)
            for b_ in self.psb:
                b_.excl = True
            self.ps_rr = 0
            self.xs_rr = 0
            self.held = set()
            self.setup(es)
            for r in cfg.get("passes", (0, 1)):
                self.run_pass(r)
            self.fence()
        return nc

    def bank(self):
        while True:
            i = self.ps_rr
            self.ps_rr = (self.ps_rr + 1) % 7
            if i not in self.held:
                return i

    def setup(self, es):
        nc, P, I = self.nc, self.P, self.I
        sb = self.sb
        self.cst = sb(es, "cst", [128, 768], F32)
        self.cstb = Buf()
        self.ones_f = sb(es, "ones_f", [128, 128], F32)
        self.ones_b = sb(es, "ones_b", [128, 128], BF16)
        self.onesb = Buf()
        self.mods = sb(es, "mods", [128, 2, 2, NMOD * KC], F32)
        self.modsb = Buf()
        self.normgT = sb(es, "normgT", [128, 192], F32)
        self.vecT = sb(es, "vecT", [128, 90], F32)
        self.convw = sb(es, "convw", [128, 48, 3], F32)
        self.gs = sb(es, "gs", [128, 12, KC], F32)
        self.gg = sb(es, "gg", [128, 12, KC], F32)
        self.dtb = sb(es, "dtb", [128, 128], F32)
        self.Aneg = sb(es, "Aneg", [128, 128], F32)
        self.Dsum = sb(es, "Dsum", [128, 64], F32)
        self.smallb = Buf()
        self.epsc = sb(es, "epsc", [128, 1], F32)
        self.onec = sb(es, "onec", [128, 1], F32)
        self.tmp = [sb(es, "tmp%d" % i, [128, 512], F32) for i in range(6)]
        self.tmpb = bufs(6)
        self.rstd = sb(es, "rstd", [128, 512], F32)
        self.rstdb = Buf()
        self.tmp_rr = 0

        P.dma("sp", self.cst[:], I["cst"], None, w=[self.cstb])
        P.op("dve", lambda: nc.vector.memset(self.ones_f[:], 1.0), w=[self.onesb])
        P.op("dve", lambda: nc.vector.memset(self.ones_b[:], 1.0), w=[self.onesb])
        P.op("dve", lambda: nc.vector.memset(self.epsc[:], EPS), w=[self.smallb])
        P.op("dve", lambda: nc.vector.memset(self.onec[:], 1.0), w=[self.smallb])

        with ExitStack() as s2:
            stage = [sb(s2, "stg%d" % i, [128, 128], F32) for i in range(2)]
            stageb = bufs(2)
            adabT = sb(s2, "adabT", [128, 288], F32)
            adabTb = Buf()
            condT = sb(s2, "condT", [128, 32], F32)
            scT = sb(s2, "scT", [128, KC, 2], BF16)
            condb = Buf()
            si = [0]

            def load_T(dst_ap, rows_ap, R, wb):
                k = si[0] % 2
                si[0] += 1
                P.dma("sp", stage[k][0:R, :], rows_ap, None, w=[stageb[k]])
                bk = self.bank()
                P.op("pe", lambda: nc.tensor.transpose(self.ps[bk][:, 0:R], stage[k][0:R, :],
                                                       self.cst[0:R, C_ID * 128:C_ID * 128 + R]),
                     r=[stageb[k], self.cstb], w=[self.psb[bk]])
                P.op("dve", lambda: nc.vector.tensor_copy(out=dst_ap, in_=self.ps[bk][:, 0:R]),
                     r=[self.psb[bk]], w=[wb])

            ab = I["ada_b"].rearrange("l (c p) -> (l c) p", p=128)
            for i in range(3):
                load_T(adabT[:, i * 96:(i + 1) * 96], ab[i * 96:(i + 1) * 96, :], 96, adabTb)
            ng = I["norm_g"].rearrange("l i (c p) -> (l i c) p", p=128)
            for i in range(2):
                load_T(self.normgT[:, i * 96:(i + 1) * 96], ng[i * 96:(i + 1) * 96, :], 96, self.smallb)
            load_T(condT[:, :], I["cond"].rearrange("r (c p) -> (r c) p", p=128), 32, condb)
            k = si[0] % 2
            si[0] += 1
            P.dma("sp", stage[k][0:8, :], I["pool_scale"].rearrange("o (c p) -> (o c) p", p=128), None, w=[stageb[k]])
            P.dma("sp", stage[k][8:10, :], I["qk_norm_g"].rearrange("o i p -> (o i) p"), None, w=[stageb[k]])
            P.dma("sp", stage[k][10:58, :], I["ssm_conv_b"].rearrange("o (c p) -> (o c) p", p=128), None, w=[stageb[k]])
            P.dma("sp", stage[k][58:90, :], I["ssm_norm_g"].rearrange("o (c p) -> (o c) p", p=128), None, w=[stageb[k]])
            bk = self.bank()
            P.op("pe", lambda: nc.tensor.transpose(self.ps[bk][:, 0:90], stage[k][0:90, :],
                                                   self.cst[0:90, C_ID * 128:C_ID * 128 + 90]),
                 r=[stageb[k], self.cstb], w=[self.psb[bk]])
            P.op("dve", lambda: nc.vector.tensor_copy(out=self.vecT[:, :], in_=self.ps[bk][:, 0:90]),
                 r=[self.psb[bk]], w=[self.smallb])
            cw = I["ssm_conv_w"].rearrange("o (c p) j -> p (o c) j", p=128)
            for i in range(4):
                P.dma("sp", self.convw[:, i * 12:(i + 1) * 12, :], cw[:, i * 12:(i + 1) * 12, :], None, w=[self.smallb])
            P.dma("sp", self.dtb[:, :], I["ssm_dt_bias"].rearrange("o a h -> o (a h)").partition_broadcast(128).squeeze(1),
                  None, w=[self.smallb])
            P.dma("sp", self.Aneg[:, :], I["ssm_A_log"].rearrange("o a h -> o (a h)").partition_broadcast(128).squeeze(1),
                  None, w=[self.smallb])
            dtmp = self.tmp[0]
            P.dma("sp", dtmp[:, 0:128], I["ssm_D"].rearrange("o a h -> o (a h)").partition_broadcast(128).squeeze(1),
                  None, w=[self.tmpb[0]])
            P.op("act", lambda: nc.scalar.activation(out=self.Aneg[:, :], in_=self.Aneg[:, :], func=AF.Exp),
                 r=[self.smallb], w=[self.smallb])
            P.op("dve", lambda: nc.vector.tensor_scalar(out=self.Aneg[:, :], in0=self.Aneg[:, :], scalar1=-1.0,
                                                        scalar2=None, op0=ALU.mult),
                 r=[self.smallb], w=[self.smallb])
            P.op("dve", lambda: nc.vector.tensor_tensor(out=self.Dsum[:, :], in0=dtmp[:, 0:64], in1=dtmp[:, 64:128],
                                                        op=ALU.add),
                 r=[self.tmpb[0]], w=[self.smallb])
            P.op("act", lambda: nc.scalar.activation(out=scT[:, :, :].rearrange("p c r -> p r c"),
                                                     in_=condT[:, :].rearrange("p (r c) -> p r c", r=2),
                                                     func=AF.Silu),
                 r=[condb], w=[condb])
            NB = 1024
            aslot = [sb(s2, "aslot%d" % i, [128, KC, NB], BF16) for i in range(2)]
            aslotb = bufs(2)
            asem = [P.dsem("aslot%d" % i) for i in range(2)]
            blocks = [(l, b) for l in range(2) for b in range(NMOD * D // NB)]

            def aload(i):
                l, b = blocks[i]
                src = I["ada_w"][l, :, b * NB:(b + 1) * NB].rearrange("(k p) n -> p k n", p=128)
                P.dma("pool", aslot[i % 2][:, :, :], src, asem[i % 2], w=[aslotb[i % 2]])

            aload(0)
            for i, (l, b) in enumerate(blocks):
                if i + 1 < len(blocks):
                    aload(i + 1)
                sl = aslot[i % 2]
                bk = self.bank()
                pv = self.ps[bk][:, 0:16].rearrange("p (c r) -> p c r", r=2)
                for cc in range(8):
                    for kc in range(KC):
                        last = (cc == 7 and kc == KC - 1)
                        P.op("pe", lambda: nc.tensor.matmul(pv[:, cc, :], lhsT=sl[:, kc, cc * 128:(cc + 1) * 128],
                                                            rhs=scT[:, kc, :], start=(kc == 0), stop=(kc == KC - 1)),
                             r=[aslotb[i % 2], condb], w=[self.psb[bk]], inc=last)
                for r in range(2):
                    P.op("dve", lambda: nc.vector.tensor_tensor(
                        out=self.mods[:, l, r, b * 8:(b + 1) * 8], in0=pv[:, :, r],
                        in1=adabT[:, l * 144 + b * 8:l * 144 + (b + 1) * 8], op=ALU.add),
                        r=[self.psb[bk], adabTb], w=[self.modsb])
            for l in range(2):
                for s in range(3):
                    for r in range(2):
                        idx = (l * 3 + s) * 2 + r
                        wgt = 1.0 if s == 1 else 0.5
                        P.op("dve", lambda: nc.vector.scalar_tensor_tensor(
                            out=self.gs[:, idx, :], in0=self.mods[:, l, r, (3 * s + 1) * KC:(3 * s + 2) * KC],
                            scalar=1.0, in1=self.normgT[:, (l * 6 + 2 * s) * KC:(l * 6 + 2 * s + 1) * KC],
                            op0=ALU.add, op1=ALU.mult), r=[self.modsb, self.smallb], w=[self.modsb])
                        P.op("dve", lambda: nc.vector.scalar_tensor_tensor(
                            out=self.gg[:, idx, :], in0=self.mods[:, l, r, (3 * s + 2) * KC:(3 * s + 3) * KC],
                            scalar=wgt, in1=self.normgT[:, (l * 6 + 2 * s + 1) * KC:(l * 6 + 2 * s + 2) * KC],
                            op0=ALU.mult, op1=ALU.mult), r=[self.modsb, self.smallb], w=[self.modsb])
            self.fence()

    def fence(self):
        P = self.P
        es = ["pe", "dve", "act", "pool"]
        sp = P.eng["sp"]
        for e in es:
            P.flush(e)
        for tl in [P.tl[e] for e in es] + P.dsems:
            if tl.n > P.seen["sp"].get(tl, 0):
                sp.wait_ge(tl.sem, tl.n)
                P.seen["sp"][tl] = tl.n
        sp.sem_inc(P.fsem.sem, 1)
        P.fsem.n += 1
        for e in es:
            P.eng[e].wait_ge(P.fsem.sem, P.fsem.n)
            for tl in [P.tl[o] for o in es + ["sp"]] + P.dsems:
                P.seen[e][tl] = tl.n

    def sh_ap(self, l, s, r, c):
        return self.mods[:, l, r, 3 * s * KC + c:3 * s * KC + c + 1]

    def gs_ap(self, l, s, r, c):
        idx = (l * 3 + s) * 2 + r
        return self.gs[:, idx, c:c + 1]

    def gg_ap(self, l, s, r, c):
        idx = (l * 3 + s) * 2 + r
        return self.gg[:, idx, c:c + 1]

    def newtmp(self):
        i = self.tmp_rr
        self.tmp_rr = (self.tmp_rr + 1) % 6
        return i

    def rstd_from_stats(self, sbk, dim):
        nc, P = self.nc, self.P
        P.op("act", lambda: nc.scalar.activation(out=self.rstd[:, :], in_=self.ps[sbk][:, :], func=AF.Sqrt,
                                                 scale=1.0 / dim, bias=self.epsc[:, 0:1]),
             r=[self.psb[sbk], self.smallb], w=[self.rstdb])
        P.op("dve", lambda: nc.vector.reciprocal(out=self.rstd[:, :], in_=self.rstd[:, :]),
             r=[self.rstdb], w=[self.rstdb])

    def modulate(self, h, hb, tt, l, s, r, u_ap, ub):
        nc, P = self.nc, self.P
        sbk = 7
        for c in range(KC):
            ti = self.newtmp()
            P.op("act", lambda: nc.scalar.activation(out=self.tmp[ti][:, :], in_=h[:, c, tt * 512:(tt + 1) * 512],
                                                     func=AF.Square), r=[hb[c, tt]], w=[self.tmpb[ti]])
            P.op("pe", lambda: nc.tensor.matmul(self.ps[sbk][:, :], lhsT=self.ones_f[:, :], rhs=self.tmp[ti][:, :],
                                                start=(c == 0), stop=(c == KC - 1)),
                 r=[self.tmpb[ti], self.onesb], w=[self.psb[sbk]])
        self.rstd_from_stats(sbk, D)
        for c in range(KC):
            ti = self.newtmp()
            P.op("dve", lambda: nc.vector.tensor_tensor(out=self.tmp[ti][:, :], in0=h[:, c, tt * 512:(tt + 1) * 512],
                                                        in1=self.rstd[:, :], op=ALU.mult),
                 r=[hb[c, tt], self.rstdb], w=[self.tmpb[ti]])
            P.op("act", lambda: nc.scalar.activation(out=u_ap(c), in_=self.tmp[ti][:, :], func=AF.Identity,
                                                     scale=self.gs_ap(l, s, r, c), bias=self.sh_ap(l, s, r, c)),
                 r=[self.tmpb[ti], self.modsb], w=[ub(c)])

    def residual(self, h, hb, tt, l, s, r, y_ap, yb, sbk, dim=D):
        nc, P = self.nc, self.P
        self.rstd_from_stats(sbk, dim)
        for c in range(KC):
            ti = self.newtmp()
            P.op("dve", lambda: nc.vector.tensor_tensor(out=self.tmp[ti][:, :], in0=y_ap(c), in1=self.rstd[:, :],
                                                        op=ALU.mult), r=[yb(c), self.rstdb], w=[self.tmpb[ti]])
            hv = h[:, c, tt * 512:(tt + 1) * 512]
            P.op("dve", lambda: nc.vector.scalar_tensor_tensor(out=hv, in0=self.tmp[ti][:, :],
                                                               scalar=self.gg_ap(l, s, r, c), in1=hv,
                                                               op0=ALU.mult, op1=ALU.add),
                 r=[self.tmpb[ti], hb[c, tt], self.modsb], w=[hb[c, tt]])

    def out_proj_residual(self, w_ap, KCin, rhs_ap, rhsb, slots, slotb, ssem, h, hb, tt, l, s, r, uy, uyb, post_scale=None):
        nc, P = self.nc, self.P
        sbk = 7
        ns = len(slots)

        def wload(oc):
            src = w_ap[:, oc * 128:(oc + 1) * 128].rearrange("(k p) n -> p k n", p=128)
            P.dma("pool", slots[oc % ns][:, 0:KCin, :], src, ssem[oc % ns], w=[slotb[oc % ns]])

        pend = None
        wload(0)
        for oc in range(KC):
            if oc + 1 < KC:
                wload(oc + 1)
            bk = self.bank()
            sl = slots[oc % ns]
            for kc in range(KCin):
                P.op("pe", lambda: nc.tensor.matmul(self.ps[bk][:, :], lhsT=sl[:, kc, :], rhs=rhs_ap(kc),
                                                    start=(kc == 0), stop=(kc == KCin - 1)),
                     r=[slotb[oc % ns], rhsb(kc)], w=[self.psb[bk]], inc=(kc == KCin - 1))
            if pend is not None:
                pend()
            yv = uy[:, oc * 512:(oc + 1) * 512]
            ti = self.newtmp()
            if post_scale is None:
                P.op("act", lambda: nc.scalar.copy(out=yv, in_=self.ps[bk][:, :]), r=[self.psb[bk]], w=[uyb[oc]])
                P.op("act", lambda: nc.scalar.activation(out=self.tmp[ti][:, :], in_=self.ps[bk][:, :], func=AF.Square),
                     r=[self.psb[bk]], w=[self.tmpb[ti]])
            else:
                ps_ap, ps_b = post_scale
                P.op("dve", lambda: nc.vector.tensor_tensor(out=yv, in0=self.ps[bk][:, :], in1=ps_ap, op=ALU.mult),
                     r=[self.psb[bk], ps_b], w=[uyb[oc]])
                P.op("act", lambda: nc.scalar.activation(out=self.tmp[ti][:, :], in_=yv, func=AF.Square),
                     r=[uyb[oc]], w=[self.tmpb[ti]])

            def mk(ti=ti, oc=oc):
                def f():
                    P.op("pe", lambda: nc.tensor.matmul(self.ps[sbk][:, :], lhsT=self.ones_f[:, :],
                                                        rhs=self.tmp[ti][:, :], start=(oc == 0), stop=(oc == KC - 1)),
                         r=[self.tmpb[ti], self.onesb], w=[self.psb[sbk]])
                return f
            pend = mk()
        pend()
        self.residual(h, hb, tt, l, s, r, lambda c: uy[:, c * 512:(c + 1) * 512], lambda c: uyb[c], sbk)

    def load_x_tile(self, xs, xsb, r, tt, ht, hb):
        nc, P, I = self.nc, self.P, self.I
        x = I["x_ctx"] if r == 0 else I["x_smp"]
        for tk in range(4):
            row0 = (tt * 4 + tk) * 128
            for hf in range(2):
                k = self.xs_rr % 2
                self.xs_rr += 1
                P.dma("sp", xs[k][:, :], x[row0:row0 + 128, hf * 1024:(hf + 1) * 1024], None, w=[xsb[k]])
                for c4 in range(2):
                    bk = self.bank()
                    for q in range(4):
                        c = c4 * 4 + q
                        P.op("pe", lambda: nc.tensor.transpose(self.ps[bk][:, q * 128:(q + 1) * 128],
                                                               xs[k][:, c * 128:(c + 1) * 128],
                                                               self.cst[:, C_ID * 128:(C_ID + 1) * 128]),
                             r=[xsb[k], self.cstb], w=[self.psb[bk]])
                    c0 = hf * 8 + c4 * 4
                    P.op("dve", lambda: nc.vector.tensor_copy(
                        out=ht[:, c0:c0 + 4, tk * 128:(tk + 1) * 128],
                        in_=self.ps[bk][:, :].rearrange("p (q t) -> p q t", q=4)),
                        r=[self.psb[bk]], w=[hb[c0:c0 + 4, 0]])

    def store_y_tile(self, ost, ostb, r, tt, ht, hb):
        nc, P, O = self.nc, self.P, self.O
        y = O["y_ctx"] if r == 0 else O["y_smp"]
        for tk in range(4):
            row0 = (tt * 4 + tk) * 128
            for qd in range(4):
                k = self.xs_rr % 2
                self.xs_rr += 1
                bk = self.bank()
                for q in range(4):
                    c = qd * 4 + q
                    P.op("pe", lambda: nc.tensor.transpose(self.ps[bk][:, q * 128:(q + 1) * 128],
                                                           ht[:, c, tk * 128:(tk + 1) * 128],
                                                           self.cst[:, C_ID * 128:(C_ID + 1) * 128]),
                         r=[hb[c, 0], self.cstb], w=[self.psb[bk]])
                P.op("dve", lambda: nc.vector.tensor_copy(out=ost[k][:, :], in_=self.ps[bk][:, :]),
                     r=[self.psb[bk]], w=[ostb[k]])
                P.dma("sp", y[row0:row0 + 128, qd * 512:(qd + 1) * 512], ost[k][:, :], None, r=[ostb[k]])

    def h_load(self, h, hb, T, sem):
        P = self.P
        for tt in range(T // 512):
            P.dma("sp", h[:, :, tt * 512:(tt + 1) * 512], self.hscr[:, :, tt * 512:(tt + 1) * 512], None, w=[hb[:, tt]])

    def h_store(self, h, hb, T, sem):
        P = self.P
        for tt in range(T // 512):
            P.dma("sp", self.hscr[:, :, tt * 512:(tt + 1) * 512], h[:, :, tt * 512:(tt + 1) * 512], None, r=[hb[:, tt]])

    def ffn(self, T, l, f, r, uy, uyb, src, dst):
        nc, P, I = self.nc, self.P, self.I
        s = 0 if f == 0 else 2
        NT = T // 512
        w_in = I["ffn_w_in"][l, f]
        w_out = I["ffn_w_out"][l, f]
        uyh = uy[:, :].bitcast(BF16)
        tag = "%d%d%d" % (r, l, f)
        hscr = self.hscr
        with ExitStack() as es:
            g = self.sb(es, "g_" + tag, [128, JH, T], BF16)
            gb = bufs(JH, NT)
            ub = bufs(KC, NT)
            with ExitStack() as ea:
                ht = self.sb(ea, "hA_" + tag, [128, KC, 512], F32)
                if src == "x":
                    xs = [self.sb(ea, "xs%d_%s" % (i, tag), [128, 1024], F32) for i in range(2)]
                    xsb = bufs(2)
                for tt in range(NT):
                    hb = bufs(KC, 1)
                    if src == "x":
                        self.load_x_tile(xs, xsb, r, tt, ht, hb)
                        P.dma("sp", hscr[:, :, tt * 512:(tt + 1) * 512], ht[:, :, :], None, r=[hb])
                    else:
                        P.dma("sp", ht[:, :, :], hscr[:, :, tt * 512:(tt + 1) * 512], None, w=[hb])
                    if not self.cfg.get("ffn_io_only"):
                        self.modulate(ht, hb, 0, l, s, r, lambda c: uyh[:, c * T + tt * 512:c * T + (tt + 1) * 512],
                                      lambda c: ub[c, tt])
                    self.fence()
            with ExitStack() as eb:
                NW = 4
                win = [self.sb(eb, "wi%d_%s" % (i, tag), [128, KC, 256], BF16) for i in range(NW)]
                winb = bufs(NW)
                wsem = [P.dsem("wi%d_%s" % (i, tag)) for i in range(NW)]

                def wload(j):
                    for half in range(2):
                        srcw = w_in[:, half * DFF + j * 128: half * DFF + (j + 1) * 128].rearrange("(k p) n -> p k n", p=128)
                        P.dma("pool", win[j % NW][:, :, half * 128:(half + 1) * 128], srcw, wsem[j % NW], w=[winb[j % NW]])

                JN = 0 if self.cfg.get("ffn_io_only") else JH
                for j in range(min(NW - 1, JN)):
                    wload(j)
                for j in range(JN):
                    if j + NW - 1 < JN:
                        wload(j + NW - 1)
                    sl = win[j % NW]
                    banks = [[self.bank() for tt in range(NT)] for half in range(2)]
                    for half in range(2):
                        for kc in range(KC):
                            for tt in range(NT):
                                bk = banks[half][tt]
                                P.op("pe", lambda: nc.tensor.matmul(
                                    self.ps[bk][:, :], lhsT=sl[:, kc, half * 128:(half + 1) * 128],
                                    rhs=uyh[:, kc * T + tt * 512:kc * T + (tt + 1) * 512],
                                    start=(kc == 0), stop=(kc == KC - 1)),
                                    r=[winb[j % NW], ub[kc, tt]], w=[self.psb[bk]])
                    for tt in range(NT):
                        ti = self.newtmp()
                        P.op("act", lambda: nc.scalar.activation(out=self.tmp[ti][:, :], in_=self.ps[banks[0][tt]][:, :],
                                                                 func=AF.Silu), r=[self.psb[banks[0][tt]]], w=[self.tmpb[ti]])
                        P.op("dve", lambda: nc.vector.tensor_tensor(out=g[:, j, tt * 512:(tt + 1) * 512], in0=self.tmp[ti][:, :],
                                                                    in1=self.ps[banks[1][tt]][:, :], op=ALU.mult),
                             r=[self.tmpb[ti], self.psb[banks[1][tt]]], w=[gb[j, tt]])
                self.fence()
            with ExitStack() as ec:
                ht = self.sb(ec, "hC_" + tag, [128, KC, 512], F32)
                wout = [self.sb(ec, "wo%d_%s" % (i, tag), [128, JH, 128], BF16) for i in range(2)]
                woutb = bufs(2)
                wosem = [P.dsem("wo%d_%s" % (i, tag)) for i in range(2)]
                if dst == "y":
                    ost = [self.sb(ec, "os%d_%s" % (i, tag), [128, 512], F32) for i in range(2)]
                    ostb = bufs(2)
                for tt in range(NT):
                    hb = bufs(KC, 1)
                    P.dma("sp", ht[:, :, :], hscr[:, :, tt * 512:(tt + 1) * 512], None, w=[hb])
                    if not self.cfg.get("ffn_io_only"):
                        self.out_proj_residual(w_out, JH, lambda kc: g[:, kc, tt * 512:(tt + 1) * 512], lambda kc: gb[kc, tt],
                                               wout, woutb, wosem, ht, hb, 0, l, s, r, uy, uyb)
                    if dst == "y":
                        self.store_y_tile(ost, ostb, r, tt, ht, hb)
                    else:
                        P.dma("sp", hscr[:, :, tt * 512:(tt + 1) * 512], ht[:, :, :], None, r=[hb])
                    self.fence()

    def run_pass(self, r):
        nc, P = self.nc, self.P
        cfg = self.cfg
        T = 512 if r == 0 else 1024
        with ExitStack() as es:
            uy = self.sb(es, "uy%d" % r, [128, 8192], F32)
            uyb = bufs(KC)
            phases = []
            for l in range(2):
                if cfg.get("ffn", True):
                    phases.append(("ffn", l, 0))
                if cfg.get("mixer", True) and l in cfg.get("mix_layers", (0, 1)):
                    phases.append(("mix", l, 0))
                if cfg.get("ffn", True):
                    phases.append(("ffn", l, 1))
            assert phases[0][0] == "ffn" and phases[-1][0] == "ffn"
            for i, (kind, l, f) in enumerate(phases):
                if kind == "ffn":
                    self.ffn(T, l, f, r, uy, uyb, "x" if i == 0 else "scr", "y" if i == len(phases) - 1 else "scr")
                    continue
                with ExitStack() as em:
                    win = [self.sb(em, "win%d_%d%d" % (k_, r, l), [128, KC, 256], BF16) for k_ in range(2)]
                    winb = bufs(2)
                    wsem = [P.dsem("win%d_%d%d" % (k_, r, l)) for k_ in range(2)]
                    if l == 0:
                        self.even_mixer(T, r, uy, uyb, win, winb, wsem)
                    else:
                        self.odd_mixer(T, r, uy, uyb, win, winb, wsem)
                    self.fence()
            self.fence()

    def bank_hold(self):
        i = self.bank()
        self.held.add(i)
        return i

    def bank_release(self, i):
        self.held.discard(i)

    def mix_out_residual(self, w_ap, KCin, rhs_of_tile, rhsb_of_tile, T, l, r, uy, uyb, win, winb, wsem, tag,
                         post_scale=None):
        nc, P = self.nc, self.P
        with ExitStack() as es:
            ht = [self.sb(es, "ht%d_%s" % (i, tag), [128, KC, 512], F32) for i in range(1)]
            slots = [win[0][:, :, 0:128], win[1][:, :, 0:128]] if KCin <= KC else None
            if slots is None:
                wo = [self.sb(es, "mwo%d_%s" % (i, tag), [128, KCin, 128], BF16) for i in range(2)]
                slots = [wo[0][:, :, :], wo[1][:, :, :]]
                slb = bufs(2)
                slsem = [P.dsem("mwo%d_%s" % (i, tag)) for i in range(2)]
            else:
                slb, slsem = winb, wsem
            for tt in range(T // 512):
                htb = bufs(KC, 1)
                P.dma("sp", ht[0][:, :, :], self.hscr[:, :, tt * 512:(tt + 1) * 512], None, w=[htb])
                self.out_proj_residual(w_ap, KCin, rhs_of_tile(tt), rhsb_of_tile(tt), slots, slb, slsem,
                                       ht[0], htb, 0, l, 1, r, uy, uyb, post_scale=(post_scale(tt) if post_scale else None))
                P.dma("sp", self.hscr[:, :, tt * 512:(tt + 1) * 512], ht[0][:, :, :], None, r=[htb])
                self.fence()

    def even_mixer(self, T, r, uy, uyb, win, winb, wsem):
        nc, P, I, O = self.nc, self.P, self.I, self.O
        l, s = 0, 1
        nseq = 2 if r == 0 else 1
        L = T // nseq
        NT = T // 512
        uyh = uy[:, :].bitcast(BF16)
        w_in = I["mix_w_in"][0]
        ident = self.cst[:, C_ID * 128:(C_ID + 1) * 128]
        tag = "e%d" % r
        NK = L + (PAST if r == 1 else 0)
        NKC = NK // 128
        with ExitStack() as es:
            cat = self.sb(es, "cat" + tag, [128, KC, T], BF16)
            catb = bufs(KC, NT)
            qT = self.sb(es, "qT" + tag, [128, 8, T], BF16)
            qTb = bufs(8, NT)
            kT = self.sb(es, "kT" + tag, [128, 2, nseq * NK], BF16)
            kTb = bufs(2)
            vtok = self.sb(es, "vtok" + tag, [128, nseq * NKC, 256], BF16)
            vtokb = Buf()
            gq = self.sb(es, "gq" + tag, [128, 2], F32)
            gqb = Buf()
            ub = bufs(KC, NT)
            P.op("dve", lambda: nc.vector.tensor_scalar(out=gq[:, 0:1], in0=self.vecT[:, 8:9], scalar1=128.0 ** -0.5,
                                                        scalar2=None, op0=ALU.mult), r=[self.smallb], w=[gqb])
            P.op("dve", lambda: nc.vector.tensor_copy(out=gq[:, 1:2], in_=self.vecT[:, 9:10]), r=[self.smallb], w=[gqb])
            with ExitStack() as e1:
                h = self.sb(e1, "h_" + tag, [128, KC, T], F32)
                hb = bufs(KC, NT)
                self.h_load(h, hb, T, None)
                for tt in range(NT):
                    self.modulate(h, hb, tt, l, s, r, lambda c: uyh[:, c * T + tt * 512:c * T + (tt + 1) * 512],
                                  lambda c: ub[c, tt])
                self.fence()
            with ExitStack() as e1:
                xpp = self.sb(e1, "xpp" + tag, [128, nseq, L + 16], F32)
                xppb = Buf()
                sA = self.sb(e1, "sA" + tag, [128, nseq, L + 16], F32)
                sB = self.sb(e1, "sB" + tag, [128, nseq, L + 16], F32)
                sb_ = Buf()
                dbf = self.sb(e1, "dbf" + tag, [128, 2, T], BF16)
                dbfb = bufs(2)
                pcnt = self.sb(e1, "pcnt" + tag, [128, 4, L], F32)
                pcntb = Buf()
                pw = self.sb(e1, "pw" + tag, [128, 4, 2, 256], BF16)
                pwb = Buf()
                P.dma("sp", pcnt[:, :, :], I["pcnt_ctx"] if r == 0 else I["pcnt_smp"], None, w=[pcntb])
                for gi_ in range(4):
                    P.dma("pool", pw[:, gi_, :, :], I["pool_w"][0, gi_].rearrange("(c p) d -> p c d", p=128), None, w=[pwb])
                P.op("dve", lambda: nc.vector.memset(xpp[:, :, :], 0.0), w=[xppb])
                if r == 1:
                    rope = self.sb(e1, "rope" + tag, [128, 2, T], F32)
                    ropeb = Buf()
                    P.dma("sp", rope[:, :, :], I["rope"], None, w=[ropeb])
                    ck = self.sb(e1, "ck" + tag, [128, 4, 256], F32)
                    ckb = Buf()
                    P.dma("sp", ck[:, :, :], I["cache_k"].rearrange("(b p) f -> p b f", p=128), None, w=[ckb])
                    P.dma("pool", vtok[:, 0:4, :], I["cache_v"].rearrange("(b p) f -> p b f", p=128), None, w=[vtokb])
                    for kh in range(2):
                        bk = self.bank()
                        for blk in range(4):
                            P.op("pe", lambda: nc.tensor.transpose(self.ps[bk][:, blk * 128:(blk + 1) * 128],
                                                                   ck[:, blk, kh * 128:(kh + 1) * 128], ident),
                                 r=[ckb, self.cstb], w=[self.psb[bk]], inc=(blk == 3))
                        P.op("dve", lambda: nc.vector.tensor_copy(out=kT[:, kh, 0:512], in_=self.ps[bk][:, :]),
                             r=[self.psb[bk]], w=[kTb[kh]])
                else:
                    nkst = self.sb(e1, "nkst" + tag, [128, 4, 256], F32)
                    nkstb = bufs(4)
                    nvst = self.sb(e1, "nvst" + tag, [128, 4, 256], F32)
                    nvstb = bufs(4)

                def u_ap(kc, tt):
                    return uyh[:, kc * T + tt * 512:kc * T + (tt + 1) * 512]

                def wload(i):
                    srcw = w_in[:, i * 256:(i + 1) * 256].rearrange("(k p) n -> p k n", p=128)
                    P.dma("pool", win[i % 2][:, :, :], srcw, wsem[i % 2], w=[winb[i % 2]])

                NI = self.cfg.get("even_ni", 10)
                if NI > 0:
                    wload(0)
                for i in range(NI):
                    if i + 1 < NI:
                        wload(i + 1)
                    sl = win[i % 2]
                    if i == 9:
                        for tb in range(T // 128):
                            bk = self.bank()
                            for kc in range(KC):
                                P.op("pe", lambda: nc.tensor.matmul(
                                    self.ps[bk][:, 0:256], lhsT=uyh[:, kc * T + tb * 128:kc * T + (tb + 1) * 128],
                                    rhs=sl[:, kc, :], start=(kc == 0), stop=(kc == KC - 1)),
                                    r=[winb[i % 2], ub[kc, tb // 4]], w=[self.psb[bk]], inc=(kc == KC - 1))
                            if r == 0:
                                kcx = tb
                            else:
                                kcx = 4 + tb
                            if self.cfg.get("v_dbg", 3) >= 2:
                                P.op("dve", lambda: nc.vector.tensor_copy(out=vtok[:, kcx, :], in_=self.ps[bk][:, 0:256]),
                                     r=[self.psb[bk]], w=[vtokb])
                            if r == 0 and self.cfg.get("v_dbg", 3) >= 3:
                                P.op("act", lambda: nc.scalar.copy(out=nvst[:, tb, :], in_=self.ps[bk][:, 0:256]),
                                     r=[self.psb[bk]], w=[nvstb[tb]])
                                P.dma("sp", O["new_v"][tb * 128:(tb + 1) * 128, :], nvst[:, tb, :], None, r=[nvstb[tb]])
                        continue
                    for sub in range(2):
                        cc = i * 2 + sub
                        for tt in range(NT):
                            bk = self.bank()
                            for kc in range(KC):
                                P.op("pe", lambda: nc.tensor.matmul(
                                    self.ps[bk][:, :], lhsT=sl[:, kc, sub * 128:(sub + 1) * 128], rhs=u_ap(kc, tt),
                                    start=(kc == 0), stop=(kc == KC - 1)),
                                    r=[winb[i % 2], ub[kc, tt]], w=[self.psb[bk]], inc=(kc == KC - 1))
                            if cc < 8:
                                if r == 0:
                                    P.op("act", lambda: nc.scalar.copy(
                                        out=xpp[:, :, 8:8 + L], in_=self.ps[bk][:, :].rearrange("p (s t) -> p s t", s=2)),
                                        r=[self.psb[bk]], w=[xppb])
                                else:
                                    P.op("act", lambda: nc.scalar.copy(
                                        out=xpp[:, 0, 8 + tt * 512:8 + (tt + 1) * 512], in_=self.ps[bk][:, :]),
                                        r=[self.psb[bk]], w=[xppb])
                            else:
                                isq = cc < 16
                                hd = cc - 8 if isq else cc - 16
                                t1 = self.newtmp()
                                P.op("act", lambda: nc.scalar.activation(out=self.tmp[t1][:, :], in_=self.ps[bk][:, :],
                                                                         func=AF.Square), r=[self.psb[bk]], w=[self.tmpb[t1]])
                                sbk = self.bank()
                                P.op("pe", lambda: nc.tensor.matmul(self.ps[sbk][:, :], lhsT=self.ones_f[:, :],
                                                                    rhs=self.tmp[t1][:, :], start=True, stop=True),
                                     r=[self.tmpb[t1], self.onesb], w=[self.psb[sbk]])
                                t2 = self.newtmp()
                                P.op("act", lambda: nc.scalar.activation(out=self.tmp[t2][:, :], in_=self.ps[sbk][:, :],
                                                                         func=AF.Sqrt, scale=1.0 / 128, bias=self.epsc[:, 0:1]),
                                     r=[self.psb[sbk], self.smallb], w=[self.tmpb[t2]])
                                P.op("dve", lambda: nc.vector.reciprocal(out=self.tmp[t2][:, :], in_=self.tmp[t2][:, :]),
                                     r=[self.tmpb[t2]], w=[self.tmpb[t2]])
                                t3 = self.newtmp()
                                P.op("dve", lambda: nc.vector.tensor_tensor(out=self.tmp[t3][:, :], in0=self.ps[bk][:, :],
                                                                            in1=self.tmp[t2][:, :], op=ALU.mult),
                                     r=[self.psb[bk], self.tmpb[t2]], w=[self.tmpb[t3]])
                                gcol = gq[:, 0:1] if isq else gq[:, 1:2]
                                if r == 0:
                                    if isq:
                                        P.op("act", lambda: nc.scalar.activation(
                                            out=qT[:, hd, tt * 512:(tt + 1) * 512], in_=self.tmp[t3][:, :],
                                            func=AF.Copy, scale=gcol), r=[self.tmpb[t3], gqb], w=[qTb[hd, tt]])
                                    else:
                                        t4 = self.newtmp()
                                        P.op("act", lambda: nc.scalar.activation(
                                            out=self.tmp[t4][:, :], in_=self.tmp[t3][:, :], func=AF.Copy, scale=gcol),
                                            r=[self.tmpb[t3], gqb], w=[self.tmpb[t4]])
                                        P.op("dve", lambda: nc.vector.tensor_copy(out=kT[:, hd, 0:512], in_=self.tmp[t4][:, :]),
                                             r=[self.tmpb[t4]], w=[kTb[hd]])
                                        tbk = self.bank()
                                        for blk in range(4):
                                            P.op("pe", lambda: nc.tensor.transpose(
                                                self.ps[tbk][:, blk * 128:(blk + 1) * 128],
                                                self.tmp[t4][:, blk * 128:(blk + 1) * 128], ident),
                                                r=[self.tmpb[t4], self.cstb], w=[self.psb[tbk]], inc=(blk == 3))
                                        P.op("dve", lambda: nc.vector.tensor_copy(
                                            out=nkst[:, :, hd * 128:(hd + 1) * 128],
                                            in_=self.ps[tbk][:, :].rearrange("p (b d) -> p b d", b=4)),
                                            r=[self.psb[tbk]], w=[nkstb])
                                        if hd == 1:
                                            for blk in range(4):
                                                P.dma("sp", O["new_k"][blk * 128:(blk + 1) * 128, :], nkst[:, blk, :], None,
                                                      r=[nkstb[blk]])
                                else:
                                    t4 = self.newtmp()
                                    P.op("act", lambda: nc.scalar.activation(
                                        out=self.tmp[t4][:, :], in_=self.tmp[t3][:, :], func=AF.Copy, scale=gcol),
                                        r=[self.tmpb[t3], gqb], w=[self.tmpb[t4]])
                                    rbk = self.bank()
                                    P.op("pe", lambda: nc.tensor.matmul(self.ps[rbk][:, :],
                                                                        lhsT=self.cst[:, C_RT * 128:(C_RT + 1) * 128],
                                                                        rhs=self.tmp[t4][:, :], start=True, stop=True),
                                         r=[self.tmpb[t4], self.cstb], w=[self.psb[rbk]])
                                    t5 = self.newtmp()
                                    P.op("dve", lambda: nc.vector.tensor_tensor(
                                        out=self.tmp[t5][:, :], in0=self.tmp[t4][:, :], in1=rope[:, 0, tt * 512:(tt + 1) * 512],
                                        op=ALU.mult), r=[self.tmpb[t4], ropeb], w=[self.tmpb[t5]])
                                    t6 = self.newtmp()
                                    P.op("dve", lambda: nc.vector.tensor_tensor(
                                        out=self.tmp[t6][:, :], in0=self.ps[rbk][:, :], in1=rope[:, 1, tt * 512:(tt + 1) * 512],
                                        op=ALU.mult), r=[self.psb[rbk], ropeb], w=[self.tmpb[t6]])
                                    if isq:
                                        dst, dstb = qT[:, hd, tt * 512:(tt + 1) * 512], qTb[hd, tt]
                                    else:
                                        dst, dstb = kT[:, hd, PAST + tt * 512:PAST + (tt + 1) * 512], kTb[hd]
                                    P.op("dve", lambda: nc.vector.tensor_tensor(out=dst, in0=self.tmp[t5][:, :],
                                                                                in1=self.tmp[t6][:, :], op=ALU.add),
                                         r=[self.tmpb[t5], self.tmpb[t6]], w=[dstb])
                        if cc < 8:
                            gi = cc // 2
                            wv = (2, 4, 8, 16)[gi]
                            hw = wv // 2
                            W = L + 16
                            cur = xpp
                            step = 1
                            bufs_ = [sA, sB]
                            bi = 0
                            while step < wv:
                                nxt = bufs_[bi]
                                bi ^= 1
                                n_valid = W - 2 * step + 1 if step > 1 else W - 1
                                n_valid = W - (2 * step - 1)
                                P.op("dve", lambda: nc.vector.tensor_tensor(
                                    out=nxt[:, :, 0:n_valid], in0=cur[:, :, 0:n_valid], in1=cur[:, :, step:step + n_valid],
                                    op=ALU.add), r=[xppb, sb_], w=[sb_])
                                cur = nxt
                                step *= 2
                            off = 8 - hw
                            other = bufs_[bi]
                            P.op("dve", lambda: nc.vector.tensor_tensor(
                                out=other[:, :, 0:L], in0=cur[:, :, off:off + L],
                                in1=pcnt[:, gi, :].unsqueeze(1).broadcast_to([128, nseq, L]) if nseq > 1 else pcnt[:, gi:gi + 1, :],
                                op=ALU.mult), r=[sb_, pcntb], w=[sb_])
                            P.op("dve", lambda: nc.vector.tensor_tensor(
                                out=dbf[:, cc % 2, :].rearrange("p (s t) -> p s t", s=nseq), in0=other[:, :, 0:L],
                                in1=xpp[:, :, 8:8 + L], op=ALU.subtract), r=[sb_, xppb], w=[dbfb[cc % 2]])
                            if cc % 2 == 1:
                                for do in range(2):
                                    for tt in range(NT):
                                        bk = self.bank()
                                        for c in range(2):
                                            P.op("pe", lambda: nc.tensor.matmul(
                                                self.ps[bk][:, :], lhsT=pw[:, gi, c, do * 128:(do + 1) * 128],
                                                rhs=dbf[:, c, tt * 512:(tt + 1) * 512], start=(c == 0), stop=(c == 1)),
                                                r=[pwb, dbfb[c]], w=[self.psb[bk]], inc=(c == 1))
                                        oc = gi * 2 + do
                                        P.op("act", lambda: nc.scalar.activation(
                                            out=cat[:, oc, tt * 512:(tt + 1) * 512], in_=self.ps[bk][:, :], func=AF.Copy,
                                            scale=self.vecT[:, oc:oc + 1]), r=[self.psb[bk], self.smallb], w=[catb[oc, tt]])
                self.fence()
            if self.cfg.get("even_stop", 9) < 2:
                return
            with ExitStack() as e2:
                pT = [self.sb(e2, "pT%d%s" % (i, tag), [128, 512], BF16) for i in range(3)]
                pTb = bufs(3)
                pi = 0
                NQ = L if r == 0 else 512
                for sq in range(nseq):
                    for qt in range(L // NQ):
                        q0 = sq * L + qt * NQ
                        for hd in range(8):
                            kh = hd // 4
                            ob = self.bank_hold()
                            db = self.bank_hold()

                            def s_mm(kcx):
                                bk = self.bank()
                                P.op("pe", lambda: nc.tensor.matmul(
                                    self.ps[bk][:, 0:NQ], lhsT=kT[:, kh, sq * NK + kcx * 128:sq * NK + (kcx + 1) * 128],
                                    rhs=qT[:, hd, q0:q0 + NQ], start=True, stop=True),
                                    r=[kTb[kh], qTb[hd, q0 // 512]], w=[self.psb[bk]])
                                return bk
                            sbk_next = s_mm(0)
                            for kcx in range(NKC):
                                sbk = sbk_next
                                k_ = pi % 3
                                pi += 1
                                P.op("act", lambda: nc.scalar.activation(out=pT[k_][:, 0:NQ], in_=self.ps[sbk][:, 0:NQ],
                                                                         func=AF.Exp), r=[self.psb[sbk]], w=[pTb[k_]])
                                if kcx + 1 < NKC:
                                    sbk_next = s_mm(kcx + 1)
                                P.op("pe", lambda: nc.tensor.matmul(
                                    self.ps[ob][:, 0:NQ], lhsT=vtok[:, sq * NKC + kcx, kh * 128:(kh + 1) * 128],
                                    rhs=pT[k_][:, 0:NQ], start=(kcx == 0), stop=(kcx == NKC - 1)),
                                    r=[vtokb, pTb[k_]], w=[self.psb[ob]], inc=False)
                                P.op("pe", lambda: nc.tensor.matmul(
                                    self.ps[db][:, 0:NQ], lhsT=self.ones_b[:, :], rhs=pT[k_][:, 0:NQ],
                                    start=(kcx == 0), stop=(kcx == NKC - 1)),
                                    r=[self.onesb, pTb[k_]], w=[self.psb[db], self.psb[ob]])
                            t1 = self.newtmp()
                            P.op("dve", lambda: nc.vector.reciprocal(out=self.tmp[t1][:, 0:NQ], in_=self.ps[db][:, 0:NQ]),
                                 r=[self.psb[db]], w=[self.tmpb[t1]])
                            P.op("dve", lambda: nc.vector.tensor_tensor(
                                out=cat[:, 8 + hd, q0:q0 + NQ], in0=self.ps[ob][:, 0:NQ], in1=self.tmp[t1][:, 0:NQ],
                                op=ALU.mult), r=[self.psb[ob], self.tmpb[t1]], w=[catb[8 + hd, q0 // 512]])
                            self.bank_release(ob)
                            self.bank_release(db)
                self.fence()
            if self.cfg.get("even_stop", 9) < 3:
                return
            self.mix_out_residual(I["mix_w_out"][0], KC, lambda tt: (lambda kc: cat[:, kc, tt * 512:(tt + 1) * 512]),
                                  lambda tt: (lambda kc: catb[kc, tt]), T, l, r, uy, uyb, win, winb, wsem, tag)
            self.fence()

    def odd_mixer(self, T, r, uy, uyb, win, winb, wsem):
        nc, P, I, O = self.nc, self.P, self.I, self.O
        l, s = 1, 1
        nseq = 2 if r == 0 else 1
        L = T // nseq
        NT = T // 512
        NTB = T // 128
        NCH = L // 128
        uyh = uy[:, :].bitcast(BF16)
        w_in = I["ssm_w_in"][0]
        ident = self.cst[:, C_ID * 128:(C_ID + 1) * 128]
        cU = self.cst[:, C_U * 128:(C_U + 1) * 128]
        cLi = self.cst[:, C_LI * 128:(C_LI + 1) * 128]
        cLs = self.cst[:, C_LS * 128:(C_LS + 1) * 128]
        cUs = self.cst[:, C_US * 128:(C_US + 1) * 128]
        tag = "o%d" % r
        ygscr = self.ygscr
        with ExitStack() as es:
            ub = bufs(KC, NT)
            with ExitStack() as e1:
                h = self.sb(e1, "h_" + tag, [128, KC, T], F32)
                hb = bufs(KC, NT)
                self.h_load(h, hb, T, None)
                for tt in range(NT):
                    self.modulate(h, hb, tt, l, s, r, lambda c: uyh[:, c * T + tt * 512:c * T + (tt + 1) * 512],
                                  lambda c: ub[c, tt])
                self.fence()
            dt = self.sb(es, "dt" + tag, [128, NTB, 128], F32)
            dtA = self.sb(es, "dtA" + tag, [128, NTB, 128], F32)
            ea = self.sb(es, "ea" + tag, [128, NTB, 128], F32)
            dtd = self.sb(es, "dtd" + tag, [128, NTB, 128], F32)
            etot = self.sb(es, "etot" + tag, [128, NTB, 128], F32)
            dtb_ = Buf()
            ssq = self.sb(es, "ssq" + tag, [128, NTB, 8], F32)
            ssqb = Buf()
            rrow = self.sb(es, "rrow" + tag, [128, T], F32)
            rrowb = Buf()

            def u_blk(kc, tb):
                return uyh[:, kc * T + tb * 128:kc * T + (tb + 1) * 128]

            srcw = w_in[:, 10240:10368].rearrange("(k p) n -> p k n", p=128)
            P.dma("pool", win[0][:, :, 0:128], srcw, wsem[0], w=[winb[0]])
            for tb in range(NTB):
                bk = self.bank()
                for kc in range(KC):
                    P.op("pe", lambda: nc.tensor.matmul(self.ps[bk][:, 0:128], lhsT=u_blk(kc, tb), rhs=win[0][:, kc, 0:128],
                                                        start=(kc == 0), stop=(kc == KC - 1)),
                         r=[winb[0], ub[kc, tb // 4]], w=[self.psb[bk]], inc=(kc == KC - 1))
                t1 = self.newtmp()
                xb = self.tmp[t1][:, 0:128]
                P.op("dve", lambda: nc.vector.tensor_tensor(out=xb, in0=self.ps[bk][:, 0:128], in1=self.dtb[:, :], op=ALU.add),
                     r=[self.psb[bk], self.smallb], w=[self.tmpb[t1]])
                t2 = self.newtmp()
                ab = self.tmp[t2][:, 0:128]
                P.op("act", lambda: nc.scalar.activation(out=ab, in_=xb, func=AF.Abs),
                     r=[self.tmpb[t1]], w=[self.tmpb[t2]])
                P.op("act", lambda: nc.scalar.activation(out=ab, in_=ab, func=AF.Exp, scale=-1.0),
                     r=[self.tmpb[t2]], w=[self.tmpb[t2]])
                P.op("act", lambda: nc.scalar.activation(out=ab, in_=ab, func=AF.Ln, bias=self.onec[:, 0:1]),
                     r=[self.tmpb[t2], self.smallb], w=[self.tmpb[t2]])
                P.op("dve", lambda: nc.vector.scalar_tensor_tensor(out=dt[:, tb, :], in0=xb, scalar=0.0, in1=ab,
                                                                   op0=ALU.max, op1=ALU.add),
                     r=[self.tmpb[t1], self.tmpb[t2]], w=[dtb_])
                P.op("dve", lambda: nc.vector.tensor_tensor(out=dtA[:, tb, :], in0=dt[:, tb, :], in1=self.Aneg[:, :], op=ALU.mult),
                     r=[dtb_, self.smallb], w=[dtb_])
                bk2 = self.bank()
                P.op("pe", lambda: nc.tensor.matmul(self.ps[bk2][:, 0:64], lhsT=cU, rhs=dtA[:, tb, 0:64], start=True, stop=True),
                     r=[dtb_, self.cstb], w=[self.psb[bk2]], inc=False)
                P.op("pe", lambda: nc.tensor.matmul(self.ps[bk2][:, 64:128], lhsT=cLi, rhs=dtA[:, tb, 64:128], start=True, stop=True),
                     r=[dtb_, self.cstb], w=[self.psb[bk2]], inc=False)
                P.op("pe", lambda: nc.tensor.matmul(self.ps[bk2][:, 128:256], lhsT=self.ones_f[:, :], rhs=dtA[:, tb, :],
                                                    start=True, stop=True),
                     r=[dtb_, self.onesb], w=[self.psb[bk2]])
                P.op("act", lambda: nc.scalar.activation(out=ea[:, tb, :], in_=self.ps[bk2][:, 0:128], func=AF.Exp),
                     r=[self.psb[bk2]], w=[dtb_])
                P.op("act", lambda: nc.scalar.activation(out=etot[:, tb, :], in_=self.ps[bk2][:, 128:256], func=AF.Exp),
                     r=[self.psb[bk2]], w=[dtb_])
                t3 = self.newtmp()
                ac = self.tmp[t3][:, 0:128]
                P.op("act", lambda: nc.scalar.copy(out=ac, in_=self.ps[bk2][:, 0:128]), r=[self.psb[bk2]], w=[self.tmpb[t3]])
                P.op("dve", lambda: nc.vector.tensor_tensor(out=ac, in0=self.ps[bk2][:, 128:256], in1=ac, op=ALU.subtract),
                     r=[self.psb[bk2], self.tmpb[t3]], w=[self.tmpb[t3]])
                P.op("act", lambda: nc.scalar.activation(out=ac, in_=ac, func=AF.Exp), r=[self.tmpb[t3]], w=[self.tmpb[t3]])
                P.op("dve", lambda: nc.vector.tensor_tensor(out=dtd[:, tb, :], in0=ac, in1=dt[:, tb, :], op=ALU.mult),
                     r=[self.tmpb[t3], dtb_], w=[dtb_])
            with ExitStack() as e2:
                sb = self.sb
                x_tok = sb(e2, "xtok" + tag, [128, NTB, 512], F32)
                xtokb = Buf()
                zs = sb(e2, "zs" + tag, [128, NTB, 512], BF16)
                zsb = Buf()
                yacc = sb(e2, "yacc" + tag, [128, NTB, 512], F32)
                yaccb = bufs(NTB)
                BT = sb(e2, "BT" + tag, [128, T], BF16)
                CT = sb(e2, "CT" + tag, [128, T], BF16)
                bcb = bufs(2)
                Btok = sb(e2, "Btok" + tag, [128, NTB, 128], BF16)
                Btokb = Buf()
                xc = sb(e2, "xc" + tag, [128, nseq, L + 2], F32)
                xcb = Buf()
                cva = sb(e2, "cva" + tag, [128, nseq, L], F32)
                cvb = sb(e2, "cvb" + tag, [128, nseq, L], F32)
                cvab = Buf()
                cvbb = Buf()
                S = [sb(e2, "S%d%s" % (d, tag), [128, 512], F32) for d in range(2)]
                Sb16 = [sb(e2, "Sb%d%s" % (d, tag), [128, 512], BF16) for d in range(2)]
                Sbuf = bufs(2)
                rb = [sb(e2, "rb%d%s" % (i, tag), [128, 8, 128], F32) for i in range(2)]
                rbb = bufs(2)
                dec = [sb(e2, "dec%d%s" % (i, tag), [128, 4, 128], F32) for i in range(2)]
                decb = bufs(2)
                wm = [sb(e2, "wm%d%s" % (i, tag), [128, 8, 128], BF16) for i in range(2)]
                wmb = bufs(2)
                cbm = [sb(e2, "cbm%d%s" % (i, tag), [128, 128], F32) for i in range(2)]
                cbmb = bufs(2)
                xdt = [sb(e2, "xdt%d%s" % (i, tag), [128, 512], BF16) for i in range(2)]
                xdtb = bufs(2)
                xdd = [sb(e2, "xdd%d%s" % (i, tag), [128, 512], BF16) for i in range(2)]
                xddb = bufs(2)
                ygst = sb(e2, "ygst" + tag, [128, 4, T], BF16)
                ygstb = Buf()
                stst = sb(e2, "stst" + tag, [128, 4, 128], F32)
                ststb = Buf()
                P.op("dve", lambda: nc.vector.memset(xc[:, :, :], 0.0), w=[xcb])
                it = 0
                nload = [0]

                def wload(col0, ncol, dcol=0, k=None):
                    if k is None:
                        k = nload[0] % 2
                        nload[0] += 1
                    srcw_ = w_in[:, col0:col0 + ncol].rearrange("(k p) n -> p k n", p=128)
                    P.dma("pool", win[k][:, :, dcol:dcol + ncol], srcw_, wsem[k], w=[winb[k]])
                    return k

                def fm_chunk(k, sub, cidx, kind, dst, dstb):
                    for tt in range(NT):
                        bk = self.bank()
                        for kc in range(KC):
                            P.op("pe", lambda: nc.tensor.matmul(
                                self.ps[bk][:, :], lhsT=win[k][:, kc, sub * 128:(sub + 1) * 128],
                                rhs=uyh[:, kc * T + tt * 512:kc * T + (tt + 1) * 512], start=(kc == 0), stop=(kc == KC - 1)),
                                r=[winb[k], ub[kc, tt]], w=[self.psb[bk]], inc=(kc == KC - 1))
                        if r == 0:
                            P.op("act", lambda: nc.scalar.copy(out=xc[:, :, 1:1 + L],
                                                               in_=self.ps[bk][:, :].rearrange("p (s t) -> p s t", s=2)),
                                 r=[self.psb[bk]], w=[xcb])
                        else:
                            P.op("act", lambda: nc.scalar.copy(out=xc[:, 0, 1 + tt * 512:1 + (tt + 1) * 512], in_=self.ps[bk][:, :]),
                                 r=[self.psb[bk]], w=[xcb])
                    w0 = self.convw[:, cidx, 0:1]
                    w1 = self.convw[:, cidx, 1:2]
                    w2 = self.convw[:, cidx, 2:3]
                    bcol = self.vecT[:, 10 + cidx:11 + cidx]
                    P.op("dve", lambda: nc.vector.tensor_scalar(out=cva[:, :, :], in0=xc[:, :, 0:L], scalar1=w0, scalar2=None,
                                                                op0=ALU.mult), r=[xcb, self.smallb], w=[cvab])
                    P.op("dve", lambda: nc.vector.scalar_tensor_tensor(out=cvb[:, :, :], in0=xc[:, :, 1:L + 1], scalar=w1,
                                                                       in1=cva[:, :, :], op0=ALU.mult, op1=ALU.add),
                         r=[xcb, cvab, self.smallb], w=[cvbb])
                    P.op("dve", lambda: nc.vector.scalar_tensor_tensor(out=cva[:, :, :], in0=xc[:, :, 2:L + 2], scalar=w2,
                                                                       in1=cvb[:, :, :], op0=ALU.mult, op1=ALU.add),
                         r=[xcb, cvbb, self.smallb], w=[cvab])
                    cflat = cva[:, :, :].rearrange("p s t -> p (s t)")
                    vflat = cvb[:, :, :].rearrange("p s t -> p (s t)")
                    P.op("act", lambda: nc.scalar.activation(out=vflat, in_=cflat, func=AF.Silu, bias=bcol),
                         r=[cvab, self.smallb], w=[cvbb])
                    if kind in ("B", "C"):
                        P.op("act", lambda: nc.scalar.copy(out=dst[:, :], in_=vflat), r=[cvbb], w=[dstb])
                    if kind in ("x", "B"):
                        for t4 in range(NTB // 4):
                            bk = self.bank()
                            for q in range(4):
                                tb = t4 * 4 + q
                                P.op("pe", lambda: nc.tensor.transpose(self.ps[bk][:, q * 128:(q + 1) * 128],
                                                                       vflat[:, tb * 128:(tb + 1) * 128], ident),
                                     r=[cvbb, self.cstb], w=[self.psb[bk]], inc=(q == 3))
                            pv = self.ps[bk][:, :].rearrange("p (q f) -> p q f", q=4)
                            if kind == "x":
                                P.op("dve", lambda: nc.vector.tensor_copy(out=x_tok[:, t4 * 4:(t4 + 1) * 4, dst * 128:(dst + 1) * 128],
                                                                          in_=pv), r=[self.psb[bk]], w=[xtokb])
                            else:
                                P.op("dve", lambda: nc.vector.tensor_copy(out=Btok[:, t4 * 4:(t4 + 1) * 4, :], in_=pv),
                                     r=[self.psb[bk]], w=[Btokb])

                for g in range(NG):
                    for half in range(2):
                        k = wload(g * 512 + half * 256, 256)
                        for tb in range(NTB):
                            bk = self.bank()
                            for kc in range(KC):
                                P.op("pe", lambda: nc.tensor.matmul(self.ps[bk][:, 0:256], lhsT=u_blk(kc, tb), rhs=win[k][:, kc, :],
                                                                    start=(kc == 0), stop=(kc == KC - 1)),
                                     r=[winb[k], ub[kc, tb // 4]], w=[self.psb[bk]], inc=(kc == KC - 1))
                            P.op("act", lambda: nc.scalar.activation(out=zs[:, tb, half * 256:(half + 1) * 256],
                                                                     in_=self.ps[bk][:, 0:256], func=AF.Silu),
                                 r=[self.psb[bk]], w=[zsb])
                    for half in range(2):
                        k = wload(4096 + g * 512 + half * 256, 256)
                        for sub in range(2):
                            ch = half * 2 + sub
                            fm_chunk(k, sub, g * 4 + ch, "x", ch, None)
                    k = wload(8192 + g * 128, 128, 0)
                    wload(9216 + g * 128, 128, 128, k=k)
                    fm_chunk(k, 0, 32 + g, "B", BT, bcb[0])
                    fm_chunk(k, 1, 40 + g, "C", CT, bcb[1])
                    for tb in range(NTB):
                        P.op("dve", lambda: nc.vector.tensor_tensor(
                            out=yacc[:, tb, :].rearrange("p (h q) -> p h q", h=8),
                            in0=x_tok[:, tb, :].rearrange("p (h q) -> p h q", h=8),
                            in1=self.Dsum[:, g * 8:(g + 1) * 8].unsqueeze(2).broadcast_to([128, 8, 64]), op=ALU.mult),
                            r=[xtokb, self.smallb], w=[yaccb[tb]])
                    for sq in range(nseq):
                        for d in range(2):
                            Mk = cU if d == 0 else cLi
                            Ml = cLs if d == 0 else cUs
                            hcol = d * 64 + g * 8
                            if r == 1:
                                P.dma("sp", stst[:, :, :], I["state"][d, g * 512:(g + 1) * 512, :].rearrange("(q p) n -> p q n", p=128),
                                      None, w=[ststb])
                                bk = self.bank()
                                for q in range(4):
                                    P.op("pe", lambda: nc.tensor.transpose(self.ps[bk][:, q * 128:(q + 1) * 128], stst[:, q, :], ident),
                                         r=[ststb, self.cstb], w=[self.psb[bk]], inc=(q == 3))
                                P.op("dve", lambda: nc.vector.tensor_copy(out=S[d][:, :], in_=self.ps[bk][:, :]),
                                     r=[self.psb[bk]], w=[Sbuf[d]])
                                P.op("act", lambda: nc.scalar.copy(out=Sb16[d][:, :], in_=self.ps[bk][:, :]),
                                     r=[self.psb[bk]], w=[Sbuf[d]])
                            order = range(NCH) if d == 0 else range(NCH - 1, -1, -1)
                            for ci, c in enumerate(order):
                                tb = sq * NCH + c
                                first = (ci == 0 and r == 0)
                                tok = slice(tb * 128, (tb + 1) * 128)
                                i2 = it % 2
                                it += 1
                                bk = self.bank()
                                P.op("pe", lambda: nc.tensor.matmul(self.ps[bk][:, 0:128], lhsT=BT[:, tok], rhs=CT[:, tok],
                                                                    start=True, stop=True), r=[bcb], w=[self.psb[bk]])
                                P.op("dve", lambda: nc.vector.tensor_tensor(out=cbm[i2][:, :], in0=self.ps[bk][:, 0:128], in1=Mk,
                                                                            op=ALU.mult), r=[self.psb[bk], self.cstb], w=[cbmb[i2]])
                                P.op("dve", lambda: nc.vector.tensor_tensor(
                                    out=rb[i2][:, :, :], in0=Mk.unsqueeze(1).broadcast_to([128, 8, 128]),
                                    in1=dtA[:, tb, hcol:hcol + 8].unsqueeze(2).broadcast_to([128, 8, 128]), op=ALU.mult),
                                    r=[self.cstb, dtb_], w=[rbb[i2]])
                                for hf in range(2):
                                    bk = self.bank()
                                    P.op("pe", lambda: nc.tensor.matmul(
                                        self.ps[bk][:, :], lhsT=Ml, rhs=rb[i2][:, hf * 4:(hf + 1) * 4, :].rearrange("p h i -> p (h i)"),
                                        start=True, stop=True), r=[rbb[i2], self.cstb], w=[self.psb[bk]])
                                    P.op("act", lambda: nc.scalar.activation(out=dec[hf][:, :, :].rearrange("p h i -> p (h i)"),
                                                                             in_=self.ps[bk][:, :], func=AF.Exp),
                                         r=[self.psb[bk]], w=[decb[hf]])
                                    P.op("dve", lambda: nc.vector.tensor_tensor(
                                        out=wm[i2][:, hf * 4:(hf + 1) * 4, :], in0=dec[hf][:, :, :],
                                        in1=cbm[i2][:, :].unsqueeze(1).broadcast_to([128, 4, 128]), op=ALU.mult),
                                        r=[decb[hf], cbmb[i2]], w=[wmb[i2]])
                                xv = x_tok[:, tb, :].rearrange("p (h q) -> p h q", h=8)
                                P.op("dve", lambda: nc.vector.tensor_tensor(
                                    out=xdt[i2][:, :].rearrange("p (h q) -> p h q", h=8), in0=xv,
                                    in1=dt[:, tb, hcol:hcol + 8].unsqueeze(2).broadcast_to([128, 8, 64]), op=ALU.mult),
                                    r=[xtokb, dtb_], w=[xdtb[i2]])
                                P.op("dve", lambda: nc.vector.tensor_tensor(
                                    out=xdd[i2][:, :].rearrange("p (h q) -> p h q", h=8), in0=xv,
                                    in1=dtd[:, tb, hcol:hcol + 8].unsqueeze(2).broadcast_to([128, 8, 64]), op=ALU.mult),
                                    r=[xtokb, dtb_], w=[xddb[i2]])
                                yi = self.bank()
                                for hh in range(8):
                                    P.op("pe", lambda: nc.tensor.matmul(self.ps[yi][:, hh * 64:(hh + 1) * 64], lhsT=wm[i2][:, hh, :],
                                                                        rhs=xdt[i2][:, hh * 64:(hh + 1) * 64], start=True, stop=True),
                                         r=[wmb[i2], xdtb[i2]], w=[self.psb[yi]], inc=(hh == 7))
                                if not first:
                                    ysb = self.bank()
                                    P.op("pe", lambda: nc.tensor.matmul(self.ps[ysb][:, :], lhsT=CT[:, tok], rhs=Sb16[d][:, :],
                                                                        start=True, stop=True), r=[bcb[1], Sbuf[d]], w=[self.psb[ysb]])
                                    t1 = self.newtmp()
                                    P.op("dve", lambda: nc.vector.tensor_tensor(
                                        out=self.tmp[t1][:, :].rearrange("p (h q) -> p h q", h=8),
                                        in0=self.ps[ysb][:, :].rearrange("p (h q) -> p h q", h=8),
                                        in1=ea[:, tb, hcol:hcol + 8].unsqueeze(2).broadcast_to([128, 8, 64]), op=ALU.mult),
                                        r=[self.psb[ysb], dtb_], w=[self.tmpb[t1]])
                                    P.op("dve", lambda: nc.vector.tensor_tensor(out=self.tmp[t1][:, :], in0=self.tmp[t1][:, :],
                                                                                in1=self.ps[yi][:, :], op=ALU.add),
                                         r=[self.tmpb[t1], self.psb[yi]], w=[self.tmpb[t1]])
                                    P.op("dve", lambda: nc.vector.tensor_tensor(out=yacc[:, tb, :], in0=yacc[:, tb, :],
                                                                                in1=self.tmp[t1][:, :], op=ALU.add),
                                         r=[self.tmpb[t1], yaccb[tb]], w=[yaccb[tb]])
                                else:
                                    P.op("dve", lambda: nc.vector.tensor_tensor(out=yacc[:, tb, :], in0=yacc[:, tb, :],
                                                                                in1=self.ps[yi][:, :], op=ALU.add),
                                         r=[self.psb[yi], yaccb[tb]], w=[yaccb[tb]])
                                su = self.bank()
                                P.op("pe", lambda: nc.tensor.matmul(self.ps[su][:, :], lhsT=Btok[:, tb, :], rhs=xdd[i2][:, :],
                                                                    start=True, stop=True), r=[Btokb, xddb[i2]], w=[self.psb[su]])
                                if first:
                                    P.op("dve", lambda: nc.vector.tensor_copy(out=S[d][:, :], in_=self.ps[su][:, :]),
                                         r=[self.psb[su]], w=[Sbuf[d]])
                                else:
                                    P.op("dve", lambda: nc.vector.tensor_tensor(
                                        out=S[d][:, :].rearrange("p (h q) -> p h q", h=8),
                                        in0=S[d][:, :].rearrange("p (h q) -> p h q", h=8),
                                        in1=etot[:, tb, hcol:hcol + 8].unsqueeze(2).broadcast_to([128, 8, 64]), op=ALU.mult),
                                        r=[Sbuf[d], dtb_], w=[Sbuf[d]])
                                    P.op("dve", lambda: nc.vector.tensor_tensor(out=S[d][:, :], in0=S[d][:, :], in1=self.ps[su][:, :],
                                                                                op=ALU.add), r=[Sbuf[d], self.psb[su]], w=[Sbuf[d]])
                                P.op("act", lambda: nc.scalar.copy(out=Sb16[d][:, :], in_=S[d][:, :]), r=[Sbuf[d]], w=[Sbuf[d]])
                            if r == 0:
                                bk = self.bank()
                                for q in range(4):
                                    P.op("pe", lambda: nc.tensor.transpose(self.ps[bk][:, q * 128:(q + 1) * 128],
                                                                           S[d][:, q * 128:(q + 1) * 128], ident),
                                         r=[Sbuf[d], self.cstb], w=[self.psb[bk]], inc=(q == 3))
                                P.op("dve", lambda: nc.vector.tensor_copy(out=stst[:, :, :],
                                                                          in_=self.ps[bk][:, :].rearrange("p (q n) -> p q n", q=4)),
                                     r=[self.psb[bk]], w=[ststb])
                                P.dma("sp", O["new_ssm"][sq, d, g * 512:(g + 1) * 512, :].rearrange("(q p) n -> p q n", p=128),
                                      stst[:, :, :], None, r=[ststb])
                    for tb in range(NTB):
                        P.op("dve", lambda: nc.vector.tensor_tensor(out=yacc[:, tb, :], in0=yacc[:, tb, :], in1=zs[:, tb, :],
                                                                    op=ALU.mult), r=[yaccb[tb], zsb], w=[yaccb[tb]])
                        t1 = self.newtmp()
                        P.op("act", lambda: nc.scalar.activation(out=self.tmp[t1][:, :], in_=yacc[:, tb, :], func=AF.Square,
                                                                 accum_out=ssq[:, tb, g:g + 1]),
                             r=[yaccb[tb]], w=[self.tmpb[t1], ssqb])
                    for ch in range(4):
                        for t4 in range(NTB // 4):
                            bk = self.bank()
                            for q in range(4):
                                tb = t4 * 4 + q
                                P.op("pe", lambda: nc.tensor.transpose(self.ps[bk][:, q * 128:(q + 1) * 128],
                                                                       yacc[:, tb, ch * 128:(ch + 1) * 128], ident),
                                     r=[yaccb[tb], self.cstb], w=[self.psb[bk]], inc=(q == 3))
                            P.op("act", lambda: nc.scalar.activation(out=ygst[:, ch, t4 * 512:(t4 + 1) * 512], in_=self.ps[bk][:, :],
                                                                     func=AF.Copy, scale=self.vecT[:, 58 + g * 4 + ch:59 + g * 4 + ch]),
                                 r=[self.psb[bk], self.smallb], w=[ygstb])
                    P.dma("sp", ygscr[:, g * 4:(g + 1) * 4, 0:T], ygst[:, :, :], None, r=[ygstb])
                self.fence()
            t1 = self.newtmp()
            rt = self.tmp[t1][:, 0:NTB]
            P.op("dve", lambda: nc.vector.tensor_reduce(out=rt, in_=ssq[:, :, :], axis=AX.X, op=ALU.add),
                 r=[ssqb], w=[self.tmpb[t1]])
            P.op("act", lambda: nc.scalar.activation(out=rt, in_=rt, func=AF.Sqrt, scale=1.0 / D_INNER, bias=self.epsc[:, 0:1]),
                 r=[self.tmpb[t1], self.smallb], w=[self.tmpb[t1]])
            P.op("dve", lambda: nc.vector.reciprocal(out=rt, in_=rt), r=[self.tmpb[t1]], w=[self.tmpb[t1]])
            for t4 in range(NTB // 4):
                t2 = self.newtmp()
                for q in range(4):
                    tb = t4 * 4 + q
                    P.op("dve", lambda: nc.vector.tensor_scalar(out=self.tmp[t2][:, q * 128:(q + 1) * 128], in0=ident,
                                                                scalar1=rt[:, tb:tb + 1], scalar2=None, op0=ALU.mult),
                         r=[self.tmpb[t1], self.cstb], w=[self.tmpb[t2]])
                bk = self.bank()
                P.op("pe", lambda: nc.tensor.matmul(self.ps[bk][:, :], lhsT=self.ones_f[:, :], rhs=self.tmp[t2][:, :],
                                                    start=True, stop=True), r=[self.tmpb[t2], self.onesb], w=[self.psb[bk]])
                P.op("act", lambda: nc.scalar.copy(out=rrow[:, t4 * 512:(t4 + 1) * 512], in_=self.ps[bk][:, :]),
                     r=[self.psb[bk]], w=[rrowb])
            self.fence()
            with ExitStack() as e3:
                ygt = self.sb(e3, "ygt" + tag, [128, 32, 512], BF16)
                ygtb_holder = [None]

                def rhs_of_tile(tt):
                    b_ = Buf()
                    ygtb_holder[0] = b_
                    P.dma("sp", ygt[:, :, :], ygscr[:, :, tt * 512:(tt + 1) * 512], None, w=[b_])
                    return lambda kc: ygt[:, kc, :]

                def rhsb_of_tile(tt):
                    return lambda kc: ygtb_holder[0]

                self.mix_out_residual(I["ssm_w_out"][0], 32, rhs_of_tile, rhsb_of_tile, T, l, r, uy, uyb, win, winb, wsem, tag,
                                      post_scale=lambda tt: (rrow[:, tt * 512:(tt + 1) * 512], rrowb))
                self.fence()


_CONSTS = None


def make_in_maps(inp):
    global _CONSTS
    if _CONSTS is None:
        _CONSTS = _host_consts()
    cst, rope, pc_ctx, pc_smp = _CONSTS
    f = lambda a: np.ascontiguousarray(np.asarray(a, dtype=np.float32))
    shared = {k: f(inp[k]) for k in ("ada_w", "ada_b", "norm_g", "ffn_w_in", "ffn_w_out", "mix_w_in", "pool_w",
                                     "pool_scale", "qk_norm_g", "mix_w_out", "ssm_w_in", "ssm_conv_w", "ssm_conv_b",
                                     "ssm_dt_bias", "ssm_A_log", "ssm_D", "ssm_norm_g", "ssm_w_out")}
    shared.update(cst=cst, rope=rope, pcnt_ctx=pc_ctx, pcnt_smp=pc_smp)
    xp = f(inp["x_prompt"])
    xs = f(inp["x_sample"])
    ck = f(inp["cache_k"])
    cv = f(inp["cache_v"])
    st = f(inp["state_ssm"])
    c = f(inp["c"])
    cc = f(inp["c_ctx"])
    maps = []
    for i in range(N_CORES):
        m = dict(shared)
        m["x_ctx"] = xp[2 * i:2 * i + 2].reshape(2 * L_CTX, D)
        m["x_smp"] = xs[i]
        m["cache_k"] = ck[i, 0].reshape(PAST, 256)
        m["cache_v"] = cv[i, 0].reshape(PAST, 256)
        m["state"] = st[i, 0].reshape(2, NH * HP, NS)
        m["cond"] = np.stack([cc, c[i]], axis=0)
        maps.append(m)
    return maps


def run(inp, cfg=None, n_cores=N_CORES):
    cfg = cfg or {}
    b = Builder(cfg)
    nc = b.build()
    maps = make_in_maps(inp)[:n_cores]
    res = run_bass_kernel_spmd(nc, maps, core_ids=list(range(n_cores)))
    return res.results


def kernel(**inputs):
    rs = run(inputs)
    y_prompt = np.stack([r["y_ctx"] for r in rs]).reshape(16, L_CTX, D)
    y_sample = np.stack([r["y_smp"] for r in rs]).reshape(8, L_SMP, D)
    new_k = np.stack([r["new_k"] for r in rs]).reshape(16, 1, L_CTX, 2, 128)
    new_v = np.stack([r["new_v"] for r in rs]).reshape(16, 1, L_CTX, 2, 128)
    new_ssm = np.stack([r["new_ssm"] for r in rs]).reshape(16, 1, 2, NH, HP, NS)
    return (y_prompt.astype(np.float32), y_sample.astype(np.float32), new_k.astype(np.float32),
            new_v.astype(np.float32), new_ssm.astype(np.float32))
```

```python
import math
from contextlib import ExitStack

import numpy as np
import concourse.bass as bass
import concourse.mybir as mybir
from concourse.bass_utils import run_bass_kernel_spmd

F32 = mybir.dt.float32
BF16 = mybir.dt.bfloat16
AF = mybir.ActivationFunctionType
ALU = mybir.AluOpType
AX = mybir.AxisListType

N_CORES = 8
D = 2048
KC = D // 128
DFF = 5632
JH = DFF // 128
NMOD = 9
EPS = 1e-6
MIX_IN = 2560
D_INNER = 4096
CONV_DIM = 6144
SSM_IN = 10368
NH = 64
HP = 64
NG = 8
NS = 128
PAST = 512
L_CTX = 256
L_SMP = 1024


class TL:
    __slots__ = ("sem", "n", "name")

    def __init__(self, sem, name):
        self.sem = sem
        self.n = 0
        self.name = name


class Buf:
    __slots__ = ("w", "r", "excl")

    def __init__(self):
        self.w = {}
        self.r = {}
        self.excl = False


def bufs(*shape):
    a = np.empty(shape, dtype=object)
    for idx in np.ndindex(*shape):
        a[idx] = Buf()
    return a


def flat(*items):
    out = []
    for it in items:
        if it is None:
            continue
        if isinstance(it, Buf):
            out.append(it)
        elif isinstance(it, np.ndarray):
            out.extend(it.ravel().tolist())
        else:
            for x in it:
                out.extend(flat(x))
    return out


class Prog:
    def __init__(self, nc, es):
        self.nc = nc
        self.es = es
        self.eng = {"pe": nc.tensor, "dve": nc.vector, "act": nc.scalar, "pool": nc.gpsimd, "sp": nc.sync}
        self.tl = {k: TL(es.enter_context(nc.semaphore("s_" + k)), k) for k in self.eng}
        self.seen = {k: {} for k in self.eng}
        self.nsem = 5
        self.out_sems = []
        self.dsems = []
        self.rings = {}
        self.last = {}
        self.ring_pos = {}
        self.fsem = TL(es.enter_context(nc.semaphore("s_fence")), "fence")

    def dsem(self, name):
        self.nsem += 1
        tl = TL(self.es.enter_context(self.nc.semaphore("d_" + name)), name)
        self.dsems.append(tl)
        return tl

    def flush(self, e):
        ins = self.last.get(e)
        if ins is not None:
            tl = self.tl[e]
            ins.then_inc(tl.sem, 1)
            tl.n += 1
            self.last[e] = None

    def _waits(self, e, r, w):
        need = {}
        own = self.tl[e]
        for b in r:
            for tl, v in b.w.items():
                if tl is own and e == "pe":
                    continue
                if need.get(tl, 0) < v:
                    need[tl] = v
            if b.excl:
                for tl, v in b.r.items():
                    if tl is own:
                        continue
                    if need.get(tl, 0) < v:
                        need[tl] = v
        for b in w:
            for tl, v in b.w.items():
                if tl is own and e == "pe":
                    continue
                if need.get(tl, 0) < v:
                    need[tl] = v
            for tl, v in b.r.items():
                if tl is own and e == "pe":
                    continue
                if need.get(tl, 0) < v:
                    need[tl] = v
        seen = self.seen[e]
        h = self.eng[e]
        for tl, v in need.items():
            if seen.get(tl, 0) >= v:
                continue
            if v > tl.n:
                assert tl.name in self.eng and v == tl.n + 1, (tl.name, v, tl.n)
                self.flush(tl.name)
            h.wait_ge(tl.sem, v)
            seen[tl] = v

    def op(self, e, fn, r=(), w=(), inc=True):
        r = flat(r)
        w = flat(w)
        self._waits(e, r, w)
        ins = fn()
        tl = self.tl[e]
        self.last[e] = ins
        v = tl.n + 1
        for b in w:
            b.w[tl] = v
        for b in r:
            if b.r.get(tl, 0) < v:
                b.r[tl] = v
        return ins

    def ring_next(self, e):
        ring = self.rings.setdefault(e, [])
        if len(ring) < 12:
            tl = self.dsem("ring_%s%d" % (e, len(ring)))
            ring.append(tl)
            self.ring_pos[e] = len(ring) - 1
            return tl
        i = (self.ring_pos[e] + 1) % len(ring)
        self.ring_pos[e] = i
        tl = ring[i]
        if self.seen[e].get(tl, 0) < tl.n:
            self.eng[e].wait_ge(tl.sem, tl.n)
            self.seen[e][tl] = tl.n
        return tl

    def dma(self, e, out, in_, ds=None, r=(), w=(), **kw):
        r = flat(r)
        w = flat(w)
        self._waits(e, r, w)
        if ds is None:
            ds = self.ring_next(e)
        ins = self.eng[e].dma_start(out=out, in_=in_, **kw)
        ins.then_inc(ds.sem, 16)
        ds.n += 16
        for b in w:
            b.w[ds] = ds.n
        for b in r:
            b.r[ds] = ds.n
        return ins

    def wait_all(self, e, tls):
        h = self.eng[e]
        for tl in tls:
            if tl.n > 0:
                h.wait_ge(tl.sem, tl.n)


def _host_consts():
    k = np.arange(128)
    ident = np.eye(128, dtype=np.float32)
    U = (k[:, None] <= k[None, :]).astype(np.float32)
    Li = (k[:, None] >= k[None, :]).astype(np.float32)
    Ls = (k[:, None] > k[None, :]).astype(np.float32)
    Us = (k[:, None] < k[None, :]).astype(np.float32)
    R = np.zeros((128, 128), np.float32)
    for p in range(128):
        if (p % 64) < 32:
            R[p, p + 32] = -1.0
        else:
            R[p, p - 32] = 1.0
    Rt = np.ascontiguousarray(R.T)
    cst = np.concatenate([ident, U, Li, Ls, Us, Rt], axis=1)
    t = np.arange(L_SMP)
    row = (t // 64).astype(np.float32)
    col = (t % 64).astype(np.float32)
    inv_freq = (10000.0 ** (-np.arange(32, dtype=np.float32) / 32)).astype(np.float32)
    rope = np.zeros((128, 2, L_SMP), np.float32)
    for p in range(128):
        pos = row if p < 64 else col
        ang = (pos * inv_freq[p % 32]).astype(np.float32)
        rope[p, 0] = np.cos(ang)
        rope[p, 1] = np.sin(ang)
    def cnt(L):
        o = np.zeros((4, L), np.float32)
        tt = np.arange(L)
        for gi, w in enumerate((2, 4, 8, 16)):
            lo = np.clip(tt - w // 2, 0, L)
            hi = np.clip(tt - w // 2 + w, 0, L)
            o[gi] = 1.0 / (hi - lo).astype(np.float32)
        return np.ascontiguousarray(np.broadcast_to(o[None], (128, 4, L)))
    return cst, rope, cnt(L_CTX), cnt(L_SMP)


C_ID, C_U, C_LI, C_LS, C_US, C_RT = range(6)


class Builder:
    def __init__(self, cfg):
        self.cfg = cfg
        self.nc = bass.Bass("TRN2", target_bir_lowering=False)

    def din(self, name, shape, dt=F32):
        return self.nc.dram_tensor(name, list(shape), dt, kind="ExternalInput").ap()

    def dout(self, name, shape, dt=F32):
        return self.nc.dram_tensor(name, list(shape), dt, kind="ExternalOutput").ap()

    def sb(self, es, name, shape, dt):
        return es.enter_context(self.nc.sbuf_tensor("t_" + name, list(shape), dt))

    def build(self):
        nc = self.nc
        cfg = self.cfg
        I = {}
        I["x_ctx"] = self.din("x_ctx", [2 * L_CTX, D])
        I["x_smp"] = self.din("x_smp", [L_SMP, D])
        I["cache_k"] = self.din("cache_k", [PAST, 256])
        I["cache_v"] = self.din("cache_v", [PAST, 256])
        I["state"] = self.din("state", [2, NH * HP, NS])
        I["cond"] = self.din("cond", [2, D])
        I["ada_w"] = self.din("ada_w", [2, D, NMOD * D])
        I["ada_b"] = self.din("ada_b", [2, NMOD * D])
        I["norm_g"] = self.din("norm_g", [2, 6, D])
        I["ffn_w_in"] = self.din("ffn_w_in", [2, 2, D, 2 * DFF])
        I["ffn_w_out"] = self.din("ffn_w_out", [2, 2, DFF, D])
        I["mix_w_in"] = self.din("mix_w_in", [1, D, MIX_IN])
        I["pool_w"] = self.din("pool_w", [1, 4, 256, 256])
        I["pool_scale"] = self.din("pool_scale", [1, 1024])
        I["qk_norm_g"] = self.din("qk_norm_g", [1, 2, 128])
        I["mix_w_out"] = self.din("mix_w_out", [1, D, D])
        I["ssm_w_in"] = self.din("ssm_w_in", [1, D, SSM_IN])
        I["ssm_conv_w"] = self.din("ssm_conv_w", [1, CONV_DIM, 3])
        I["ssm_conv_b"] = self.din("ssm_conv_b", [1, CONV_DIM])
        I["ssm_dt_bias"] = self.din("ssm_dt_bias", [1, 2, NH])
        I["ssm_A_log"] = self.din("ssm_A_log", [1, 2, NH])
        I["ssm_D"] = self.din("ssm_D", [1, 2, NH])
        I["ssm_norm_g"] = self.din("ssm_norm_g", [1, D_INNER])
        I["ssm_w_out"] = self.din("ssm_w_out", [1, D_INNER, D])
        I["cst"] = self.din("cst", [128, 768])
        I["rope"] = self.din("rope", [128, 2, L_SMP])
        I["pcnt_ctx"] = self.din("pcnt_ctx", [128, 4, L_CTX])
        I["pcnt_smp"] = self.din("pcnt_smp", [128, 4, L_SMP])
        O = {}
        O["y_ctx"] = self.dout("y_ctx", [2 * L_CTX, D])
        O["y_smp"] = self.dout("y_smp", [L_SMP, D])
        O["new_k"] = self.dout("new_k", [2 * L_CTX, 256])
        O["new_v"] = self.dout("new_v", [2 * L_CTX, 256])
        O["new_ssm"] = self.dout("new_ssm", [2, 2, NH * HP, NS])
        self.I, self.O = I, O
        self.hscr = nc.dram_tensor("hscr", [128, KC, L_SMP], F32, kind="Internal").ap()
        self.ygscr = nc.dram_tensor("ygscr", [128, 32, L_SMP], BF16, kind="Internal").ap()
        self.wi_scr = nc.dram_tensor("wi_scr", [4 * JH, 128, KC * 256], BF16, kind="Internal").ap()
        self.wo_scr = nc.dram_tensor("wo_scr", [4 * KC, 128, JH * 128], BF16, kind="Internal").ap()

        with ExitStack() as es:
            P = Prog(nc, es)
            self.P = P
            self.ps = [es.enter_context(nc.psum_tensor("ps%d" % i, [128, 512], F32)) for i in range(8)]
            self.psb = bufs(8)
            for b_ in self.psb:
                b_.excl = True
            self.ps_rr = 0
            self.xs_rr = 0
            self.held = set()
            self.setup(es)
            for r in cfg.get("passes", (1, 0)):
                self.run_pass(r)
            self.fence()
        return nc

    def bank(self):
        while True:
            i = self.ps_rr
            self.ps_rr = (self.ps_rr + 1) % 7
            if i not in self.held:
                return i

    def setup(self, es):
        nc, P, I = self.nc, self.P, self.I
        sb = self.sb
        self.cst = sb(es, "cst", [128, 768], F32)
        self.cstb = Buf()
        self.ones_f = sb(es, "ones_f", [128, 128], F32)
        self.ones_b = sb(es, "ones_b", [128, 128], BF16)
        self.onesb = Buf()
        self.mods = sb(es, "mods", [128, 2, 2, NMOD * KC], F32)
        self.modsb = Buf()
        self.normgT = sb(es, "normgT", [128, 192], F32)
        self.vecT = sb(es, "vecT", [128, 90], F32)
        self.convw = sb(es, "convw", [128, 48, 3], F32)
        self.gs = sb(es, "gs", [128, 12, KC], F32)
        self.gg = sb(es, "gg", [128, 12, KC], F32)
        self.dtb = sb(es, "dtb", [128, 128], F32)
        self.Aneg = sb(es, "Aneg", [128, 128], F32)
        self.Dsum = sb(es, "Dsum", [128, 64], F32)
        self.smallb = Buf()
        self.epsc = sb(es, "epsc", [128, 1], F32)
        self.onec = sb(es, "onec", [128, 1], F32)
        self.tmp = [sb(es, "tmp%d" % i, [128, 512], F32) for i in range(6)]
        self.tmpb = bufs(6)
        self.rstd = sb(es, "rstd", [128, 512], F32)
        self.rstdb = Buf()
        self.tmp_rr = 0

        P.dma("sp", self.cst[:], I["cst"], None, w=[self.cstb])
        P.op("dve", lambda: nc.vector.memset(self.ones_f[:], 1.0), w=[self.onesb])
        P.op("dve", lambda: nc.vector.memset(self.ones_b[:], 1.0), w=[self.onesb])
        P.op("dve", lambda: nc.vector.memset(self.epsc[:], EPS), w=[self.smallb])
        P.op("dve", lambda: nc.vector.memset(self.onec[:], 1.0), w=[self.smallb])

        with ExitStack() as s2:
            stage = [sb(s2, "stg%d" % i, [128, 128], F32) for i in range(2)]
            stageb = bufs(2)
            adabT = sb(s2, "adabT", [128, 288], F32)
            adabTb = Buf()
            condT = sb(s2, "condT", [128, 32], F32)
            scT = sb(s2, "scT", [128, KC, 2], BF16)
            condb = Buf()
            si = [0]

            def load_T(dst_ap, rows_ap, R, wb):
                k = si[0] % 2
                si[0] += 1
                P.dma("sp", stage[k][0:R, :], rows_ap, None, w=[stageb[k]])
                bk = self.bank()
                P.op("pe", lambda: nc.tensor.transpose(self.ps[bk][:, 0:R], stage[k][0:R, :],
                                                       self.cst[0:R, C_ID * 128:C_ID * 128 + R]),
                     r=[stageb[k], self.cstb], w=[self.psb[bk]])
                P.op("dve", lambda: nc.vector.tensor_copy(out=dst_ap, in_=self.ps[bk][:, 0:R]),
                     r=[self.psb[bk]], w=[wb])

            ab = I["ada_b"].rearrange("l (c p) -> (l c) p", p=128)
            for i in range(3):
                load_T(adabT[:, i * 96:(i + 1) * 96], ab[i * 96:(i + 1) * 96, :], 96, adabTb)
            ng = I["norm_g"].rearrange("l i (c p) -> (l i c) p", p=128)
            for i in range(2):
                load_T(self.normgT[:, i * 96:(i + 1) * 96], ng[i * 96:(i + 1) * 96, :], 96, self.smallb)
            load_T(condT[:, :], I["cond"].rearrange("r (c p) -> (r c) p", p=128), 32, condb)
            k = si[0] % 2
            si[0] += 1
            P.dma("sp", stage[k][0:8, :], I["pool_scale"].rearrange("o (c p) -> (o c) p", p=128), None, w=[stageb[k]])
            P.dma("sp", stage[k][8:10, :], I["qk_norm_g"].rearrange("o i p -> (o i) p"), None, w=[stageb[k]])
            P.dma("sp", stage[k][10:58, :], I["ssm_conv_b"].rearrange("o (c p) -> (o c) p", p=128), None, w=[stageb[k]])
            P.dma("sp", stage[k][58:90, :], I["ssm_norm_g"].rearrange("o (c p) -> (o c) p", p=128), None, w=[stageb[k]])
            bk = self.bank()
            P.op("pe", lambda: nc.tensor.transpose(self.ps[bk][:, 0:90], stage[k][0:90, :],
                                                   self.cst[0:90, C_ID * 128:C_ID * 128 + 90]),
                 r=[stageb[k], self.cstb], w=[self.psb[bk]])
            P.op("dve", lambda: nc.vector.tensor_copy(out=self.vecT[:, :], in_=self.ps[bk][:, 0:90]),
                 r=[self.psb[bk]], w=[self.smallb])
            cw = I["ssm_conv_w"].rearrange("o (c p) j -> p (o c) j", p=128)
            for i in range(4):
                P.dma("sp", self.convw[:, i * 12:(i + 1) * 12, :], cw[:, i * 12:(i + 1) * 12, :], None, w=[self.smallb])
            P.dma("sp", self.dtb[:, :], I["ssm_dt_bias"].rearrange("o a h -> o (a h)").partition_broadcast(128).squeeze(1),
                  None, w=[self.smallb])
            P.dma("sp", self.Aneg[:, :], I["ssm_A_log"].rearrange("o a h -> o (a h)").partition_broadcast(128).squeeze(1),
                  None, w=[self.smallb])
            dtmp = self.tmp[0]
            P.dma("sp", dtmp[:, 0:128], I["ssm_D"].rearrange("o a h -> o (a h)").partition_broadcast(128).squeeze(1),
                  None, w=[self.tmpb[0]])
            P.op("act", lambda: nc.scalar.activation(out=self.Aneg[:, :], in_=self.Aneg[:, :], func=AF.Exp),
                 r=[self.smallb], w=[self.smallb])
            P.op("dve", lambda: nc.vector.tensor_scalar(out=self.Aneg[:, :], in0=self.Aneg[:, :], scalar1=-1.0,
                                                        scalar2=None, op0=ALU.mult),
                 r=[self.smallb], w=[self.smallb])
            P.op("dve", lambda: nc.vector.tensor_tensor(out=self.Dsum[:, :], in0=dtmp[:, 0:64], in1=dtmp[:, 64:128],
                                                        op=ALU.add),
                 r=[self.tmpb[0]], w=[self.smallb])
            P.op("act", lambda: nc.scalar.activation(out=scT[:, :, :].rearrange("p c r -> p r c"),
                                                     in_=condT[:, :].rearrange("p (r c) -> p r c", r=2),
                                                     func=AF.Silu),
                 r=[condb], w=[condb])
            NB = 1024
            aslot = [sb(s2, "aslot%d" % i, [128, KC, NB], BF16) for i in range(2)]
            aslotb = bufs(2)
            asem = [P.dsem("aslot%d" % i) for i in range(2)]
            blocks = [(l, b) for l in range(2) for b in range(NMOD * D // NB)]

            def aload(i):
                l, b = blocks[i]
                src = I["ada_w"][l, :, b * NB:(b + 1) * NB].rearrange("(k p) n -> p k n", p=128)
                P.dma("pool", aslot[i % 2][:, :, :], src, asem[i % 2], w=[aslotb[i % 2]])

            aload(0)
            for i, (l, b) in enumerate(blocks):
                if i + 1 < len(blocks):
                    aload(i + 1)
                sl = aslot[i % 2]
                bk = self.bank()
                pv = self.ps[bk][:, 0:16].rearrange("p (c r) -> p c r", r=2)
                for cc in range(8):
                    for kc in range(KC):
                        last = (cc == 7 and kc == KC - 1)
                        P.op("pe", lambda: nc.tensor.matmul(pv[:, cc, :], lhsT=sl[:, kc, cc * 128:(cc + 1) * 128],
                                                            rhs=scT[:, kc, :], start=(kc == 0), stop=(kc == KC - 1)),
                             r=[aslotb[i % 2], condb], w=[self.psb[bk]], inc=last)
                for r in range(2):
                    P.op("dve", lambda: nc.vector.tensor_tensor(
                        out=self.mods[:, l, r, b * 8:(b + 1) * 8], in0=pv[:, :, r],
                        in1=adabT[:, l * 144 + b * 8:l * 144 + (b + 1) * 8], op=ALU.add),
                        r=[self.psb[bk], adabTb], w=[self.modsb])
            for l in range(2):
                for s in range(3):
                    for r in range(2):
                        idx = (l * 3 + s) * 2 + r
                        wgt = 1.0 if s == 1 else 0.5
                        P.op("dve", lambda: nc.vector.scalar_tensor_tensor(
                            out=self.gs[:, idx, :], in0=self.mods[:, l, r, (3 * s + 1) * KC:(3 * s + 2) * KC],
                            scalar=1.0, in1=self.normgT[:, (l * 6 + 2 * s) * KC:(l * 6 + 2 * s + 1) * KC],
                            op0=ALU.add, op1=ALU.mult), r=[self.modsb, self.smallb], w=[self.modsb])
                        P.op("dve", lambda: nc.vector.scalar_tensor_tensor(
                            out=self.gg[:, idx, :], in0=self.mods[:, l, r, (3 * s + 2) * KC:(3 * s + 3) * KC],
                            scalar=wgt, in1=self.normgT[:, (l * 6 + 2 * s + 1) * KC:(l * 6 + 2 * s + 2) * KC],
                            op0=ALU.mult, op1=ALU.mult), r=[self.modsb, self.smallb], w=[self.modsb])
            self.fence()

    def fence(self):
        P = self.P
        es = ["pe", "dve", "act", "pool"]
        sp = P.eng["sp"]
        for e in es:
            P.flush(e)
        for tl in [P.tl[e] for e in es] + P.dsems:
            if tl.n > P.seen["sp"].get(tl, 0):
                sp.wait_ge(tl.sem, tl.n)
                P.seen["sp"][tl] = tl.n
        sp.sem_inc(P.fsem.sem, 1)
        P.fsem.n += 1
        for e in es:
            P.eng[e].wait_ge(P.fsem.sem, P.fsem.n)
            for tl in [P.tl[o] for o in es + ["sp"]] + P.dsems:
                P.seen[e][tl] = tl.n

    def sh_ap(self, l, s, r, c):
        return self.mods[:, l, r, 3 * s * KC + c:3 * s * KC + c + 1]

    def gs_ap(self, l, s, r, c):
        idx = (l * 3 + s) * 2 + r
        return self.gs[:, idx, c:c + 1]

    def gg_ap(self, l, s, r, c):
        idx = (l * 3 + s) * 2 + r
        return self.gg[:, idx, c:c + 1]

    def newtmp(self):
        i = self.tmp_rr
        self.tmp_rr = (self.tmp_rr + 1) % 6
        return i

    def rstd_from_stats(self, sbk, dim):
        nc, P = self.nc, self.P
        P.op("act", lambda: nc.scalar.activation(out=self.rstd[:, :], in_=self.ps[sbk][:, :], func=AF.Sqrt,
                                                 scale=1.0 / dim, bias=self.epsc[:, 0:1]),
             r=[self.psb[sbk], self.smallb], w=[self.rstdb])
        P.op("dve", lambda: nc.vector.reciprocal(out=self.rstd[:, :], in_=self.rstd[:, :]),
             r=[self.rstdb], w=[self.rstdb])

    def modulate(self, h, hb, tt, l, s, r, u_ap, ub):
        nc, P = self.nc, self.P
        sbk = 7
        for c in range(KC):
            ti = self.newtmp()
            P.op("act", lambda: nc.scalar.activation(out=self.tmp[ti][:, :], in_=h[:, c, tt * 512:(tt + 1) * 512],
                                                     func=AF.Square), r=[hb[c, tt]], w=[self.tmpb[ti]])
            P.op("pe", lambda: nc.tensor.matmul(self.ps[sbk][:, :], lhsT=self.ones_f[:, :], rhs=self.tmp[ti][:, :],
                                                start=(c == 0), stop=(c == KC - 1)),
                 r=[self.tmpb[ti], self.onesb], w=[self.psb[sbk]])
        self.rstd_from_stats(sbk, D)
        for c in range(KC):
            ti = self.newtmp()
            P.op("dve", lambda: nc.vector.tensor_tensor(out=self.tmp[ti][:, :], in0=h[:, c, tt * 512:(tt + 1) * 512],
                                                        in1=self.rstd[:, :], op=ALU.mult),
                 r=[hb[c, tt], self.rstdb], w=[self.tmpb[ti]])
            P.op("act", lambda: nc.scalar.activation(out=u_ap(c), in_=self.tmp[ti][:, :], func=AF.Identity,
                                                     scale=self.gs_ap(l, s, r, c), bias=self.sh_ap(l, s, r, c)),
                 r=[self.tmpb[ti], self.modsb], w=[ub(c)])

    def residual(self, h, hb, tt, l, s, r, y_ap, yb, sbk, dim=D):
        nc, P = self.nc, self.P
        self.rstd_from_stats(sbk, dim)
        for c in range(KC):
            ti = self.newtmp()
            P.op("dve", lambda: nc.vector.tensor_tensor(out=self.tmp[ti][:, :], in0=y_ap(c), in1=self.rstd[:, :],
                                                        op=ALU.mult), r=[yb(c), self.rstdb], w=[self.tmpb[ti]])
            hv = h[:, c, tt * 512:(tt + 1) * 512]
            P.op("dve", lambda: nc.vector.scalar_tensor_tensor(out=hv, in0=self.tmp[ti][:, :],
                                                               scalar=self.gg_ap(l, s, r, c), in1=hv,
                                                               op0=ALU.mult, op1=ALU.add),
                 r=[self.tmpb[ti], hb[c, tt], self.modsb], w=[hb[c, tt]])

    def out_proj_residual(self, w_ap, KCin, rhs_ap, rhsb, slots, slotb, ssem, h, hb, tt, l, s, r, uy, uyb, post_scale=None, wload_fn=None):
        nc, P = self.nc, self.P
        sbk = 7
        ns = len(slots)

        def wload(oc):
            if wload_fn is not None:
                return wload_fn(oc, oc % ns)
            src = w_ap[:, oc * 128:(oc + 1) * 128].rearrange("(k p) n -> p k n", p=128)
            P.dma("pool", slots[oc % ns][:, 0:KCin, :], src, ssem[oc % ns], w=[slotb[oc % ns]])

        pend = None
        for oc in range(min(ns - 1, KC)):
            wload(oc)
        for oc in range(KC):
            if oc + ns - 1 < KC:
                wload(oc + ns - 1)
            bk = self.bank()
            sl = slots[oc % ns]
            for kc in range(KCin):
                P.op("pe", lambda: nc.tensor.matmul(self.ps[bk][:, :], lhsT=sl[:, kc, :], rhs=rhs_ap(kc),
                                                    start=(kc == 0), stop=(kc == KCin - 1)),
                     r=[slotb[oc % ns], rhsb(kc)], w=[self.psb[bk]], inc=(kc == KCin - 1))
            if pend is not None:
                pend()
            yv = uy[:, oc * 512:(oc + 1) * 512]
            ti = self.newtmp()
            if post_scale is None:
                P.op("act", lambda: nc.scalar.copy(out=yv, in_=self.ps[bk][:, :]), r=[self.psb[bk]], w=[uyb[oc]])
                P.op("act", lambda: nc.scalar.activation(out=self.tmp[ti][:, :], in_=self.ps[bk][:, :], func=AF.Square),
                     r=[self.psb[bk]], w=[self.tmpb[ti]])
            else:
                ps_ap, ps_b = post_scale
                P.op("dve", lambda: nc.vector.tensor_tensor(out=yv, in0=self.ps[bk][:, :], in1=ps_ap, op=ALU.mult),
                     r=[self.psb[bk], ps_b], w=[uyb[oc]])
                P.op("act", lambda: nc.scalar.activation(out=self.tmp[ti][:, :], in_=yv, func=AF.Square),
                     r=[uyb[oc]], w=[self.tmpb[ti]])

            def mk(ti=ti, oc=oc):
                def f():
                    P.op("pe", lambda: nc.tensor.matmul(self.ps[sbk][:, :], lhsT=self.ones_f[:, :],
                                                        rhs=self.tmp[ti][:, :], start=(oc == 0), stop=(oc == KC - 1)),
                         r=[self.tmpb[ti], self.onesb], w=[self.psb[sbk]])
                return f
            pend = mk()
        pend()
        self.residual(h, hb, tt, l, s, r, lambda c: uy[:, c * 512:(c + 1) * 512], lambda c: uyb[c], sbk)

    def load_x_tile(self, xs, xsb, r, tt, ht, hb):
        nc, P, I = self.nc, self.P, self.I
        x = I["x_ctx"] if r == 0 else I["x_smp"]
        for tk in range(4):
            row0 = (tt * 4 + tk) * 128
            for hf in range(2):
                k = self.xs_rr % 2
                self.xs_rr += 1
                P.dma("sp", xs[k][:, :], x[row0:row0 + 128, hf * 1024:(hf + 1) * 1024], None, w=[xsb[k]])
                for c4 in range(2):
                    bk = self.bank()
                    for q in range(4):
                        c = c4 * 4 + q
                        P.op("pe", lambda: nc.tensor.transpose(self.ps[bk][:, q * 128:(q + 1) * 128],
                                                               xs[k][:, c * 128:(c + 1) * 128],
                                                               self.cst[:, C_ID * 128:(C_ID + 1) * 128]),
                             r=[xsb[k], self.cstb], w=[self.psb[bk]])
                    c0 = hf * 8 + c4 * 4
                    P.op("dve", lambda: nc.vector.tensor_copy(
                        out=ht[:, c0:c0 + 4, tk * 128:(tk + 1) * 128],
                        in_=self.ps[bk][:, :].rearrange("p (q t) -> p q t", q=4)),
                        r=[self.psb[bk]], w=[hb[c0:c0 + 4, 0]])

    def store_y_tile(self, ost, ostb, r, tt, ht, hb):
        nc, P, O = self.nc, self.P, self.O
        y = O["y_ctx"] if r == 0 else O["y_smp"]
        for tk in range(4):
            row0 = (tt * 4 + tk) * 128
            for qd in range(4):
                k = self.xs_rr % 2
                self.xs_rr += 1
                bk = self.bank()
                for q in range(4):
                    c = qd * 4 + q
                    P.op("pe", lambda: nc.tensor.transpose(self.ps[bk][:, q * 128:(q + 1) * 128],
                                                           ht[:, c, tk * 128:(tk + 1) * 128],
                                                           self.cst[:, C_ID * 128:(C_ID + 1) * 128]),
                         r=[hb[c, 0], self.cstb], w=[self.psb[bk]])
                P.op("dve", lambda: nc.vector.tensor_copy(out=ost[k][:, :], in_=self.ps[bk][:, :]),
                     r=[self.psb[bk]], w=[ostb[k]])
                P.dma("sp", y[row0:row0 + 128, qd * 512:(qd + 1) * 512], ost[k][:, :], None, r=[ostb[k]])

    def h_load(self, h, hb, T, sem):
        P = self.P
        for tt in range(T // 512):
            P.dma("sp", h[:, :, tt * 512:(tt + 1) * 512], self.hscr[:, :, tt * 512:(tt + 1) * 512], None, w=[hb[:, tt]])

    def h_store(self, h, hb, T, sem):
        P = self.P
        for tt in range(T // 512):
            P.dma("sp", self.hscr[:, :, tt * 512:(tt + 1) * 512], h[:, :, tt * 512:(tt + 1) * 512], None, r=[hb[:, tt]])

    def ffn(self, T, l, f, r, uy, uyb, src, dst):
        nc, P, I = self.nc, self.P, self.I
        s = 0 if f == 0 else 2
        NT = T // 512
        w_in = I["ffn_w_in"][l, f]
        w_out = I["ffn_w_out"][l, f]
        uyh = uy[:, :].bitcast(BF16)
        tag = "%d%d%d" % (r, l, f)
        hscr = self.hscr
        with ExitStack() as es:
            g = self.sb(es, "g_" + tag, [128, JH, T], BF16)
            gb = bufs(JH, NT)
            ub = bufs(KC, NT)
            with ExitStack() as ea:
                ht = self.sb(ea, "hA_" + tag, [128, KC, 512], F32)
                if src == "x":
                    xs = [self.sb(ea, "xs%d_%s" % (i, tag), [128, 1024], F32) for i in range(2)]
                    xsb = bufs(2)
                for tt in range(NT):
                    hb = bufs(KC, 1)
                    if src == "x":
                        self.load_x_tile(xs, xsb, r, tt, ht, hb)
                        P.dma("sp", hscr[:, :, tt * 512:(tt + 1) * 512], ht[:, :, :], None, r=[hb])
                    else:
                        P.dma("sp", ht[:, :, :], hscr[:, :, tt * 512:(tt + 1) * 512], None, w=[hb])
                    if not self.cfg.get("ffn_io_only"):
                        self.modulate(ht, hb, 0, l, s, r, lambda c: uyh[:, c * T + tt * 512:c * T + (tt + 1) * 512],
                                      lambda c: ub[c, tt])
                    self.fence()
            with ExitStack() as eb:
                NW = 4
                win = [self.sb(eb, "wi%d_%s" % (i, tag), [128, KC, 256], BF16) for i in range(NW)]
                winb = bufs(NW)
                wsem = [P.dsem("wi%d_%s" % (i, tag)) for i in range(NW)]

                fi = l * 2 + f
                have_scr = (r == 0) and (1 in self.cfg.get("passes", (1, 0)))

                def wload(j):
                    k = j % NW
                    flat_slot = win[k][:, :, :].rearrange("p k n -> p (k n)")
                    if have_scr:
                        P.dma("sp", flat_slot, self.wi_scr[fi * JH + j], wsem[k], w=[winb[k]])
                        return
                    for half in range(2):
                        srcw = w_in[:, half * DFF + j * 128: half * DFF + (j + 1) * 128].rearrange("(k p) n -> p k n", p=128)
                        P.dma("pool", win[k][:, :, half * 128:(half + 1) * 128], srcw, wsem[k], w=[winb[k]])
                    if r == 1:
                        P.dma("sp", self.wi_scr[fi * JH + j], flat_slot, None, r=[winb[k]])

                JN = 0 if self.cfg.get("ffn_io_only") else JH
                for j in range(min(NW - 1, JN)):
                    wload(j)
                for j in range(JN):
                    if j + NW - 1 < JN:
                        wload(j + NW - 1)
                    sl = win[j % NW]
                    banks = [[self.bank() for tt in range(NT)] for half in range(2)]
                    for half in range(2):
                        for kc in range(KC):
                            for tt in range(NT):
                                bk = banks[half][tt]
                                P.op("pe", lambda: nc.tensor.matmul(
                                    self.ps[bk][:, :], lhsT=sl[:, kc, half * 128:(half + 1) * 128],
                                    rhs=uyh[:, kc * T + tt * 512:kc * T + (tt + 1) * 512],
                                    start=(kc == 0), stop=(kc == KC - 1)),
                                    r=[winb[j % NW], ub[kc, tt]], w=[self.psb[bk]])
                    for tt in range(NT):
                        ti = self.newtmp()
                        P.op("act", lambda: nc.scalar.activation(out=self.tmp[ti][:, :], in_=self.ps[banks[0][tt]][:, :],
                                                                 func=AF.Silu), r=[self.psb[banks[0][tt]]], w=[self.tmpb[ti]])
                        P.op("dve", lambda: nc.vector.tensor_tensor(out=g[:, j, tt * 512:(tt + 1) * 512], in0=self.tmp[ti][:, :],
                                                                    in1=self.ps[banks[1][tt]][:, :], op=ALU.mult),
                             r=[self.tmpb[ti], self.psb[banks[1][tt]]], w=[gb[j, tt]])
                self.fence()
            with ExitStack() as ec:
                ht = self.sb(ec, "hC_" + tag, [128, KC, 512], F32)
                NO = 3 if T == 512 else 2
                wout = [self.sb(ec, "wo%d_%s" % (i, tag), [128, JH, 128], BF16) for i in range(NO)]
                woutb = bufs(NO)
                wosem = [P.dsem("wo%d_%s" % (i, tag)) for i in range(NO)]
                if dst == "y":
                    ost = [self.sb(ec, "os%d_%s" % (i, tag), [128, 512], F32) for i in range(2)]
                    ostb = bufs(2)
                for tt in range(NT):
                    hb = bufs(KC, 1)
                    P.dma("sp", ht[:, :, :], hscr[:, :, tt * 512:(tt + 1) * 512], None, w=[hb])
                    def wo_load(oc, k, tt=tt):
                        flat_slot = wout[k][:, :, :].rearrange("p k n -> p (k n)")
                        if have_scr or tt > 0:
                            P.dma("sp", flat_slot, self.wo_scr[fi * KC + oc], wosem[k], w=[woutb[k]])
                            return
                        srcw = w_out[:, oc * 128:(oc + 1) * 128].rearrange("(k p) n -> p k n", p=128)
                        P.dma("pool", wout[k][:, :, :], srcw, wosem[k], w=[woutb[k]])
                        if r == 1:
                            P.dma("sp", self.wo_scr[fi * KC + oc], flat_slot, None, r=[woutb[k]])

                    if not self.cfg.get("ffn_io_only"):
                        self.out_proj_residual(w_out, JH, lambda kc: g[:, kc, tt * 512:(tt + 1) * 512], lambda kc: gb[kc, tt],
                                               wout, woutb, wosem, ht, hb, 0, l, s, r, uy, uyb, wload_fn=wo_load)
                    if dst == "y":
                        self.store_y_tile(ost, ostb, r, tt, ht, hb)
                    else:
                        P.dma("sp", hscr[:, :, tt * 512:(tt + 1) * 512], ht[:, :, :], None, r=[hb])
                    self.fence()

    def run_pass(self, r):
        nc, P = self.nc, self.P
        cfg = self.cfg
        T = 512 if r == 0 else 1024
        with ExitStack() as es:
            uy = self.sb(es, "uy%d" % r, [128, 8192], F32)
            uyb = bufs(KC)
            phases = []
            for l in range(2):
                if cfg.get("ffn", True):
                    phases.append(("ffn", l, 0))
                if cfg.get("mixer", True) and l in cfg.get("mix_layers", (0, 1)):
                    phases.append(("mix", l, 0))
                if cfg.get("ffn", True):
                    phases.append(("ffn", l, 1))
            assert phases[0][0] == "ffn" and phases[-1][0] == "ffn"
            for i, (kind, l, f) in enumerate(phases):
                if kind == "ffn":
                    self.ffn(T, l, f, r, uy, uyb, "x" if i == 0 else "scr", "y" if i == len(phases) - 1 else "scr")
                    continue
                with ExitStack() as em:
                    win = [self.sb(em, "win%d_%d%d" % (k_, r, l), [128, KC, 256], BF16) for k_ in range(2)]
                    winb = bufs(2)
                    wsem = [P.dsem("win%d_%d%d" % (k_, r, l)) for k_ in range(2)]
                    if l == 0:
                        self.even_mixer(T, r, uy, uyb, win, winb, wsem)
                    else:
                        self.odd_mixer(T, r, uy, uyb, win, winb, wsem)
                    self.fence()
            self.fence()

    def bank_hold(self):
        i = self.bank()
        self.held.add(i)
        return i

    def bank_release(self, i):
        self.held.discard(i)

    def mix_out_residual(self, w_ap, KCin, rhs_of_tile, rhsb_of_tile, T, l, r, uy, uyb, win, winb, wsem, tag,
                         post_scale=None):
        nc, P = self.nc, self.P
        with ExitStack() as es:
            ht = [self.sb(es, "ht%d_%s" % (i, tag), [128, KC, 512], F32) for i in range(1)]
            slots = [win[0][:, :, 0:128], win[1][:, :, 0:128]] if KCin <= KC else None
            if slots is None:
                wo = [self.sb(es, "mwo%d_%s" % (i, tag), [128, KCin, 128], BF16) for i in range(2)]
                slots = [wo[0][:, :, :], wo[1][:, :, :]]
                slb = bufs(2)
                slsem = [P.dsem("mwo%d_%s" % (i, tag)) for i in range(2)]
            else:
                slb, slsem = winb, wsem
            for tt in range(T // 512):
                htb = bufs(KC, 1)
                P.dma("sp", ht[0][:, :, :], self.hscr[:, :, tt * 512:(tt + 1) * 512], None, w=[htb])
                self.out_proj_residual(w_ap, KCin, rhs_of_tile(tt), rhsb_of_tile(tt), slots, slb, slsem,
                                       ht[0], htb, 0, l, 1, r, uy, uyb, post_scale=(post_scale(tt) if post_scale else None))
                P.dma("sp", self.hscr[:, :, tt * 512:(tt + 1) * 512], ht[0][:, :, :], None, r=[htb])
                self.fence()

    def even_mixer(self, T, r, uy, uyb, win, winb, wsem):
        nc, P, I, O = self.nc, self.P, self.I, self.O
        l, s = 0, 1
        nseq = 2 if r == 0 else 1
        L = T // nseq
        NT = T // 512
        uyh = uy[:, :].bitcast(BF16)
        w_in = I["mix_w_in"][0]
        ident = self.cst[:, C_ID * 128:(C_ID + 1) * 128]
        tag = "e%d" % r
        NK = L + (PAST if r == 1 else 0)
        NKC = NK // 128
        with ExitStack() as es:
            cat = self.sb(es, "cat" + tag, [128, KC, T], BF16)
            catb = bufs(KC, NT)
            qT = self.sb(es, "qT" + tag, [128, 8, T], BF16)
            qTb = bufs(8, NT)
            kT = self.sb(es, "kT" + tag, [128, 2, nseq * NK], BF16)
            kTb = bufs(2)
            vtok = self.sb(es, "vtok" + tag, [128, nseq * NKC, 256], BF16)
            vtokb = Buf()
            gq = self.sb(es, "gq" + tag, [128, 2], F32)
            gqb = Buf()
            ub = bufs(KC, NT)
            P.op("dve", lambda: nc.vector.tensor_scalar(out=gq[:, 0:1], in0=self.vecT[:, 8:9], scalar1=128.0 ** -0.5,
                                                        scalar2=None, op0=ALU.mult), r=[self.smallb], w=[gqb])
            P.op("dve", lambda: nc.vector.tensor_copy(out=gq[:, 1:2], in_=self.vecT[:, 9:10]), r=[self.smallb], w=[gqb])
            with ExitStack() as e1:
                h = self.sb(e1, "h_" + tag, [128, KC, T], F32)
                hb = bufs(KC, NT)
                self.h_load(h, hb, T, None)
                for tt in range(NT):
                    self.modulate(h, hb, tt, l, s, r, lambda c: uyh[:, c * T + tt * 512:c * T + (tt + 1) * 512],
                                  lambda c: ub[c, tt])
                self.fence()
            with ExitStack() as e1:
                xpp = self.sb(e1, "xpp" + tag, [128, nseq, L + 16], F32)
                xppb = Buf()
                sA = self.sb(e1, "sA" + tag, [128, nseq, L + 16], F32)
                sB = self.sb(e1, "sB" + tag, [128, nseq, L + 16], F32)
                sb_ = Buf()
                dbf = self.sb(e1, "dbf" + tag, [128, 2, T], BF16)
                dbfb = bufs(2)
                pcnt = self.sb(e1, "pcnt" + tag, [128, 4, L], F32)
                pcntb = Buf()
                pw = self.sb(e1, "pw" + tag, [128, 4, 2, 256], BF16)
                pwb = Buf()
                P.dma("sp", pcnt[:, :, :], I["pcnt_ctx"] if r == 0 else I["pcnt_smp"], None, w=[pcntb])
                for gi_ in range(4):
                    P.dma("pool", pw[:, gi_, :, :], I["pool_w"][0, gi_].rearrange("(c p) d -> p c d", p=128), None, w=[pwb])
                P.op("dve", lambda: nc.vector.memset(xpp[:, :, :], 0.0), w=[xppb])
                if r == 1:
                    rope = self.sb(e1, "rope" + tag, [128, 2, T], F32)
                    ropeb = Buf()
                    P.dma("sp", rope[:, :, :], I["rope"], None, w=[ropeb])
                    ck = self.sb(e1, "ck" + tag, [128, 4, 256], F32)
                    ckb = Buf()
                    P.dma("sp", ck[:, :, :], I["cache_k"].rearrange("(b p) f -> p b f", p=128), None, w=[ckb])
                    P.dma("pool", vtok[:, 0:4, :], I["cache_v"].rearrange("(b p) f -> p b f", p=128), None, w=[vtokb])
                    for kh in range(2):
                        bk = self.bank()
                        for blk in range(4):
                            P.op("pe", lambda: nc.tensor.transpose(self.ps[bk][:, blk * 128:(blk + 1) * 128],
                                                                   ck[:, blk, kh * 128:(kh + 1) * 128], ident),
                                 r=[ckb, self.cstb], w=[self.psb[bk]], inc=(blk == 3))
                        P.op("dve", lambda: nc.vector.tensor_copy(out=kT[:, kh, 0:512], in_=self.ps[bk][:, :]),
                             r=[self.psb[bk]], w=[kTb[kh]])
                else:
                    nkst = self.sb(e1, "nkst" + tag, [128, 4, 256], F32)
                    nkstb = bufs(4)
                    nvst = self.sb(e1, "nvst" + tag, [128, 4, 256], F32)
                    nvstb = bufs(4)

                def u_ap(kc, tt):
                    return uyh[:, kc * T + tt * 512:kc * T + (tt + 1) * 512]

                def wload(i):
                    srcw = w_in[:, i * 256:(i + 1) * 256].rearrange("(k p) n -> p k n", p=128)
                    P.dma("pool", win[i % 2][:, :, :], srcw, wsem[i % 2], w=[winb[i % 2]])

                NI = self.cfg.get("even_ni", 10)
                if NI > 0:
                    wload(0)
                for i in range(NI):
                    if i + 1 < NI:
                        wload(i + 1)
                    sl = win[i % 2]
                    if i == 9:
                        for tb in range(T // 128):
                            bk = self.bank()
                            for kc in range(KC):
                                P.op("pe", lambda: nc.tensor.matmul(
                                    self.ps[bk][:, 0:256], lhsT=uyh[:, kc * T + tb * 128:kc * T + (tb + 1) * 128],
                                    rhs=sl[:, kc, :], start=(kc == 0), stop=(kc == KC - 1)),
                                    r=[winb[i % 2], ub[kc, tb // 4]], w=[self.psb[bk]], inc=(kc == KC - 1))
                            if r == 0:
                                kcx = tb
                            else:
                                kcx = 4 + tb
                            if self.cfg.get("v_dbg", 3) >= 2:
                                P.op("dve", lambda: nc.vector.tensor_copy(out=vtok[:, kcx, :], in_=self.ps[bk][:, 0:256]),
                                     r=[self.psb[bk]], w=[vtokb])
                            if r == 0 and self.cfg.get("v_dbg", 3) >= 3:
                                P.op("act", lambda: nc.scalar.copy(out=nvst[:, tb, :], in_=self.ps[bk][:, 0:256]),
                                     r=[self.psb[bk]], w=[nvstb[tb]])
                                P.dma("sp", O["new_v"][tb * 128:(tb + 1) * 128, :], nvst[:, tb, :], None, r=[nvstb[tb]])
                        continue
                    for sub in range(2):
                        cc = i * 2 + sub
                        for tt in range(NT):
                            bk = self.bank()
                            for kc in range(KC):
                                P.op("pe", lambda: nc.tensor.matmul(
                                    self.ps[bk][:, :], lhsT=sl[:, kc, sub * 128:(sub + 1) * 128], rhs=u_ap(kc, tt),
                                    start=(kc == 0), stop=(kc == KC - 1)),
                                    r=[winb[i % 2], ub[kc, tt]], w=[self.psb[bk]], inc=(kc == KC - 1))
                            if cc < 8:
                                if r == 0:
                                    P.op("act", lambda: nc.scalar.copy(
                                        out=xpp[:, :, 8:8 + L], in_=self.ps[bk][:, :].rearrange("p (s t) -> p s t", s=2)),
                                        r=[self.psb[bk]], w=[xppb])
                                else:
                                    P.op("act", lambda: nc.scalar.copy(
                                        out=xpp[:, 0, 8 + tt * 512:8 + (tt + 1) * 512], in_=self.ps[bk][:, :]),
                                        r=[self.psb[bk]], w=[xppb])
                            else:
                                isq = cc < 16
                                hd = cc - 8 if isq else cc - 16
                                t1 = self.newtmp()
                                P.op("act", lambda: nc.scalar.activation(out=self.tmp[t1][:, :], in_=self.ps[bk][:, :],
                                                                         func=AF.Square), r=[self.psb[bk]], w=[self.tmpb[t1]])
                                sbk = self.bank()
                                P.op("pe", lambda: nc.tensor.matmul(self.ps[sbk][:, :], lhsT=self.ones_f[:, :],
                                                                    rhs=self.tmp[t1][:, :], start=True, stop=True),
                                     r=[self.tmpb[t1], self.onesb], w=[self.psb[sbk]])
                                t2 = self.newtmp()
                                P.op("act", lambda: nc.scalar.activation(out=self.tmp[t2][:, :], in_=self.ps[sbk][:, :],
                                                                         func=AF.Sqrt, scale=1.0 / 128, bias=self.epsc[:, 0:1]),
                                     r=[self.psb[sbk], self.smallb], w=[self.tmpb[t2]])
                                P.op("dve", lambda: nc.vector.reciprocal(out=self.tmp[t2][:, :], in_=self.tmp[t2][:, :]),
                                     r=[self.tmpb[t2]], w=[self.tmpb[t2]])
                                t3 = self.newtmp()
                                P.op("dve", lambda: nc.vector.tensor_tensor(out=self.tmp[t3][:, :], in0=self.ps[bk][:, :],
                                                                            in1=self.tmp[t2][:, :], op=ALU.mult),
                                     r=[self.psb[bk], self.tmpb[t2]], w=[self.tmpb[t3]])
                                gcol = gq[:, 0:1] if isq else gq[:, 1:2]
                                if r == 0:
                                    if isq:
                                        P.op("act", lambda: nc.scalar.activation(
                                            out=qT[:, hd, tt * 512:(tt + 1) * 512], in_=self.tmp[t3][:, :],
                                            func=AF.Copy, scale=gcol), r=[self.tmpb[t3], gqb], w=[qTb[hd, tt]])
                                    else:
                                        t4 = self.newtmp()
                                        P.op("act", lambda: nc.scalar.activation(
                                            out=self.tmp[t4][:, :], in_=self.tmp[t3][:, :], func=AF.Copy, scale=gcol),
                                            r=[self.tmpb[t3], gqb], w=[self.tmpb[t4]])
                                        P.op("dve", lambda: nc.vector.tensor_copy(out=kT[:, hd, 0:512], in_=self.tmp[t4][:, :]),
                                             r=[self.tmpb[t4]], w=[kTb[hd]])
                                        tbk = self.bank()
                                        for blk in range(4):
                                            P.op("pe", lambda: nc.tensor.transpose(
                                                self.ps[tbk][:, blk * 128:(blk + 1) * 128],
                                                self.tmp[t4][:, blk * 128:(blk + 1) * 128], ident),
                                                r=[self.tmpb[t4], self.cstb], w=[self.psb[tbk]], inc=(blk == 3))
                                        P.op("dve", lambda: nc.vector.tensor_copy(
                                            out=nkst[:, :, hd * 128:(hd + 1) * 128],
                                            in_=self.ps[tbk][:, :].rearrange("p (b d) -> p b d", b=4)),
                                            r=[self.psb[tbk]], w=[nkstb])
                                        if hd == 1:
                                            for blk in range(4):
                                                P.dma("sp", O["new_k"][blk * 128:(blk + 1) * 128, :], nkst[:, blk, :], None,
                                                      r=[nkstb[blk]])
                                else:
                                    t4 = self.newtmp()
                                    P.op("act", lambda: nc.scalar.activation(
                                        out=self.tmp[t4][:, :], in_=self.tmp[t3][:, :], func=AF.Copy, scale=gcol),
                                        r=[self.tmpb[t3], gqb], w=[self.tmpb[t4]])
                                    rbk = self.bank()
                                    P.op("pe", lambda: nc.tensor.matmul(self.ps[rbk][:, :],
                                                                        lhsT=self.cst[:, C_RT * 128:(C_RT + 1) * 128],
                                                                        rhs=self.tmp[t4][:, :], start=True, stop=True),
                                         r=[self.tmpb[t4], self.cstb], w=[self.psb[rbk]])
                                    t5 = self.newtmp()
                                    P.op("dve", lambda: nc.vector.tensor_tensor(
                                        out=self.tmp[t5][:, :], in0=self.tmp[t4][:, :], in1=rope[:, 0, tt * 512:(tt + 1) * 512],
                                        op=ALU.mult), r=[self.tmpb[t4], ropeb], w=[self.tmpb[t5]])
                                    t6 = self.newtmp()
                                    P.op("dve", lambda: nc.vector.tensor_tensor(
                                        out=self.tmp[t6][:, :], in0=self.ps[rbk][:, :], in1=rope[:, 1, tt * 512:(tt + 1) * 512],
                                        op=ALU.mult), r=[self.psb[rbk], ropeb], w=[self.tmpb[t6]])
                                    if isq:
                                        dst, dstb = qT[:, hd, tt * 512:(tt + 1) * 512], qTb[hd, tt]
                                    else:
                                        dst, dstb = kT[:, hd, PAST + tt * 512:PAST + (tt + 1) * 512], kTb[hd]
                                    P.op("dve", lambda: nc.vector.tensor_tensor(out=dst, in0=self.tmp[t5][:, :],
                                                                                in1=self.tmp[t6][:, :], op=ALU.add),
                                         r=[self.tmpb[t5], self.tmpb[t6]], w=[dstb])
                        if cc < 8:
                            gi = cc // 2
                            wv = (2, 4, 8, 16)[gi]
                            hw = wv // 2
                            W = L + 16
                            cur = xpp
                            step = 1
                            bufs_ = [sA, sB]
                            bi = 0
                            while step < wv:
                                nxt = bufs_[bi]
                                bi ^= 1
                                n_valid = W - 2 * step + 1 if step > 1 else W - 1
                                n_valid = W - (2 * step - 1)
                                P.op("dve", lambda: nc.vector.tensor_tensor(
                                    out=nxt[:, :, 0:n_valid], in0=cur[:, :, 0:n_valid], in1=cur[:, :, step:step + n_valid],
                                    op=ALU.add), r=[xppb, sb_], w=[sb_])
                                cur = nxt
                                step *= 2
                            off = 8 - hw
                            other = bufs_[bi]
                            P.op("dve", lambda: nc.vector.tensor_tensor(
                                out=other[:, :, 0:L], in0=cur[:, :, off:off + L],
                                in1=pcnt[:, gi, :].unsqueeze(1).broadcast_to([128, nseq, L]) if nseq > 1 else pcnt[:, gi:gi + 1, :],
                                op=ALU.mult), r=[sb_, pcntb], w=[sb_])
                            P.op("dve", lambda: nc.vector.tensor_tensor(
                                out=dbf[:, cc % 2, :].rearrange("p (s t) -> p s t", s=nseq), in0=other[:, :, 0:L],
                                in1=xpp[:, :, 8:8 + L], op=ALU.subtract), r=[sb_, xppb], w=[dbfb[cc % 2]])
                            if cc % 2 == 1:
                                for do in range(2):
                                    for tt in range(NT):
                                        bk = self.bank()
                                        for c in range(2):
                                            P.op("pe", lambda: nc.tensor.matmul(
                                                self.ps[bk][:, :], lhsT=pw[:, gi, c, do * 128:(do + 1) * 128],
                                                rhs=dbf[:, c, tt * 512:(tt + 1) * 512], start=(c == 0), stop=(c == 1)),
                                                r=[pwb, dbfb[c]], w=[self.psb[bk]], inc=(c == 1))
                                        oc = gi * 2 + do
                                        P.op("act", lambda: nc.scalar.activation(
                                            out=cat[:, oc, tt * 512:(tt + 1) * 512], in_=self.ps[bk][:, :], func=AF.Copy,
                                            scale=self.vecT[:, oc:oc + 1]), r=[self.psb[bk], self.smallb], w=[catb[oc, tt]])
                self.fence()
            if self.cfg.get("even_stop", 9) < 2:
                return
            with ExitStack() as e2:
                pT = [self.sb(e2, "pT%d%s" % (i, tag), [128, 512], BF16) for i in range(3)]
                pTb = bufs(3)
                pi = 0
                NQ = L if r == 0 else 512
                for sq in range(nseq):
                    for qt in range(L // NQ):
                        q0 = sq * L + qt * NQ
                        for hd in range(8):
                            kh = hd // 4
                            ob = self.bank_hold()
                            db = self.bank_hold()

                            def s_mm(kcx):
                                bk = self.bank()
                                P.op("pe", lambda: nc.tensor.matmul(
                                    self.ps[bk][:, 0:NQ], lhsT=kT[:, kh, sq * NK + kcx * 128:sq * NK + (kcx + 1) * 128],
                                    rhs=qT[:, hd, q0:q0 + NQ], start=True, stop=True),
                                    r=[kTb[kh], qTb[hd, q0 // 512]], w=[self.psb[bk]])
                                return bk
                            sbk_next = s_mm(0)
                            for kcx in range(NKC):
                                sbk = sbk_next
                                k_ = pi % 3
                                pi += 1
                                P.op("act", lambda: nc.scalar.activation(out=pT[k_][:, 0:NQ], in_=self.ps[sbk][:, 0:NQ],
                                                                         func=AF.Exp), r=[self.psb[sbk]], w=[pTb[k_]])
                                if kcx + 1 < NKC:
                                    sbk_next = s_mm(kcx + 1)
                                P.op("pe", lambda: nc.tensor.matmul(
                                    self.ps[ob][:, 0:NQ], lhsT=vtok[:, sq * NKC + kcx, kh * 128:(kh + 1) * 128],
                                    rhs=pT[k_][:, 0:NQ], start=(kcx == 0), stop=(kcx == NKC - 1)),
                                    r=[vtokb, pTb[k_]], w=[self.psb[ob]], inc=False)
                                P.op("pe", lambda: nc.tensor.matmul(
                                    self.ps[db][:, 0:NQ], lhsT=self.ones_b[:, :], rhs=pT[k_][:, 0:NQ],
                                    start=(kcx == 0), stop=(kcx == NKC - 1)),
                                    r=[self.onesb, pTb[k_]], w=[self.psb[db], self.psb[ob]])
                            t1 = self.newtmp()
                            P.op("dve", lambda: nc.vector.reciprocal(out=self.tmp[t1][:, 0:NQ], in_=self.ps[db][:, 0:NQ]),
                                 r=[self.psb[db]], w=[self.tmpb[t1]])
                            P.op("dve", lambda: nc.vector.tensor_tensor(
                                out=cat[:, 8 + hd, q0:q0 + NQ], in0=self.ps[ob][:, 0:NQ], in1=self.tmp[t1][:, 0:NQ],
                                op=ALU.mult), r=[self.psb[ob], self.tmpb[t1]], w=[catb[8 + hd, q0 // 512]])
                            self.bank_release(ob)
                            self.bank_release(db)
                self.fence()
            if self.cfg.get("even_stop", 9) < 3:
                return
            self.mix_out_residual(I["mix_w_out"][0], KC, lambda tt: (lambda kc: cat[:, kc, tt * 512:(tt + 1) * 512]),
                                  lambda tt: (lambda kc: catb[kc, tt]), T, l, r, uy, uyb, win, winb, wsem, tag)
            self.fence()

    def odd_mixer(self, T, r, uy, uyb, win, winb, wsem):
        nc, P, I, O = self.nc, self.P, self.I, self.O
        l, s = 1, 1
        nseq = 2 if r == 0 else 1
        L = T // nseq
        NT = T // 512
        NTB = T // 128
        NCH = L // 128
        uyh = uy[:, :].bitcast(BF16)
        w_in = I["ssm_w_in"][0]
        ident = self.cst[:, C_ID * 128:(C_ID + 1) * 128]
        cU = self.cst[:, C_U * 128:(C_U + 1) * 128]
        cLi = self.cst[:, C_LI * 128:(C_LI + 1) * 128]
        cLs = self.cst[:, C_LS * 128:(C_LS + 1) * 128]
        cUs = self.cst[:, C_US * 128:(C_US + 1) * 128]
        tag = "o%d" % r
        ygscr = self.ygscr
        with ExitStack() as es:
            ub = bufs(KC, NT)
            with ExitStack() as e1:
                h = self.sb(e1, "h_" + tag, [128, KC, T], F32)
                hb = bufs(KC, NT)
                self.h_load(h, hb, T, None)
                for tt in range(NT):
                    self.modulate(h, hb, tt, l, s, r, lambda c: uyh[:, c * T + tt * 512:c * T + (tt + 1) * 512],
                                  lambda c: ub[c, tt])
                self.fence()
            dt = self.sb(es, "dt" + tag, [128, NTB, 128], F32)
            dtA = self.sb(es, "dtA" + tag, [128, NTB, 128], F32)
            ea = self.sb(es, "ea" + tag, [128, NTB, 128], F32)
            dtd = self.sb(es, "dtd" + tag, [128, NTB, 128], F32)
            etot = self.sb(es, "etot" + tag, [128, NTB, 128], F32)
            dtb_ = Buf()
            ssq = self.sb(es, "ssq" + tag, [128, NTB, 8], F32)
            ssqb = Buf()
            rrow = self.sb(es, "rrow" + tag, [128, T], F32)
            rrowb = Buf()

            def u_blk(kc, tb):
                return uyh[:, kc * T + tb * 128:kc * T + (tb + 1) * 128]

            srcw = w_in[:, 10240:10368].rearrange("(k p) n -> p k n", p=128)
            P.dma("pool", win[0][:, :, 0:128], srcw, wsem[0], w=[winb[0]])
            for tb in range(NTB):
                bk = self.bank()
                for kc in range(KC):
                    P.op("pe", lambda: nc.tensor.matmul(self.ps[bk][:, 0:128], lhsT=u_blk(kc, tb), rhs=win[0][:, kc, 0:128],
                                                        start=(kc == 0), stop=(kc == KC - 1)),
                         r=[winb[0], ub[kc, tb // 4]], w=[self.psb[bk]], inc=(kc == KC - 1))
                t1 = self.newtmp()
                xb = self.tmp[t1][:, 0:128]
                P.op("dve", lambda: nc.vector.tensor_tensor(out=xb, in0=self.ps[bk][:, 0:128], in1=self.dtb[:, :], op=ALU.add),
                     r=[self.psb[bk], self.smallb], w=[self.tmpb[t1]])
                t2 = self.newtmp()
                ab = self.tmp[t2][:, 0:128]
                P.op("act", lambda: nc.scalar.activation(out=ab, in_=xb, func=AF.Abs),
                     r=[self.tmpb[t1]], w=[self.tmpb[t2]])
                P.op("act", lambda: nc.scalar.activation(out=ab, in_=ab, func=AF.Exp, scale=-1.0),
                     r=[self.tmpb[t2]], w=[self.tmpb[t2]])
                P.op("act", lambda: nc.scalar.activation(out=ab, in_=ab, func=AF.Ln, bias=self.onec[:, 0:1]),
                     r=[self.tmpb[t2], self.smallb], w=[self.tmpb[t2]])
                P.op("dve", lambda: nc.vector.scalar_tensor_tensor(out=dt[:, tb, :], in0=xb, scalar=0.0, in1=ab,
                                                                   op0=ALU.max, op1=ALU.add),
                     r=[self.tmpb[t1], self.tmpb[t2]], w=[dtb_])
                P.op("dve", lambda: nc.vector.tensor_tensor(out=dtA[:, tb, :], in0=dt[:, tb, :], in1=self.Aneg[:, :], op=ALU.mult),
                     r=[dtb_, self.smallb], w=[dtb_])
                bk2 = self.bank()
                P.op("pe", lambda: nc.tensor.matmul(self.ps[bk2][:, 0:64], lhsT=cU, rhs=dtA[:, tb, 0:64], start=True, stop=True),
                     r=[dtb_, self.cstb], w=[self.psb[bk2]], inc=False)
                P.op("pe", lambda: nc.tensor.matmul(self.ps[bk2][:, 64:128], lhsT=cLi, rhs=dtA[:, tb, 64:128], start=True, stop=True),
                     r=[dtb_, self.cstb], w=[self.psb[bk2]], inc=False)
                P.op("pe", lambda: nc.tensor.matmul(self.ps[bk2][:, 128:256], lhsT=self.ones_f[:, :], rhs=dtA[:, tb, :],
                                                    start=True, stop=True),
                     r=[dtb_, self.onesb], w=[self.psb[bk2]])
                P.op("act", lambda: nc.scalar.activation(out=ea[:, tb, :], in_=self.ps[bk2][:, 0:128], func=AF.Exp),
                     r=[self.psb[bk2]], w=[dtb_])
                P.op("act", lambda: nc.scalar.activation(out=etot[:, tb, :], in_=self.ps[bk2][:, 128:256], func=AF.Exp),
                     r=[self.psb[bk2]], w=[dtb_])
                t3 = self.newtmp()
                ac = self.tmp[t3][:, 0:128]
                P.op("act", lambda: nc.scalar.copy(out=ac, in_=self.ps[bk2][:, 0:128]), r=[self.psb[bk2]], w=[self.tmpb[t3]])
                P.op("dve", lambda: nc.vector.tensor_tensor(out=ac, in0=self.ps[bk2][:, 128:256], in1=ac, op=ALU.subtract),
                     r=[self.psb[bk2], self.tmpb[t3]], w=[self.tmpb[t3]])
                P.op("act", lambda: nc.scalar.activation(out=ac, in_=ac, func=AF.Exp), r=[self.tmpb[t3]], w=[self.tmpb[t3]])
                P.op("dve", lambda: nc.vector.tensor_tensor(out=dtd[:, tb, :], in0=ac, in1=dt[:, tb, :], op=ALU.mult),
                     r=[self.tmpb[t3], dtb_], w=[dtb_])
            with ExitStack() as e2:
                sb = self.sb
                x_tok = sb(e2, "xtok" + tag, [128, NTB, 512], F32)
                xtokb = Buf()
                zs = sb(e2, "zs" + tag, [128, NTB, 512], BF16)
                zsb = Buf()
                yacc = sb(e2, "yacc" + tag, [128, NTB, 512], F32)
                yaccb = bufs(NTB)
                BT = sb(e2, "BT" + tag, [128, T], BF16)
                CT = sb(e2, "CT" + tag, [128, T], BF16)
                bcb = bufs(2)
                Btok = sb(e2, "Btok" + tag, [128, NTB, 128], BF16)
                Btokb = Buf()
                xc = sb(e2, "xc" + tag, [128, nseq, L + 2], F32)
                xcb = Buf()
                cva = sb(e2, "cva" + tag, [128, nseq, L], F32)
                cvb = sb(e2, "cvb" + tag, [128, nseq, L], F32)
                cvab = Buf()
                cvbb = Buf()
                S = [sb(e2, "S%d%s" % (d, tag), [128, 512], F32) for d in range(2)]
                Sb16 = [sb(e2, "Sb%d%s" % (d, tag), [128, 512], BF16) for d in range(2)]
                Sbuf = bufs(2)
                rb = [sb(e2, "rb%d%s" % (i, tag), [128, 8, 128], F32) for i in range(2)]
                rbb = bufs(2)
                dec = [sb(e2, "dec%d%s" % (i, tag), [128, 4, 128], F32) for i in range(4)]
                decb = bufs(4)
                wm = [sb(e2, "wm%d%s" % (i, tag), [128, 8, 128], BF16) for i in range(2)]
                wmb = bufs(2)
                cbm = [sb(e2, "cbm%d%s" % (i, tag), [128, 128], F32) for i in range(2)]
                cbmb = bufs(2)
                xdt = [sb(e2, "xdt%d%s" % (i, tag), [128, 512], BF16) for i in range(2)]
                xdtb = bufs(2)
                xdd = [sb(e2, "xdd%d%s" % (i, tag), [128, 512], BF16) for i in range(2)]
                xddb = bufs(2)
                ygst = sb(e2, "ygst" + tag, [128, 4, T], BF16)
                ygstb = Buf()
                stst = sb(e2, "stst" + tag, [128, 4, 128], F32)
                ststb = Buf()
                P.op("dve", lambda: nc.vector.memset(xc[:, :, :], 0.0), w=[xcb])
                it = 0
                nload = [0]

                def wload(col0, ncol, dcol=0, k=None):
                    if k is None:
                        k = nload[0] % 2
                        nload[0] += 1
                    srcw_ = w_in[:, col0:col0 + ncol].rearrange("(k p) n -> p k n", p=128)
                    P.dma("pool", win[k][:, :, dcol:dcol + ncol], srcw_, wsem[k], w=[winb[k]])
                    return k

                def fm_chunk(k, sub, cidx, kind, dst, dstb):
                    for tt in range(NT):
                        bk = self.bank()
                        for kc in range(KC):
                            P.op("pe", lambda: nc.tensor.matmul(
                                self.ps[bk][:, :], lhsT=win[k][:, kc, sub * 128:(sub + 1) * 128],
                                rhs=uyh[:, kc * T + tt * 512:kc * T + (tt + 1) * 512], start=(kc == 0), stop=(kc == KC - 1)),
                                r=[winb[k], ub[kc, tt]], w=[self.psb[bk]], inc=(kc == KC - 1))
                        if r == 0:
                            P.op("act", lambda: nc.scalar.copy(out=xc[:, :, 1:1 + L],
                                                               in_=self.ps[bk][:, :].rearrange("p (s t) -> p s t", s=2)),
                                 r=[self.psb[bk]], w=[xcb])
                        else:
                            P.op("act", lambda: nc.scalar.copy(out=xc[:, 0, 1 + tt * 512:1 + (tt + 1) * 512], in_=self.ps[bk][:, :]),
                                 r=[self.psb[bk]], w=[xcb])
                    w0 = self.convw[:, cidx, 0:1]
                    w1 = self.convw[:, cidx, 1:2]
                    w2 = self.convw[:, cidx, 2:3]
                    bcol = self.vecT[:, 10 + cidx:11 + cidx]
                    P.op("dve", lambda: nc.vector.tensor_scalar(out=cva[:, :, :], in0=xc[:, :, 0:L], scalar1=w0, scalar2=None,
                                                                op0=ALU.mult), r=[xcb, self.smallb], w=[cvab])
                    P.op("dve", lambda: nc.vector.scalar_tensor_tensor(out=cvb[:, :, :], in0=xc[:, :, 1:L + 1], scalar=w1,
                                                                       in1=cva[:, :, :], op0=ALU.mult, op1=ALU.add),
                         r=[xcb, cvab, self.smallb], w=[cvbb])
                    P.op("dve", lambda: nc.vector.scalar_tensor_tensor(out=cva[:, :, :], in0=xc[:, :, 2:L + 2], scalar=w2,
                                                                       in1=cvb[:, :, :], op0=ALU.mult, op1=ALU.add),
                         r=[xcb, cvbb, self.smallb], w=[cvab])
                    cflat = cva[:, :, :].rearrange("p s t -> p (s t)")
                    vflat = cvb[:, :, :].rearrange("p s t -> p (s t)")
                    P.op("act", lambda: nc.scalar.activation(out=vflat, in_=cflat, func=AF.Silu, bias=bcol),
                         r=[cvab, self.smallb], w=[cvbb])
                    if kind in ("B", "C"):
                        P.op("act", lambda: nc.scalar.copy(out=dst[:, :], in_=vflat), r=[cvbb], w=[dstb])
                    if kind in ("x", "B"):
                        for t4 in range(NTB // 4):
                            bk = self.bank()
                            for q in range(4):
                                tb = t4 * 4 + q
                                P.op("pe", lambda: nc.tensor.transpose(self.ps[bk][:, q * 128:(q + 1) * 128],
                                                                       vflat[:, tb * 128:(tb + 1) * 128], ident),
                                     r=[cvbb, self.cstb], w=[self.psb[bk]], inc=(q == 3))
                            pv = self.ps[bk][:, :].rearrange("p (q f) -> p q f", q=4)
                            if kind == "x":
                                P.op("dve", lambda: nc.vector.tensor_copy(out=x_tok[:, t4 * 4:(t4 + 1) * 4, dst * 128:(dst + 1) * 128],
                                                                          in_=pv), r=[self.psb[bk]], w=[xtokb])
                            else:
                                P.op("dve", lambda: nc.vector.tensor_copy(out=Btok[:, t4 * 4:(t4 + 1) * 4, :], in_=pv),
                                     r=[self.psb[bk]], w=[Btokb])

                for g in range(NG):
                    for half in range(2):
                        k = wload(g * 512 + half * 256, 256)
                        for tb in range(NTB):
                            bk = self.bank()
                            for kc in range(KC):
                                P.op("pe", lambda: nc.tensor.matmul(self.ps[bk][:, 0:256], lhsT=u_blk(kc, tb), rhs=win[k][:, kc, :],
                                                                    start=(kc == 0), stop=(kc == KC - 1)),
                                     r=[winb[k], ub[kc, tb // 4]], w=[self.psb[bk]], inc=(kc == KC - 1))
                            P.op("act", lambda: nc.scalar.activation(out=zs[:, tb, half * 256:(half + 1) * 256],
                                                                     in_=self.ps[bk][:, 0:256], func=AF.Silu),
                                 r=[self.psb[bk]], w=[zsb])
                    for half in range(2):
                        k = wload(4096 + g * 512 + half * 256, 256)
                        for sub in range(2):
                            ch = half * 2 + sub
                            fm_chunk(k, sub, g * 4 + ch, "x", ch, None)
                    k = wload(8192 + g * 128, 128, 0)
                    wload(9216 + g * 128, 128, 128, k=k)
                    fm_chunk(k, 0, 32 + g, "B", BT, bcb[0])
                    fm_chunk(k, 1, 40 + g, "C", CT, bcb[1])
                    for tb in range(NTB):
                        P.op("dve", lambda: nc.vector.tensor_tensor(
                            out=yacc[:, tb, :].rearrange("p (h q) -> p h q", h=8),
                            in0=x_tok[:, tb, :].rearrange("p (h q) -> p h q", h=8),
                            in1=self.Dsum[:, g * 8:(g + 1) * 8].unsqueeze(2).broadcast_to([128, 8, 64]), op=ALU.mult),
                            r=[xtokb, self.smallb], w=[yaccb[tb]])
                    for sq in range(nseq):
                        for d in range(2):
                            if r == 1:
                                P.dma("sp", stst[:, :, :], I["state"][d, g * 512:(g + 1) * 512, :].rearrange("(q p) n -> p q n", p=128),
                                      None, w=[ststb])
                                bk = self.bank()
                                for q in range(4):
                                    P.op("pe", lambda: nc.tensor.transpose(self.ps[bk][:, q * 128:(q + 1) * 128], stst[:, q, :], ident),
                                         r=[ststb, self.cstb], w=[self.psb[bk]], inc=(q == 3))
                                P.op("dve", lambda: nc.vector.tensor_copy(out=S[d][:, :], in_=self.ps[bk][:, :]),
                                     r=[self.psb[bk]], w=[Sbuf[d]])
                                P.op("act", lambda: nc.scalar.copy(out=Sb16[d][:, :], in_=self.ps[bk][:, :]),
                                     r=[self.psb[bk]], w=[Sbuf[d]])
                        for ci in range(NCH):
                            for d in range(2):
                                Mk = cU if d == 0 else cLi
                                Ml = cLs if d == 0 else cUs
                                hcol = d * 64 + g * 8
                                c = ci if d == 0 else NCH - 1 - ci
                                tb = sq * NCH + c
                                first = (ci == 0 and r == 0)
                                tok = slice(tb * 128, (tb + 1) * 128)
                                i2 = it % 2
                                it += 1
                                bk = self.bank()
                                P.op("pe", lambda: nc.tensor.matmul(self.ps[bk][:, 0:128], lhsT=BT[:, tok], rhs=CT[:, tok],
                                                                    start=True, stop=True), r=[bcb], w=[self.psb[bk]])
                                P.op("dve", lambda: nc.vector.tensor_tensor(out=cbm[i2][:, :], in0=self.ps[bk][:, 0:128], in1=Mk,
                                                                            op=ALU.mult), r=[self.psb[bk], self.cstb], w=[cbmb[i2]])
                                P.op("dve", lambda: nc.vector.tensor_tensor(
                                    out=rb[i2][:, :, :], in0=Mk.unsqueeze(1).broadcast_to([128, 8, 128]),
                                    in1=dtA[:, tb, hcol:hcol + 8].unsqueeze(2).broadcast_to([128, 8, 128]), op=ALU.mult),
                                    r=[self.cstb, dtb_], w=[rbb[i2]])
                                for hf in range(2):
                                    bk = self.bank()
                                    P.op("pe", lambda: nc.tensor.matmul(
                                        self.ps[bk][:, :], lhsT=Ml, rhs=rb[i2][:, hf * 4:(hf + 1) * 4, :].rearrange("p h i -> p (h i)"),
                                        start=True, stop=True), r=[rbb[i2], self.cstb], w=[self.psb[bk]])
                                    P.op("act", lambda: nc.scalar.activation(out=dec[i2 * 2 + hf][:, :, :].rearrange("p h i -> p (h i)"),
                                                                             in_=self.ps[bk][:, :], func=AF.Exp),
                                         r=[self.psb[bk]], w=[decb[i2 * 2 + hf]])
                                    P.op("dve", lambda: nc.vector.tensor_tensor(
                                        out=wm[i2][:, hf * 4:(hf + 1) * 4, :], in0=dec[i2 * 2 + hf][:, :, :],
                                        in1=cbm[i2][:, :].unsqueeze(1).broadcast_to([128, 4, 128]), op=ALU.mult),
                                        r=[decb[i2 * 2 + hf], cbmb[i2]], w=[wmb[i2]])
                                xv = x_tok[:, tb, :].rearrange("p (h q) -> p h q", h=8)
                                P.op("dve", lambda: nc.vector.tensor_tensor(
                                    out=xdt[i2][:, :].rearrange("p (h q) -> p h q", h=8), in0=xv,
                                    in1=dt[:, tb, hcol:hcol + 8].unsqueeze(2).broadcast_to([128, 8, 64]), op=ALU.mult),
                                    r=[xtokb, dtb_], w=[xdtb[i2]])
                                P.op("dve", lambda: nc.vector.tensor_tensor(
                                    out=xdd[i2][:, :].rearrange("p (h q) -> p h q", h=8), in0=xv,
                                    in1=dtd[:, tb, hcol:hcol + 8].unsqueeze(2).broadcast_to([128, 8, 64]), op=ALU.mult),
                                    r=[xtokb, dtb_], w=[xddb[i2]])
                                yi = self.bank()
                                for hh in range(8):
                                    P.op("pe", lambda: nc.tensor.matmul(self.ps[yi][:, hh * 64:(hh + 1) * 64], lhsT=wm[i2][:, hh, :],
                                                                        rhs=xdt[i2][:, hh * 64:(hh + 1) * 64], start=True, stop=True),
                                         r=[wmb[i2], xdtb[i2]], w=[self.psb[yi]], inc=(hh == 7))
                                if not first:
                                    ysb = self.bank()
                                    P.op("pe", lambda: nc.tensor.matmul(self.ps[ysb][:, :], lhsT=CT[:, tok], rhs=Sb16[d][:, :],
                                                                        start=True, stop=True), r=[bcb[1], Sbuf[d]], w=[self.psb[ysb]])
                                    t1 = self.newtmp()
                                    P.op("dve", lambda: nc.vector.tensor_tensor(
                                        out=self.tmp[t1][:, :].rearrange("p (h q) -> p h q", h=8),
                                        in0=self.ps[ysb][:, :].rearrange("p (h q) -> p h q", h=8),
                                        in1=ea[:, tb, hcol:hcol + 8].unsqueeze(2).broadcast_to([128, 8, 64]), op=ALU.mult),
                                        r=[self.psb[ysb], dtb_], w=[self.tmpb[t1]])
                                    P.op("dve", lambda: nc.vector.tensor_tensor(out=self.tmp[t1][:, :], in0=self.tmp[t1][:, :],
                                                                                in1=self.ps[yi][:, :], op=ALU.add),
                                         r=[self.tmpb[t1], self.psb[yi]], w=[self.tmpb[t1]])
                                    P.op("dve", lambda: nc.vector.tensor_tensor(out=yacc[:, tb, :], in0=yacc[:, tb, :],
                                                                                in1=self.tmp[t1][:, :], op=ALU.add),
                                         r=[self.tmpb[t1], yaccb[tb]], w=[yaccb[tb]])
                                else:
                                    P.op("dve", lambda: nc.vector.tensor_tensor(out=yacc[:, tb, :], in0=yacc[:, tb, :],
                                                                                in1=self.ps[yi][:, :], op=ALU.add),
                                         r=[self.psb[yi], yaccb[tb]], w=[yaccb[tb]])
                                su = self.bank()
                                P.op("pe", lambda: nc.tensor.matmul(self.ps[su][:, :], lhsT=Btok[:, tb, :], rhs=xdd[i2][:, :],
                                                                    start=True, stop=True), r=[Btokb, xddb[i2]], w=[self.psb[su]])
                                if first:
                                    P.op("dve", lambda: nc.vector.tensor_copy(out=S[d][:, :], in_=self.ps[su][:, :]),
                                         r=[self.psb[su]], w=[Sbuf[d]])
                                else:
                                    P.op("dve", lambda: nc.vector.tensor_tensor(
                                        out=S[d][:, :].rearrange("p (h q) -> p h q", h=8),
                                        in0=S[d][:, :].rearrange("p (h q) -> p h q", h=8),
                                        in1=etot[:, tb, hcol:hcol + 8].unsqueeze(2).broadcast_to([128, 8, 64]), op=ALU.mult),
                                        r=[Sbuf[d], dtb_], w=[Sbuf[d]])
                                    P.op("dve", lambda: nc.vector.tensor_tensor(out=S[d][:, :], in0=S[d][:, :], in1=self.ps[su][:, :],
                                                                                op=ALU.add), r=[Sbuf[d], self.psb[su]], w=[Sbuf[d]])
                                P.op("act", lambda: nc.scalar.copy(out=Sb16[d][:, :], in_=S[d][:, :]), r=[Sbuf[d]], w=[Sbuf[d]])
                        for d in range(2):
                            if r == 0:
                                bk = self.bank()
                                for q in range(4):
                                    P.op("pe", lambda: nc.tensor.transpose(self.ps[bk][:, q * 128:(q + 1) * 128],
                                                                           S[d][:, q * 128:(q + 1) * 128], ident),
                                         r=[Sbuf[d], self.cstb], w=[self.psb[bk]], inc=(q == 3))
                                P.op("dve", lambda: nc.vector.tensor_copy(out=stst[:, :, :],
                                                                          in_=self.ps[bk][:, :].rearrange("p (q n) -> p q n", q=4)),
                                     r=[self.psb[bk]], w=[ststb])
                                P.dma("sp", O["new_ssm"][sq, d, g * 512:(g + 1) * 512, :].rearrange("(q p) n -> p q n", p=128),
                                      stst[:, :, :], None, r=[ststb])
                    for tb in range(NTB):
                        P.op("dve", lambda: nc.vector.tensor_tensor(out=yacc[:, tb, :], in0=yacc[:, tb, :], in1=zs[:, tb, :],
                                                                    op=ALU.mult), r=[yaccb[tb], zsb], w=[yaccb[tb]])
                        t1 = self.newtmp()
                        P.op("act", lambda: nc.scalar.activation(out=self.tmp[t1][:, :], in_=yacc[:, tb, :], func=AF.Square,
                                                                 accum_out=ssq[:, tb, g:g + 1]),
                             r=[yaccb[tb]], w=[self.tmpb[t1], ssqb])
                    for ch in range(4):
                        for t4 in range(NTB // 4):
                            bk = self.bank()
                            for q in range(4):
                                tb = t4 * 4 + q
                                P.op("pe", lambda: nc.tensor.transpose(self.ps[bk][:, q * 128:(q + 1) * 128],
                                                                       yacc[:, tb, ch * 128:(ch + 1) * 128], ident),
                                     r=[yaccb[tb], self.cstb], w=[self.psb[bk]], inc=(q == 3))
                            P.op("act", lambda: nc.scalar.activation(out=ygst[:, ch, t4 * 512:(t4 + 1) * 512], in_=self.ps[bk][:, :],
                                                                     func=AF.Copy, scale=self.vecT[:, 58 + g * 4 + ch:59 + g * 4 + ch]),
                                 r=[self.psb[bk], self.smallb], w=[ygstb])
                    P.dma("sp", ygscr[:, g * 4:(g + 1) * 4, 0:T], ygst[:, :, :], None, r=[ygstb])
                self.fence()
            t1 = self.newtmp()
            rt = self.tmp[t1][:, 0:NTB]
            P.op("dve", lambda: nc.vector.tensor_reduce(out=rt, in_=ssq[:, :, :], axis=AX.X, op=ALU.add),
                 r=[ssqb], w=[self.tmpb[t1]])
            P.op("act", lambda: nc.scalar.activation(out=rt, in_=rt, func=AF.Sqrt, scale=1.0 / D_INNER, bias=self.epsc[:, 0:1]),
                 r=[self.tmpb[t1], self.smallb], w=[self.tmpb[t1]])
            P.op("dve", lambda: nc.vector.reciprocal(out=rt, in_=rt), r=[self.tmpb[t1]], w=[self.tmpb[t1]])
            for t4 in range(NTB // 4):
                t2 = self.newtmp()
                for q in range(4):
                    tb = t4 * 4 + q
                    P.op("dve", lambda: nc.vector.tensor_scalar(out=self.tmp[t2][:, q * 128:(q + 1) * 128], in0=ident,
                                                                scalar1=rt[:, tb:tb + 1], scalar2=None, op0=ALU.mult),
                         r=[self.tmpb[t1], self.cstb], w=[self.tmpb[t2]])
                bk = self.bank()
                P.op("pe", lambda: nc.tensor.matmul(self.ps[bk][:, :], lhsT=self.ones_f[:, :], rhs=self.tmp[t2][:, :],
                                                    start=True, stop=True), r=[self.tmpb[t2], self.onesb], w=[self.psb[bk]])
                P.op("act", lambda: nc.scalar.copy(out=rrow[:, t4 * 512:(t4 + 1) * 512], in_=self.ps[bk][:, :]),
                     r=[self.psb[bk]], w=[rrowb])
            self.fence()
            with ExitStack() as e3:
                ygt = self.sb(e3, "ygt" + tag, [128, 32, 512], BF16)
                ygtb_holder = [None]

                def rhs_of_tile(tt):
                    b_ = Buf()
                    ygtb_holder[0] = b_
                    P.dma("sp", ygt[:, :, :], ygscr[:, :, tt * 512:(tt + 1) * 512], None, w=[b_])
                    return lambda kc: ygt[:, kc, :]

                def rhsb_of_tile(tt):
                    return lambda kc: ygtb_holder[0]

                self.mix_out_residual(I["ssm_w_out"][0], 32, rhs_of_tile, rhsb_of_tile, T, l, r, uy, uyb, win, winb, wsem, tag,
                                      post_scale=lambda tt: (rrow[:, tt * 512:(tt + 1) * 512], rrowb))
                self.fence()


_CONSTS = None


def make_in_maps(inp):
    global _CONSTS
    if _CONSTS is None:
        _CONSTS = _host_consts()
    cst, rope, pc_ctx, pc_smp = _CONSTS
    f = lambda a: np.ascontiguousarray(np.asarray(a, dtype=np.float32))
    shared = {k: f(inp[k]) for k in ("ada_w", "ada_b", "norm_g", "ffn_w_in", "ffn_w_out", "mix_w_in", "pool_w",
                                     "pool_scale", "qk_norm_g", "mix_w_out", "ssm_w_in", "ssm_conv_w", "ssm_conv_b",
                                     "ssm_dt_bias", "ssm_A_log", "ssm_D", "ssm_norm_g", "ssm_w_out")}
    shared.update(cst=cst, rope=rope, pcnt_ctx=pc_ctx, pcnt_smp=pc_smp)
    xp = f(inp["x_prompt"])
    xs = f(inp["x_sample"])
    ck = f(inp["cache_k"])
    cv = f(inp["cache_v"])
    st = f(inp["state_ssm"])
    c = f(inp["c"])
    cc = f(inp["c_ctx"])
    maps = []
    for i in range(N_CORES):
        m = dict(shared)
        m["x_ctx"] = xp[2 * i:2 * i + 2].reshape(2 * L_CTX, D)
        m["x_smp"] = xs[i]
        m["cache_k"] = ck[i, 0].reshape(PAST, 256)
        m["cache_v"] = cv[i, 0].reshape(PAST, 256)
        m["state"] = st[i, 0].reshape(2, NH * HP, NS)
        m["cond"] = np.stack([cc, c[i]], axis=0)
        maps.append(m)
    return maps


def run(inp, cfg=None, n_cores=N_CORES):
    cfg = cfg or {}
    b = Builder(cfg)
    nc = b.build()
    maps = make_in_maps(inp)[:n_cores]
    res = run_bass_kernel_spmd(nc, maps, core_ids=list(range(n_cores)))
    return res.results


def kernel(**inputs):
    rs = run(inputs)
    y_prompt = np.stack([r["y_ctx"] for r in rs]).reshape(16, L_CTX, D)
    y_sample = np.stack([r["y_smp"] for r in rs]).reshape(8, L_SMP, D)
    new_k = np.stack([r["new_k"] for r in rs]).reshape(16, 1, L_CTX, 2, 128)
    new_v = np.stack([r["new_v"] for r in rs]).reshape(16, 1, L_CTX, 2, 128)
    new_ssm = np.stack([r["new_ssm"] for r in rs]).reshape(16, 1, 2, NH, HP, NS)
    return (y_prompt.astype(np.float32), y_sample.astype(np.float32), new_k.astype(np.float32),
            new_v.astype(np.float32), new_ssm.astype(np.float32))
```

```python
import math
from contextlib import ExitStack

import numpy as np
import concourse.bass as bass
import concourse.mybir as mybir
from concourse.bass_utils import run_bass_kernel_spmd

F32 = mybir.dt.float32
BF16 = mybir.dt.bfloat16
AF = mybir.ActivationFunctionType
ALU = mybir.AluOpType
AX = mybir.AxisListType

N_CORES = 8
D = 2048
KC = D // 128
DFF = 5632
JH = DFF // 128
NMOD = 9
EPS = 1e-6
MIX_IN = 2560
D_INNER = 4096
CONV_DIM = 6144
SSM_IN = 10368
NH = 64
HP = 64
NG = 8
NS = 128
PAST = 512
L_CTX = 256
L_SMP = 1024


class TL:
    __slots__ = ("sem", "n", "name")

    def __init__(self, sem, name):
        self.sem = sem
        self.n = 0
        self.name = name


class Buf:
    __slots__ = ("w", "r", "excl")

    def __init__(self):
        self.w = {}
        self.r = {}
        self.excl = False


def bufs(*shape):
    a = np.empty(shape, dtype=object)
    for idx in np.ndindex(*shape):
        a[idx] = Buf()
    return a


def flat(*items):
    out = []
    for it in items:
        if it is None:
            continue
        if isinstance(it, Buf):
            out.append(it)
        elif isinstance(it, np.ndarray):
            out.extend(it.ravel().tolist())
        else:
            for x in it:
                out.extend(flat(x))
    return out


class Prog:
    def __init__(self, nc, es):
        self.nc = nc
        self.es = es
        self.eng = {"pe": nc.tensor, "dve": nc.vector, "act": nc.scalar, "pool": nc.gpsimd, "sp": nc.sync}
        self.tl = {k: TL(es.enter_context(nc.semaphore("s_" + k)), k) for k in self.eng}
        self.seen = {k: {} for k in self.eng}
        self.nsem = 5
        self.out_sems = []
        self.dsems = []
        self.rings = {}
        self.last = {}
        self.ring_pos = {}
        self.fsem = TL(es.enter_context(nc.semaphore("s_fence")), "fence")

    def dsem(self, name):
        self.nsem += 1
        tl = TL(self.es.enter_context(self.nc.semaphore("d_" + name)), name)
        self.dsems.append(tl)
        return tl

    def flush(self, e):
        ins = self.last.get(e)
        if ins is not None:
            tl = self.tl[e]
            ins.then_inc(tl.sem, 1)
            tl.n += 1
            self.last[e] = None

    def _waits(self, e, r, w):
        need = {}
        own = self.tl[e]
        for b in r:
            for tl, v in b.w.items():
                if tl is own and e == "pe":
                    continue
                if need.get(tl, 0) < v:
                    need[tl] = v
            if b.excl:
                for tl, v in b.r.items():
                    if tl is own:
                        continue
                    if need.get(tl, 0) < v:
                        need[tl] = v
        for b in w:
            for tl, v in b.w.items():
                if tl is own and e == "pe":
                    continue
                if need.get(tl, 0) < v:
                    need[tl] = v
            for tl, v in b.r.items():
                if tl is own and e == "pe":
                    continue
                if need.get(tl, 0) < v:
                    need[tl] = v
        seen = self.seen[e]
        h = self.eng[e]
        for tl, v in need.items():
            if seen.get(tl, 0) >= v:
                continue
            if v > tl.n:
                assert tl.name in self.eng and v == tl.n + 1, (tl.name, v, tl.n)
                self.flush(tl.name)
            h.wait_ge(tl.sem, v)
            seen[tl] = v

    def op(self, e, fn, r=(), w=(), inc=True):
        r = flat(r)
        w = flat(w)
        self._waits(e, r, w)
        ins = fn()
        tl = self.tl[e]
        self.last[e] = ins
        v = tl.n + 1
        for b in w:
            b.w[tl] = v
        for b in r:
            if b.r.get(tl, 0) < v:
                b.r[tl] = v
        return ins

    def ring_next(self, e):
        ring = self.rings.setdefault(e, [])
        if len(ring) < 12:
            tl = self.dsem("ring_%s%d" % (e, len(ring)))
            ring.append(tl)
            self.ring_pos[e] = len(ring) - 1
            return tl
        i = (self.ring_pos[e] + 1) % len(ring)
        self.ring_pos[e] = i
        tl = ring[i]
        if self.seen[e].get(tl, 0) < tl.n:
            self.eng[e].wait_ge(tl.sem, tl.n)
            self.seen[e][tl] = tl.n
        return tl

    def dma(self, e, out, in_, ds=None, r=(), w=(), **kw):
        r = flat(r)
        w = flat(w)
        self._waits(e, r, w)
        if ds is None:
            ds = self.ring_next(e)
        ins = self.eng[e].dma_start(out=out, in_=in_, **kw)
        ins.then_inc(ds.sem, 16)
        ds.n += 16
        for b in w:
            b.w[ds] = ds.n
        for b in r:
            b.r[ds] = ds.n
        return ins

    def wait_all(self, e, tls):
        h = self.eng[e]
        for tl in tls:
            if tl.n > 0:
                h.wait_ge(tl.sem, tl.n)


def _host_consts():
    k = np.arange(128)
    ident = np.eye(128, dtype=np.float32)
    U = (k[:, None] <= k[None, :]).astype(np.float32)
    Li = (k[:, None] >= k[None, :]).astype(np.float32)
    Ls = (k[:, None] > k[None, :]).astype(np.float32)
    Us = (k[:, None] < k[None, :]).astype(np.float32)
    R = np.zeros((128, 128), np.float32)
    for p in range(128):
        if (p % 64) < 32:
            R[p, p + 32] = -1.0
        else:
            R[p, p - 32] = 1.0
    Rt = np.ascontiguousarray(R.T)
    cst = np.concatenate([ident, U, Li, Ls, Us, Rt], axis=1)
    t = np.arange(L_SMP)
    row = (t // 64).astype(np.float32)
    col = (t % 64).astype(np.float32)
    inv_freq = (10000.0 ** (-np.arange(32, dtype=np.float32) / 32)).astype(np.float32)
    rope = np.zeros((128, 2, L_SMP), np.float32)
    for p in range(128):
        pos = row if p < 64 else col
        ang = (pos * inv_freq[p % 32]).astype(np.float32)
        rope[p, 0] = np.cos(ang)
        rope[p, 1] = np.sin(ang)
    def cnt(L):
        o = np.zeros((4, L), np.float32)
        tt = np.arange(L)
        for gi, w in enumerate((2, 4, 8, 16)):
            lo = np.clip(tt - w // 2, 0, L)
            hi = np.clip(tt - w // 2 + w, 0, L)
            o[gi] = 1.0 / (hi - lo).astype(np.float32)
        return np.ascontiguousarray(np.broadcast_to(o[None], (128, 4, L)))
    return cst, rope, cnt(L_CTX), cnt(L_SMP)


C_ID, C_U, C_LI, C_LS, C_US, C_RT = range(6)


class Builder:
    def __init__(self, cfg):
        self.cfg = cfg
        self.nc = bass.Bass("TRN2", target_bir_lowering=False)

    def din(self, name, shape, dt=F32):
        return self.nc.dram_tensor(name, list(shape), dt, kind="ExternalInput").ap()

    def dout(self, name, shape, dt=F32):
        return self.nc.dram_tensor(name, list(shape), dt, kind="ExternalOutput").ap()

    def sb(self, es, name, shape, dt):
        return es.enter_context(self.nc.sbuf_tensor("t_" + name, list(shape), dt))

    def build(self):
        nc = self.nc
        cfg = self.cfg
        I = {}
        I["x_ctx"] = self.din("x_ctx", [2 * L_CTX, D])
        I["x_smp"] = self.din("x_smp", [L_SMP, D])
        I["cache_k"] = self.din("cache_k", [PAST, 256])
        I["cache_v"] = self.din("cache_v", [PAST, 256])
        I["state"] = self.din("state", [2, NH * HP, NS])
        I["cond"] = self.din("cond", [2, D])
        I["ada_w"] = self.din("ada_w", [2, D, NMOD * D])
        I["ada_b"] = self.din("ada_b", [2, NMOD * D])
        I["norm_g"] = self.din("norm_g", [2, 6, D])
        I["ffn_w_in"] = self.din("ffn_w_in", [2, 2, D, 2 * DFF])
        I["ffn_w_out"] = self.din("ffn_w_out", [2, 2, DFF, D])
        I["mix_w_in"] = self.din("mix_w_in", [1, D, MIX_IN])
        I["pool_w"] = self.din("pool_w", [1, 4, 256, 256])
        I["pool_scale"] = self.din("pool_scale", [1, 1024])
        I["qk_norm_g"] = self.din("qk_norm_g", [1, 2, 128])
        I["mix_w_out"] = self.din("mix_w_out", [1, D, D])
        I["ssm_w_in"] = self.din("ssm_w_in", [1, D, SSM_IN])
        I["ssm_conv_w"] = self.din("ssm_conv_w", [1, CONV_DIM, 3])
        I["ssm_conv_b"] = self.din("ssm_conv_b", [1, CONV_DIM])
        I["ssm_dt_bias"] = self.din("ssm_dt_bias", [1, 2, NH])
        I["ssm_A_log"] = self.din("ssm_A_log", [1, 2, NH])
        I["ssm_D"] = self.din("ssm_D", [1, 2, NH])
        I["ssm_norm_g"] = self.din("ssm_norm_g", [1, D_INNER])
        I["ssm_w_out"] = self.din("ssm_w_out", [1, D_INNER, D])
        I["cst"] = self.din("cst", [128, 768])
        I["rope"] = self.din("rope", [128, 2, L_SMP])
        I["pcnt_ctx"] = self.din("pcnt_ctx", [128, 4, L_CTX])
        I["pcnt_smp"] = self.din("pcnt_smp", [128, 4, L_SMP])
        O = {}
        O["y_ctx"] = self.dout("y_ctx", [2 * L_CTX, D])
        O["y_smp"] = self.dout("y_smp", [L_SMP, D])
        O["new_k"] = self.dout("new_k", [2 * L_CTX, 256])
        O["new_v"] = self.dout("new_v", [2 * L_CTX, 256])
        O["new_ssm"] = self.dout("new_ssm", [2, 2, NH * HP, NS])
        self.I, self.O = I, O
        self.hscr = nc.dram_tensor("hscr", [128, KC, L_SMP], F32, kind="Internal").ap()
        self.ygscr = nc.dram_tensor("ygscr", [128, 32, L_SMP], BF16, kind="Internal").ap()
        self.wi_scr = nc.dram_tensor("wi_scr", [4 * JH, 128, KC * 256], BF16, kind="Internal").ap()
        self.wo_scr = nc.dram_tensor("wo_scr", [4 * KC, 128, JH * 128], BF16, kind="Internal").ap()

        with ExitStack() as es:
            P = Prog(nc, es)
            self.P = P
            self.ps = [es.enter_context(nc.psum_tensor("ps%d" % i, [128, 512], F32)) for i in range(8)]
            self.psb = bufs(8)
            for b_ in self.psb:
                b_.excl = True
            self.ps_rr = 0
            self.xs_rr = 0
            self.held = set()
            self.setup(es)
            for r in cfg.get("passes", (1, 0)):
                self.run_pass(r)
            self.fence()
        return nc

    def bank(self):
        while True:
            i = self.ps_rr
            self.ps_rr = (self.ps_rr + 1) % 7
            if i not in self.held:
                return i

    def setup(self, es):
        nc, P, I = self.nc, self.P, self.I
        sb = self.sb
        self.cst = sb(es, "cst", [128, 768], F32)
        self.cstb = Buf()
        self.ones_f = sb(es, "ones_f", [128, 128], F32)
        self.ones_b = sb(es, "ones_b", [128, 128], BF16)
        self.onesb = Buf()
        self.mods = sb(es, "mods", [128, 2, 2, NMOD * KC], F32)
        self.modsb = Buf()
        self.normgT = sb(es, "normgT", [128, 192], F32)
        self.vecT = sb(es, "vecT", [128, 90], F32)
        self.convw = sb(es, "convw", [128, 48, 3], F32)
        self.gs = sb(es, "gs", [128, 12, KC], F32)
        self.gg = sb(es, "gg", [128, 12, KC], F32)
        self.dtb = sb(es, "dtb", [128, 128], F32)
        self.Aneg = sb(es, "Aneg", [128, 128], F32)
        self.Dsum = sb(es, "Dsum", [128, 64], F32)
        self.smallb = Buf()
        self.epsc = sb(es, "epsc", [128, 1], F32)
        self.onec = sb(es, "onec", [128, 1], F32)
        self.tmp = [sb(es, "tmp%d" % i, [128, 512], F32) for i in range(6)]
        self.tmpb = bufs(6)
        self.rstd = sb(es, "rstd", [128, 512], F32)
        self.rstdb = Buf()
        self.tmp_rr = 0

        P.dma("sp", self.cst[:], I["cst"], None, w=[self.cstb])
        P.op("dve", lambda: nc.vector.memset(self.ones_f[:], 1.0), w=[self.onesb])
        P.op("dve", lambda: nc.vector.memset(self.ones_b[:], 1.0), w=[self.onesb])
        P.op("dve", lambda: nc.vector.memset(self.epsc[:], EPS), w=[self.smallb])
        P.op("dve", lambda: nc.vector.memset(self.onec[:], 1.0), w=[self.smallb])

        with ExitStack() as s2:
            stage = [sb(s2, "stg%d" % i, [128, 128], F32) for i in range(2)]
            stageb = bufs(2)
            adabT = sb(s2, "adabT", [128, 288], F32)
            adabTb = Buf()
            condT = sb(s2, "condT", [128, 32], F32)
            scT = sb(s2, "scT", [128, KC, 2], BF16)
            condb = Buf()
            si = [0]

            def load_T(dst_ap, rows_ap, R, wb):
                k = si[0] % 2
                si[0] += 1
                P.dma("sp", stage[k][0:R, :], rows_ap, None, w=[stageb[k]])
                bk = self.bank()
                P.op("pe", lambda: nc.tensor.transpose(self.ps[bk][:, 0:R], stage[k][0:R, :],
                                                       self.cst[0:R, C_ID * 128:C_ID * 128 + R]),
                     r=[stageb[k], self.cstb], w=[self.psb[bk]])
                P.op("dve", lambda: nc.vector.tensor_copy(out=dst_ap, in_=self.ps[bk][:, 0:R]),
                     r=[self.psb[bk]], w=[wb])

            ab = I["ada_b"].rearrange("l (c p) -> (l c) p", p=128)
            for i in range(3):
                load_T(adabT[:, i * 96:(i + 1) * 96], ab[i * 96:(i + 1) * 96, :], 96, adabTb)
            ng = I["norm_g"].rearrange("l i (c p) -> (l i c) p", p=128)
            for i in range(2):
                load_T(self.normgT[:, i * 96:(i + 1) * 96], ng[i * 96:(i + 1) * 96, :], 96, self.smallb)
            load_T(condT[:, :], I["cond"].rearrange("r (c p) -> (r c) p", p=128), 32, condb)
            k = si[0] % 2
            si[0] += 1
            P.dma("sp", stage[k][0:8, :], I["pool_scale"].rearrange("o (c p) -> (o c) p", p=128), None, w=[stageb[k]])
            P.dma("sp", stage[k][8:10, :], I["qk_norm_g"].rearrange("o i p -> (o i) p"), None, w=[stageb[k]])
            P.dma("sp", stage[k][10:58, :], I["ssm_conv_b"].rearrange("o (c p) -> (o c) p", p=128), None, w=[stageb[k]])
            P.dma("sp", stage[k][58:90, :], I["ssm_norm_g"].rearrange("o (c p) -> (o c) p", p=128), None, w=[stageb[k]])
            bk = self.bank()
            P.op("pe", lambda: nc.tensor.transpose(self.ps[bk][:, 0:90], stage[k][0:90, :],
                                                   self.cst[0:90, C_ID * 128:C_ID * 128 + 90]),
                 r=[stageb[k], self.cstb], w=[self.psb[bk]])
            P.op("dve", lambda: nc.vector.tensor_copy(out=self.vecT[:, :], in_=self.ps[bk][:, 0:90]),
                 r=[self.psb[bk]], w=[self.smallb])
            cw = I["ssm_conv_w"].rearrange("o (c p) j -> p (o c) j", p=128)
            for i in range(4):
                P.dma("sp", self.convw[:, i * 12:(i + 1) * 12, :], cw[:, i * 12:(i + 1) * 12, :], None, w=[self.smallb])
            P.dma("sp", self.dtb[:, :], I["ssm_dt_bias"].rearrange("o a h -> o (a h)").partition_broadcast(128).squeeze(1),
                  None, w=[self.smallb])
            P.dma("sp", self.Aneg[:, :], I["ssm_A_log"].rearrange("o a h -> o (a h)").partition_broadcast(128).squeeze(1),
                  None, w=[self.smallb])
            dtmp = self.tmp[0]
            P.dma("sp", dtmp[:, 0:128], I["ssm_D"].rearrange("o a h -> o (a h)").partition_broadcast(128).squeeze(1),
                  None, w=[self.tmpb[0]])
            P.op("act", lambda: nc.scalar.activation(out=self.Aneg[:, :], in_=self.Aneg[:, :], func=AF.Exp),
                 r=[self.smallb], w=[self.smallb])
            P.op("dve", lambda: nc.vector.tensor_scalar(out=self.Aneg[:, :], in0=self.Aneg[:, :], scalar1=-1.0,
                                                        scalar2=None, op0=ALU.mult),
                 r=[self.smallb], w=[self.smallb])
            P.op("dve", lambda: nc.vector.tensor_tensor(out=self.Dsum[:, :], in0=dtmp[:, 0:64], in1=dtmp[:, 64:128],
                                                        op=ALU.add),
                 r=[self.tmpb[0]], w=[self.smallb])
            P.op("act", lambda: nc.scalar.activation(out=scT[:, :, :].rearrange("p c r -> p r c"),
                                                     in_=condT[:, :].rearrange("p (r c) -> p r c", r=2),
                                                     func=AF.Silu),
                 r=[condb], w=[condb])
            NB = 1024
            aslot = [sb(s2, "aslot%d" % i, [128, KC, NB], BF16) for i in range(2)]
            aslotb = bufs(2)
            asem = [P.dsem("aslot%d" % i) for i in range(2)]
            blocks = [(l, b) for l in range(2) for b in range(NMOD * D // NB)]

            def aload(i):
                l, b = blocks[i]
                src = I["ada_w"][l, :, b * NB:(b + 1) * NB].rearrange("(k p) n -> p k n", p=128)
                P.dma("pool", aslot[i % 2][:, :, :], src, asem[i % 2], w=[aslotb[i % 2]])

            aload(0)
            for i, (l, b) in enumerate(blocks):
                if i + 1 < len(blocks):
                    aload(i + 1)
                sl = aslot[i % 2]
                bk = self.bank()
                pv = self.ps[bk][:, 0:16].rearrange("p (c r) -> p c r", r=2)
                for cc in range(8):
                    for kc in range(KC):
                        last = (cc == 7 and kc == KC - 1)
                        P.op("pe", lambda: nc.tensor.matmul(pv[:, cc, :], lhsT=sl[:, kc, cc * 128:(cc + 1) * 128],
                                                            rhs=scT[:, kc, :], start=(kc == 0), stop=(kc == KC - 1)),
                             r=[aslotb[i % 2], condb], w=[self.psb[bk]], inc=last)
                for r in range(2):
                    P.op("dve", lambda: nc.vector.tensor_tensor(
                        out=self.mods[:, l, r, b * 8:(b + 1) * 8], in0=pv[:, :, r],
                        in1=adabT[:, l * 144 + b * 8:l * 144 + (b + 1) * 8], op=ALU.add),
                        r=[self.psb[bk], adabTb], w=[self.modsb])
            for l in range(2):
                for s in range(3):
                    for r in range(2):
                        idx = (l * 3 + s) * 2 + r
                        wgt = 1.0 if s == 1 else 0.5
                        P.op("dve", lambda: nc.vector.scalar_tensor_tensor(
                            out=self.gs[:, idx, :], in0=self.mods[:, l, r, (3 * s + 1) * KC:(3 * s + 2) * KC],
                            scalar=1.0, in1=self.normgT[:, (l * 6 + 2 * s) * KC:(l * 6 + 2 * s + 1) * KC],
                            op0=ALU.add, op1=ALU.mult), r=[self.modsb, self.smallb], w=[self.modsb])
                        P.op("dve", lambda: nc.vector.scalar_tensor_tensor(
                            out=self.gg[:, idx, :], in0=self.mods[:, l, r, (3 * s + 2) * KC:(3 * s + 3) * KC],
                            scalar=wgt, in1=self.normgT[:, (l * 6 + 2 * s + 1) * KC:(l * 6 + 2 * s + 2) * KC],
                            op0=ALU.mult, op1=ALU.mult), r=[self.modsb, self.smallb], w=[self.modsb])
            self.fence()

    def fence(self):
        P = self.P
        es = ["pe", "dve", "act", "pool"]
        sp = P.eng["sp"]
        for e in es:
            P.flush(e)
        for tl in [P.tl[e] for e in es] + P.dsems:
            if tl.n > P.seen["sp"].get(tl, 0):
                sp.wait_ge(tl.sem, tl.n)
                P.seen["sp"][tl] = tl.n
        sp.sem_inc(P.fsem.sem, 1)
        P.fsem.n += 1
        for e in es:
            P.eng[e].wait_ge(P.fsem.sem, P.fsem.n)
            for tl in [P.tl[o] for o in es + ["sp"]] + P.dsems:
                P.seen[e][tl] = tl.n

    def sh_ap(self, l, s, r, c):
        return self.mods[:, l, r, 3 * s * KC + c:3 * s * KC + c + 1]

    def gs_ap(self, l, s, r, c):
        idx = (l * 3 + s) * 2 + r
        return self.gs[:, idx, c:c + 1]

    def gg_ap(self, l, s, r, c):
        idx = (l * 3 + s) * 2 + r
        return self.gg[:, idx, c:c + 1]

    def newtmp(self):
        i = self.tmp_rr
        self.tmp_rr = (self.tmp_rr + 1) % 6
        return i

    def rstd_from_stats(self, sbk, dim):
        nc, P = self.nc, self.P
        P.op("act", lambda: nc.scalar.activation(out=self.rstd[:, :], in_=self.ps[sbk][:, :], func=AF.Sqrt,
                                                 scale=1.0 / dim, bias=self.epsc[:, 0:1]),
             r=[self.psb[sbk], self.smallb], w=[self.rstdb])
        P.op("dve", lambda: nc.vector.reciprocal(out=self.rstd[:, :], in_=self.rstd[:, :]),
             r=[self.rstdb], w=[self.rstdb])

    def modulate(self, h, hb, tt, l, s, r, u_ap, ub):
        nc, P = self.nc, self.P
        sbk = 7
        for c in range(KC):
            ti = self.newtmp()
            P.op("act", lambda: nc.scalar.activation(out=self.tmp[ti][:, :], in_=h[:, c, tt * 512:(tt + 1) * 512],
                                                     func=AF.Square), r=[hb[c, tt]], w=[self.tmpb[ti]])
            P.op("pe", lambda: nc.tensor.matmul(self.ps[sbk][:, :], lhsT=self.ones_f[:, :], rhs=self.tmp[ti][:, :],
                                                start=(c == 0), stop=(c == KC - 1)),
                 r=[self.tmpb[ti], self.onesb], w=[self.psb[sbk]])
        self.rstd_from_stats(sbk, D)
        for c in range(KC):
            ti = self.newtmp()
            P.op("dve", lambda: nc.vector.tensor_tensor(out=self.tmp[ti][:, :], in0=h[:, c, tt * 512:(tt + 1) * 512],
                                                        in1=self.rstd[:, :], op=ALU.mult),
                 r=[hb[c, tt], self.rstdb], w=[self.tmpb[ti]])
            P.op("act", lambda: nc.scalar.activation(out=u_ap(c), in_=self.tmp[ti][:, :], func=AF.Identity,
                                                     scale=self.gs_ap(l, s, r, c), bias=self.sh_ap(l, s, r, c)),
                 r=[self.tmpb[ti], self.modsb], w=[ub(c)])

    def residual(self, h, hb, tt, l, s, r, y_ap, yb, sbk, dim=D):
        nc, P = self.nc, self.P
        self.rstd_from_stats(sbk, dim)
        for c in range(KC):
            ti = self.newtmp()
            P.op("dve", lambda: nc.vector.tensor_tensor(out=self.tmp[ti][:, :], in0=y_ap(c), in1=self.rstd[:, :],
                                                        op=ALU.mult), r=[yb(c), self.rstdb], w=[self.tmpb[ti]])
            hv = h[:, c, tt * 512:(tt + 1) * 512]
            P.op("dve", lambda: nc.vector.scalar_tensor_tensor(out=hv, in0=self.tmp[ti][:, :],
                                                               scalar=self.gg_ap(l, s, r, c), in1=hv,
                                                               op0=ALU.mult, op1=ALU.add),
                 r=[self.tmpb[ti], hb[c, tt], self.modsb], w=[hb[c, tt]])

    def out_proj_residual(self, w_ap, KCin, rhs_ap, rhsb, slots, slotb, ssem, h, hb, tt, l, s, r, uy, uyb, post_scale=None, wload_fn=None):
        nc, P = self.nc, self.P
        sbk = 7
        ns = len(slots)

        def wload(oc):
            if wload_fn is not None:
                return wload_fn(oc, oc % ns)
            src = w_ap[:, oc * 128:(oc + 1) * 128].rearrange("(k p) n -> p k n", p=128)
            P.dma("pool", slots[oc % ns][:, 0:KCin, :], src, ssem[oc % ns], w=[slotb[oc % ns]])

        pend = None
        for oc in range(min(ns - 1, KC)):
            wload(oc)
        for oc in range(KC):
            if oc + ns - 1 < KC:
                wload(oc + ns - 1)
            bk = self.bank()
            sl = slots[oc % ns]
            for kc in range(KCin):
                P.op("pe", lambda: nc.tensor.matmul(self.ps[bk][:, :], lhsT=sl[:, kc, :], rhs=rhs_ap(kc),
                                                    start=(kc == 0), stop=(kc == KCin - 1)),
                     r=[slotb[oc % ns], rhsb(kc)], w=[self.psb[bk]], inc=(kc == KCin - 1))
            if pend is not None:
                pend()
            yv = uy[:, oc * 512:(oc + 1) * 512]
            ti = self.newtmp()
            if post_scale is None:
                P.op("act", lambda: nc.scalar.copy(out=yv, in_=self.ps[bk][:, :]), r=[self.psb[bk]], w=[uyb[oc]])
                P.op("act", lambda: nc.scalar.activation(out=self.tmp[ti][:, :], in_=self.ps[bk][:, :], func=AF.Square),
                     r=[self.psb[bk]], w=[self.tmpb[ti]])
            else:
                ps_ap, ps_b = post_scale
                P.op("dve", lambda: nc.vector.tensor_tensor(out=yv, in0=self.ps[bk][:, :], in1=ps_ap, op=ALU.mult),
                     r=[self.psb[bk], ps_b], w=[uyb[oc]])
                P.op("act", lambda: nc.scalar.activation(out=self.tmp[ti][:, :], in_=yv, func=AF.Square),
                     r=[uyb[oc]], w=[self.tmpb[ti]])

            def mk(ti=ti, oc=oc):
                def f():
                    P.op("pe", lambda: nc.tensor.matmul(self.ps[sbk][:, :], lhsT=self.ones_f[:, :],
                                                        rhs=self.tmp[ti][:, :], start=(oc == 0), stop=(oc == KC - 1)),
                         r=[self.tmpb[ti], self.onesb], w=[self.psb[sbk]])
                return f
            pend = mk()
        pend()
        self.residual(h, hb, tt, l, s, r, lambda c: uy[:, c * 512:(c + 1) * 512], lambda c: uyb[c], sbk)

    def load_x_tile(self, xs, xsb, r, tt, ht, hb):
        nc, P, I = self.nc, self.P, self.I
        x = I["x_ctx"] if r == 0 else I["x_smp"]
        for tk in range(4):
            row0 = (tt * 4 + tk) * 128
            for hf in range(2):
                k = self.xs_rr % 2
                self.xs_rr += 1
                P.dma("sp", xs[k][:, :], x[row0:row0 + 128, hf * 1024:(hf + 1) * 1024], None, w=[xsb[k]])
                for c4 in range(2):
                    bk = self.bank()
                    for q in range(4):
                        c = c4 * 4 + q
                        P.op("pe", lambda: nc.tensor.transpose(self.ps[bk][:, q * 128:(q + 1) * 128],
                                                               xs[k][:, c * 128:(c + 1) * 128],
                                                               self.cst[:, C_ID * 128:(C_ID + 1) * 128]),
                             r=[xsb[k], self.cstb], w=[self.psb[bk]])
                    c0 = hf * 8 + c4 * 4
                    P.op("dve", lambda: nc.vector.tensor_copy(
                        out=ht[:, c0:c0 + 4, tk * 128:(tk + 1) * 128],
                        in_=self.ps[bk][:, :].rearrange("p (q t) -> p q t", q=4)),
                        r=[self.psb[bk]], w=[hb[c0:c0 + 4, 0]])

    def store_y_tile(self, ost, ostb, r, tt, ht, hb):
        nc, P, O = self.nc, self.P, self.O
        y = O["y_ctx"] if r == 0 else O["y_smp"]
        for tk in range(4):
            row0 = (tt * 4 + tk) * 128
            for qd in range(4):
                k = self.xs_rr % 2
                self.xs_rr += 1
                bk = self.bank()
                for q in range(4):
                    c = qd * 4 + q
                    P.op("pe", lambda: nc.tensor.transpose(self.ps[bk][:, q * 128:(q + 1) * 128],
                                                           ht[:, c, tk * 128:(tk + 1) * 128],
                                                           self.cst[:, C_ID * 128:(C_ID + 1) * 128]),
                         r=[hb[c, 0], self.cstb], w=[self.psb[bk]])
                P.op("dve", lambda: nc.vector.tensor_copy(out=ost[k][:, :], in_=self.ps[bk][:, :]),
                     r=[self.psb[bk]], w=[ostb[k]])
                P.dma("sp", y[row0:row0 + 128, qd * 512:(qd + 1) * 512], ost[k][:, :], None, r=[ostb[k]])

    def h_load(self, h, hb, T, sem):
        P = self.P
        for tt in range(T // 512):
            P.dma("sp", h[:, :, tt * 512:(tt + 1) * 512], self.hscr[:, :, tt * 512:(tt + 1) * 512], None, w=[hb[:, tt]])

    def h_store(self, h, hb, T, sem):
        P = self.P
        for tt in range(T // 512):
            P.dma("sp", self.hscr[:, :, tt * 512:(tt + 1) * 512], h[:, :, tt * 512:(tt + 1) * 512], None, r=[hb[:, tt]])

    def ffn(self, T, l, f, r, uy, uyb, src, dst):
        nc, P, I = self.nc, self.P, self.I
        s = 0 if f == 0 else 2
        NT = T // 512
        w_in = I["ffn_w_in"][l, f]
        w_out = I["ffn_w_out"][l, f]
        uyh = uy[:, :].bitcast(BF16)
        tag = "%d%d%d" % (r, l, f)
        hscr = self.hscr
        with ExitStack() as es:
            g = self.sb(es, "g_" + tag, [128, JH, T], BF16)
            gb = bufs(JH, NT)
            ub = bufs(KC, NT)
            with ExitStack() as ea:
                ht = self.sb(ea, "hA_" + tag, [128, KC, 512], F32)
                if src == "x":
                    xs = [self.sb(ea, "xs%d_%s" % (i, tag), [128, 1024], F32) for i in range(2)]
                    xsb = bufs(2)
                hb = bufs(KC, 1)
                for tt in range(NT):
                    if src == "x":
                        self.load_x_tile(xs, xsb, r, tt, ht, hb)
                        P.dma("sp", hscr[:, :, tt * 512:(tt + 1) * 512], ht[:, :, :], None, r=[hb])
                    else:
                        P.dma("sp", ht[:, :, :], hscr[:, :, tt * 512:(tt + 1) * 512], None, w=[hb])
                    if not self.cfg.get("ffn_io_only"):
                        self.modulate(ht, hb, 0, l, s, r, lambda c: uyh[:, c * T + tt * 512:c * T + (tt + 1) * 512],
                                      lambda c: ub[c, tt])
                self.fence()
            with ExitStack() as eb:
                NW = 4
                win = [self.sb(eb, "wi%d_%s" % (i, tag), [128, KC, 256], BF16) for i in range(NW)]
                winb = bufs(NW)
                wsem = [P.dsem("wi%d_%s" % (i, tag)) for i in range(NW)]

                fi = l * 2 + f
                have_scr = (r == 0) and (1 in self.cfg.get("passes", (1, 0)))

                def wload(j):
                    k = j % NW
                    flat_slot = win[k][:, :, :].rearrange("p k n -> p (k n)")
                    if have_scr:
                        P.dma("sp", flat_slot, self.wi_scr[fi * JH + j], wsem[k], w=[winb[k]])
                        return
                    for half in range(2):
                        srcw = w_in[:, half * DFF + j * 128: half * DFF + (j + 1) * 128].rearrange("(k p) n -> p k n", p=128)
                        P.dma("pool", win[k][:, :, half * 128:(half + 1) * 128], srcw, wsem[k], w=[winb[k]])
                    if r == 1:
                        P.dma("sp", self.wi_scr[fi * JH + j], flat_slot, None, r=[winb[k]])

                JN = 0 if self.cfg.get("ffn_io_only") else JH
                for j in range(min(NW - 1, JN)):
                    wload(j)
                for j in range(JN):
                    if j + NW - 1 < JN:
                        wload(j + NW - 1)
                    sl = win[j % NW]
                    banks = [[self.bank() for tt in range(NT)] for half in range(2)]
                    for half in range(2):
                        for kc in range(KC):
                            for tt in range(NT):
                                bk = banks[half][tt]
                                P.op("pe", lambda: nc.tensor.matmul(
                                    self.ps[bk][:, :], lhsT=sl[:, kc, half * 128:(half + 1) * 128],
                                    rhs=uyh[:, kc * T + tt * 512:kc * T + (tt + 1) * 512],
                                    start=(kc == 0), stop=(kc == KC - 1)),
                                    r=[winb[j % NW], ub[kc, tt]], w=[self.psb[bk]])
                    for tt in range(NT):
                        ti = self.newtmp()
                        P.op("act", lambda: nc.scalar.activation(out=self.tmp[ti][:, :], in_=self.ps[banks[0][tt]][:, :],
                                                                 func=AF.Silu), r=[self.psb[banks[0][tt]]], w=[self.tmpb[ti]])
                        P.op("dve", lambda: nc.vector.tensor_tensor(out=g[:, j, tt * 512:(tt + 1) * 512], in0=self.tmp[ti][:, :],
                                                                    in1=self.ps[banks[1][tt]][:, :], op=ALU.mult),
                             r=[self.tmpb[ti], self.psb[banks[1][tt]]], w=[gb[j, tt]])
                self.fence()
            with ExitStack() as ec:
                ht = self.sb(ec, "hC_" + tag, [128, KC, 512], F32)
                NO = 3 if T == 512 else 2
                wout = [self.sb(ec, "wo%d_%s" % (i, tag), [128, JH, 128], BF16) for i in range(NO)]
                woutb = bufs(NO)
                wosem = [P.dsem("wo%d_%s" % (i, tag)) for i in range(NO)]
                if dst == "y":
                    ost = [self.sb(ec, "os%d_%s" % (i, tag), [128, 512], F32) for i in range(2)]
                    ostb = bufs(2)
                hb = bufs(KC, 1)
                for tt in range(NT):
                    P.dma("sp", ht[:, :, :], hscr[:, :, tt * 512:(tt + 1) * 512], None, w=[hb])
                    def wo_load(oc, k, tt=tt):
                        flat_slot = wout[k][:, :, :].rearrange("p k n -> p (k n)")
                        if have_scr or tt > 0:
                            P.dma("sp", flat_slot, self.wo_scr[fi * KC + oc], wosem[k], w=[woutb[k]])
                            return
                        srcw = w_out[:, oc * 128:(oc + 1) * 128].rearrange("(k p) n -> p k n", p=128)
                        P.dma("pool", wout[k][:, :, :], srcw, wosem[k], w=[woutb[k]])
                        if r == 1:
                            P.dma("sp", self.wo_scr[fi * KC + oc], flat_slot, None, r=[woutb[k]])

                    if not self.cfg.get("ffn_io_only"):
                        self.out_proj_residual(w_out, JH, lambda kc: g[:, kc, tt * 512:(tt + 1) * 512], lambda kc: gb[kc, tt],
                                               wout, woutb, wosem, ht, hb, 0, l, s, r, uy, uyb, wload_fn=wo_load)
                    if dst == "y":
                        self.store_y_tile(ost, ostb, r, tt, ht, hb)
                    else:
                        P.dma("sp", hscr[:, :, tt * 512:(tt + 1) * 512], ht[:, :, :], None, r=[hb])
                self.fence()

    def run_pass(self, r):
        nc, P = self.nc, self.P
        cfg = self.cfg
        T = 512 if r == 0 else 1024
        with ExitStack() as es:
            uy = self.sb(es, "uy%d" % r, [128, 8192], F32)
            uyb = bufs(KC)
            phases = []
            for l in range(2):
                if cfg.get("ffn", True):
                    phases.append(("ffn", l, 0))
                if cfg.get("mixer", True) and l in cfg.get("mix_layers", (0, 1)):
                    phases.append(("mix", l, 0))
                if cfg.get("ffn", True):
                    phases.append(("ffn", l, 1))
            assert phases[0][0] == "ffn" and phases[-1][0] == "ffn"
            for i, (kind, l, f) in enumerate(phases):
                if kind == "ffn":
                    self.ffn(T, l, f, r, uy, uyb, "x" if i == 0 else "scr", "y" if i == len(phases) - 1 else "scr")
                    continue
                with ExitStack() as em:
                    win = [self.sb(em, "win%d_%d%d" % (k_, r, l), [128, KC, 256], BF16) for k_ in range(2)]
                    winb = bufs(2)
                    wsem = [P.dsem("win%d_%d%d" % (k_, r, l)) for k_ in range(2)]
                    if l == 0:
                        self.even_mixer(T, r, uy, uyb, win, winb, wsem)
                    else:
                        self.odd_mixer(T, r, uy, uyb, win, winb, wsem)
                    self.fence()
            self.fence()

    def bank_hold(self):
        i = self.bank()
        self.held.add(i)
        return i

    def bank_release(self, i):
        self.held.discard(i)

    def mix_out_residual(self, w_ap, KCin, rhs_of_tile, rhsb_of_tile, T, l, r, uy, uyb, win, winb, wsem, tag,
                         post_scale=None):
        nc, P = self.nc, self.P
        with ExitStack() as es:
            ht = [self.sb(es, "ht%d_%s" % (i, tag), [128, KC, 512], F32) for i in range(1)]
            slots = [win[0][:, :, 0:128], win[1][:, :, 0:128]] if KCin <= KC else None
            if slots is None:
                wo = [self.sb(es, "mwo%d_%s" % (i, tag), [128, KCin, 128], BF16) for i in range(2)]
                slots = [wo[0][:, :, :], wo[1][:, :, :]]
                slb = bufs(2)
                slsem = [P.dsem("mwo%d_%s" % (i, tag)) for i in range(2)]
            else:
                slb, slsem = winb, wsem
            htb = bufs(KC, 1)
            for tt in range(T // 512):
                P.dma("sp", ht[0][:, :, :], self.hscr[:, :, tt * 512:(tt + 1) * 512], None, w=[htb])
                self.out_proj_residual(w_ap, KCin, rhs_of_tile(tt), rhsb_of_tile(tt), slots, slb, slsem,
                                       ht[0], htb, 0, l, 1, r, uy, uyb, post_scale=(post_scale(tt) if post_scale else None))
                P.dma("sp", self.hscr[:, :, tt * 512:(tt + 1) * 512], ht[0][:, :, :], None, r=[htb])
            self.fence()

    def even_mixer(self, T, r, uy, uyb, win, winb, wsem):
        nc, P, I, O = self.nc, self.P, self.I, self.O
        l, s = 0, 1
        nseq = 2 if r == 0 else 1
        L = T // nseq
        NT = T // 512
        uyh = uy[:, :].bitcast(BF16)
        w_in = I["mix_w_in"][0]
        ident = self.cst[:, C_ID * 128:(C_ID + 1) * 128]
        tag = "e%d" % r
        NK = L + (PAST if r == 1 else 0)
        NKC = NK // 128
        with ExitStack() as es:
            cat = self.sb(es, "cat" + tag, [128, KC, T], BF16)
            catb = bufs(KC, NT)
            qT = self.sb(es, "qT" + tag, [128, 8, T], BF16)
            qTb = bufs(8, NT)
            kT = self.sb(es, "kT" + tag, [128, 2, nseq * NK], BF16)
            kTb = bufs(2)
            vtok = self.sb(es, "vtok" + tag, [128, nseq * NKC, 256], BF16)
            vtokb = Buf()
            gq = self.sb(es, "gq" + tag, [128, 2], F32)
            gqb = Buf()
            ub = bufs(KC, NT)
            P.op("dve", lambda: nc.vector.tensor_scalar(out=gq[:, 0:1], in0=self.vecT[:, 8:9], scalar1=128.0 ** -0.5,
                                                        scalar2=None, op0=ALU.mult), r=[self.smallb], w=[gqb])
            P.op("dve", lambda: nc.vector.tensor_copy(out=gq[:, 1:2], in_=self.vecT[:, 9:10]), r=[self.smallb], w=[gqb])
            with ExitStack() as e1:
                h = self.sb(e1, "h_" + tag, [128, KC, T], F32)
                hb = bufs(KC, NT)
                self.h_load(h, hb, T, None)
                for tt in range(NT):
                    self.modulate(h, hb, tt, l, s, r, lambda c: uyh[:, c * T + tt * 512:c * T + (tt + 1) * 512],
                                  lambda c: ub[c, tt])
                self.fence()
            with ExitStack() as e1:
                xpp = self.sb(e1, "xpp" + tag, [128, nseq, L + 16], F32)
                xppb = Buf()
                sA = self.sb(e1, "sA" + tag, [128, nseq, L + 16], F32)
                sB = self.sb(e1, "sB" + tag, [128, nseq, L + 16], F32)
                sb_ = Buf()
                dbf = self.sb(e1, "dbf" + tag, [128, 2, T], BF16)
                dbfb = bufs(2)
                pcnt = self.sb(e1, "pcnt" + tag, [128, 4, L], F32)
                pcntb = Buf()
                pw = self.sb(e1, "pw" + tag, [128, 4, 2, 256], BF16)
                pwb = Buf()
                P.dma("sp", pcnt[:, :, :], I["pcnt_ctx"] if r == 0 else I["pcnt_smp"], None, w=[pcntb])
                for gi_ in range(4):
                    P.dma("pool", pw[:, gi_, :, :], I["pool_w"][0, gi_].rearrange("(c p) d -> p c d", p=128), None, w=[pwb])
                P.op("dve", lambda: nc.vector.memset(xpp[:, :, :], 0.0), w=[xppb])
                if r == 1:
                    rope = self.sb(e1, "rope" + tag, [128, 2, T], F32)
                    ropeb = Buf()
                    P.dma("sp", rope[:, :, :], I["rope"], None, w=[ropeb])
                    ck = self.sb(e1, "ck" + tag, [128, 4, 256], F32)
                    ckb = Buf()
                    P.dma("sp", ck[:, :, :], I["cache_k"].rearrange("(b p) f -> p b f", p=128), None, w=[ckb])
                    P.dma("pool", vtok[:, 0:4, :], I["cache_v"].rearrange("(b p) f -> p b f", p=128), None, w=[vtokb])
                    for kh in range(2):
                        bk = self.bank()
                        for blk in range(4):
                            P.op("pe", lambda: nc.tensor.transpose(self.ps[bk][:, blk * 128:(blk + 1) * 128],
                                                                   ck[:, blk, kh * 128:(kh + 1) * 128], ident),
                                 r=[ckb, self.cstb], w=[self.psb[bk]], inc=(blk == 3))
                        P.op("dve", lambda: nc.vector.tensor_copy(out=kT[:, kh, 0:512], in_=self.ps[bk][:, :]),
                             r=[self.psb[bk]], w=[kTb[kh]])
                else:
                    nkst = self.sb(e1, "nkst" + tag, [128, 4, 256], F32)
                    nkstb = bufs(4)
                    nvst = self.sb(e1, "nvst" + tag, [128, 4, 256], F32)
                    nvstb = bufs(4)

                def u_ap(kc, tt):
                    return uyh[:, kc * T + tt * 512:kc * T + (tt + 1) * 512]

                def wload(i):
                    srcw = w_in[:, i * 256:(i + 1) * 256].rearrange("(k p) n -> p k n", p=128)
                    P.dma("pool", win[i % 2][:, :, :], srcw, wsem[i % 2], w=[winb[i % 2]])

                NI = self.cfg.get("even_ni", 10)
                if NI > 0:
                    wload(0)
                for i in range(NI):
                    if i + 1 < NI:
                        wload(i + 1)
                    sl = win[i % 2]
                    if i == 9:
                        for tb in range(T // 128):
                            bk = self.bank()
                            for kc in range(KC):
                                P.op("pe", lambda: nc.tensor.matmul(
                                    self.ps[bk][:, 0:256], lhsT=uyh[:, kc * T + tb * 128:kc * T + (tb + 1) * 128],
                                    rhs=sl[:, kc, :], start=(kc == 0), stop=(kc == KC - 1)),
                                    r=[winb[i % 2], ub[kc, tb // 4]], w=[self.psb[bk]], inc=(kc == KC - 1))
                            if r == 0:
                                kcx = tb
                            else:
                                kcx = 4 + tb
                            if self.cfg.get("v_dbg", 3) >= 2:
                                P.op("dve", lambda: nc.vector.tensor_copy(out=vtok[:, kcx, :], in_=self.ps[bk][:, 0:256]),
                                     r=[self.psb[bk]], w=[vtokb])
                            if r == 0 and self.cfg.get("v_dbg", 3) >= 3:
                                P.op("act", lambda: nc.scalar.copy(out=nvst[:, tb, :], in_=self.ps[bk][:, 0:256]),
                                     r=[self.psb[bk]], w=[nvstb[tb]])
                                P.dma("sp", O["new_v"][tb * 128:(tb + 1) * 128, :], nvst[:, tb, :], None, r=[nvstb[tb]])
                        continue
                    for sub in range(2):
                        cc = i * 2 + sub
                        for tt in range(NT):
                            bk = self.bank()
                            for kc in range(KC):
                                P.op("pe", lambda: nc.tensor.matmul(
                                    self.ps[bk][:, :], lhsT=sl[:, kc, sub * 128:(sub + 1) * 128], rhs=u_ap(kc, tt),
                                    start=(kc == 0), stop=(kc == KC - 1)),
                                    r=[winb[i % 2], ub[kc, tt]], w=[self.psb[bk]], inc=(kc == KC - 1))
                            if cc < 8:
                                if r == 0:
                                    P.op("act", lambda: nc.scalar.copy(
                                        out=xpp[:, :, 8:8 + L], in_=self.ps[bk][:, :].rearrange("p (s t) -> p s t", s=2)),
                                        r=[self.psb[bk]], w=[xppb])
                                else:
                                    P.op("act", lambda: nc.scalar.copy(
                                        out=xpp[:, 0, 8 + tt * 512:8 + (tt + 1) * 512], in_=self.ps[bk][:, :]),
                                        r=[self.psb[bk]], w=[xppb])
                            else:
                                isq = cc < 16
                                hd = cc - 8 if isq else cc - 16
                                t1 = self.newtmp()
                                P.op("act", lambda: nc.scalar.activation(out=self.tmp[t1][:, :], in_=self.ps[bk][:, :],
                                                                         func=AF.Square), r=[self.psb[bk]], w=[self.tmpb[t1]])
                                sbk = self.bank()
                                P.op("pe", lambda: nc.tensor.matmul(self.ps[sbk][:, :], lhsT=self.ones_f[:, :],
                                                                    rhs=self.tmp[t1][:, :], start=True, stop=True),
                                     r=[self.tmpb[t1], self.onesb], w=[self.psb[sbk]])
                                t2 = self.newtmp()
                                P.op("act", lambda: nc.scalar.activation(out=self.tmp[t2][:, :], in_=self.ps[sbk][:, :],
                                                                         func=AF.Sqrt, scale=1.0 / 128, bias=self.epsc[:, 0:1]),
                                     r=[self.psb[sbk], self.smallb], w=[self.tmpb[t2]])
                                P.op("dve", lambda: nc.vector.reciprocal(out=self.tmp[t2][:, :], in_=self.tmp[t2][:, :]),
                                     r=[self.tmpb[t2]], w=[self.tmpb[t2]])
                                t3 = self.newtmp()
                                P.op("dve", lambda: nc.vector.tensor_tensor(out=self.tmp[t3][:, :], in0=self.ps[bk][:, :],
                                                                            in1=self.tmp[t2][:, :], op=ALU.mult),
                                     r=[self.psb[bk], self.tmpb[t2]], w=[self.tmpb[t3]])
                                gcol = gq[:, 0:1] if isq else gq[:, 1:2]
                                if r == 0:
                                    if isq:
                                        P.op("act", lambda: nc.scalar.activation(
                                            out=qT[:, hd, tt * 512:(tt + 1) * 512], in_=self.tmp[t3][:, :],
                                            func=AF.Copy, scale=gcol), r=[self.tmpb[t3], gqb], w=[qTb[hd, tt]])
                                    else:
                                        t4 = self.newtmp()
                                        P.op("act", lambda: nc.scalar.activation(
                                            out=self.tmp[t4][:, :], in_=self.tmp[t3][:, :], func=AF.Copy, scale=gcol),
                                            r=[self.tmpb[t3], gqb], w=[self.tmpb[t4]])
                                        P.op("dve", lambda: nc.vector.tensor_copy(out=kT[:, hd, 0:512], in_=self.tmp[t4][:, :]),
                                             r=[self.tmpb[t4]], w=[kTb[hd]])
                                        tbk = self.bank()
                                        for blk in range(4):
                                            P.op("pe", lambda: nc.tensor.transpose(
                                                self.ps[tbk][:, blk * 128:(blk + 1) * 128],
                                                self.tmp[t4][:, blk * 128:(blk + 1) * 128], ident),
                                                r=[self.tmpb[t4], self.cstb], w=[self.psb[tbk]], inc=(blk == 3))
                                        P.op("dve", lambda: nc.vector.tensor_copy(
                                            out=nkst[:, :, hd * 128:(hd + 1) * 128],
                                            in_=self.ps[tbk][:, :].rearrange("p (b d) -> p b d", b=4)),
                                            r=[self.psb[tbk]], w=[nkstb])
                                        if hd == 1:
                                            for blk in range(4):
                                                P.dma("sp", O["new_k"][blk * 128:(blk + 1) * 128, :], nkst[:, blk, :], None,
                                                      r=[nkstb[blk]])
                                else:
                                    t4 = self.newtmp()
                                    P.op("act", lambda: nc.scalar.activation(
                                        out=self.tmp[t4][:, :], in_=self.tmp[t3][:, :], func=AF.Copy, scale=gcol),
                                        r=[self.tmpb[t3], gqb], w=[self.tmpb[t4]])
                                    rbk = self.bank()
                                    P.op("pe", lambda: nc.tensor.matmul(self.ps[rbk][:, :],
                                                                        lhsT=self.cst[:, C_RT * 128:(C_RT + 1) * 128],
                                                                        rhs=self.tmp[t4][:, :], start=True, stop=True),
                                         r=[self.tmpb[t4], self.cstb], w=[self.psb[rbk]])
                                    t5 = self.newtmp()
                                    P.op("dve", lambda: nc.vector.tensor_tensor(
                                        out=self.tmp[t5][:, :], in0=self.tmp[t4][:, :], in1=rope[:, 0, tt * 512:(tt + 1) * 512],
                                        op=ALU.mult), r=[self.tmpb[t4], ropeb], w=[self.tmpb[t5]])
                                    t6 = self.newtmp()
                                    P.op("dve", lambda: nc.vector.tensor_tensor(
                                        out=self.tmp[t6][:, :], in0=self.ps[rbk][:, :], in1=rope[:, 1, tt * 512:(tt + 1) * 512],
                                        op=ALU.mult), r=[self.psb[rbk], ropeb], w=[self.tmpb[t6]])
                                    if isq:
                                        dst, dstb = qT[:, hd, tt * 512:(tt + 1) * 512], qTb[hd, tt]
                                    else:
                                        dst, dstb = kT[:, hd, PAST + tt * 512:PAST + (tt + 1) * 512], kTb[hd]
                                    P.op("dve", lambda: nc.vector.tensor_tensor(out=dst, in0=self.tmp[t5][:, :],
                                                                                in1=self.tmp[t6][:, :], op=ALU.add),
                                         r=[self.tmpb[t5], self.tmpb[t6]], w=[dstb])
                        if cc < 8:
                            gi = cc // 2
                            wv = (2, 4, 8, 16)[gi]
                            hw = wv // 2
                            W = L + 16
                            cur = xpp
                            step = 1
                            bufs_ = [sA, sB]
                            bi = 0
                            while step < wv:
                                nxt = bufs_[bi]
                                bi ^= 1
                                n_valid = W - 2 * step + 1 if step > 1 else W - 1
                                n_valid = W - (2 * step - 1)
                                P.op("dve", lambda: nc.vector.tensor_tensor(
                                    out=nxt[:, :, 0:n_valid], in0=cur[:, :, 0:n_valid], in1=cur[:, :, step:step + n_valid],
                                    op=ALU.add), r=[xppb, sb_], w=[sb_])
                                cur = nxt
                                step *= 2
                            off = 8 - hw
                            other = bufs_[bi]
                            P.op("dve", lambda: nc.vector.tensor_tensor(
                                out=other[:, :, 0:L], in0=cur[:, :, off:off + L],
                                in1=pcnt[:, gi, :].unsqueeze(1).broadcast_to([128, nseq, L]) if nseq > 1 else pcnt[:, gi:gi + 1, :],
                                op=ALU.mult), r=[sb_, pcntb], w=[sb_])
                            P.op("dve", lambda: nc.vector.tensor_tensor(
                                out=dbf[:, cc % 2, :].rearrange("p (s t) -> p s t", s=nseq), in0=other[:, :, 0:L],
                                in1=xpp[:, :, 8:8 + L], op=ALU.subtract), r=[sb_, xppb], w=[dbfb[cc % 2]])
                            if cc % 2 == 1:
                                for do in range(2):
                                    for tt in range(NT):
                                        bk = self.bank()
                                        for c in range(2):
                                            P.op("pe", lambda: nc.tensor.matmul(
                                                self.ps[bk][:, :], lhsT=pw[:, gi, c, do * 128:(do + 1) * 128],
                                                rhs=dbf[:, c, tt * 512:(tt + 1) * 512], start=(c == 0), stop=(c == 1)),
                                                r=[pwb, dbfb[c]], w=[self.psb[bk]], inc=(c == 1))
                                        oc = gi * 2 + do
                                        P.op("act", lambda: nc.scalar.activation(
                                            out=cat[:, oc, tt * 512:(tt + 1) * 512], in_=self.ps[bk][:, :], func=AF.Copy,
                                            scale=self.vecT[:, oc:oc + 1]), r=[self.psb[bk], self.smallb], w=[catb[oc, tt]])
                self.fence()
            if self.cfg.get("even_stop", 9) < 2:
                return
            with ExitStack() as e2:
                pT = [self.sb(e2, "pT%d%s" % (i, tag), [128, 512], BF16) for i in range(3)]
                pTb = bufs(3)
                pi = 0
                NQ = L if r == 0 else 512
                for sq in range(nseq):
                    for qt in range(L // NQ):
                        q0 = sq * L + qt * NQ
                        for hd in range(8):
                            kh = hd // 4
                            ob = self.bank_hold()
                            db = self.bank_hold()

                            def s_mm(kcx):
                                bk = self.bank()
                                P.op("pe", lambda: nc.tensor.matmul(
                                    self.ps[bk][:, 0:NQ], lhsT=kT[:, kh, sq * NK + kcx * 128:sq * NK + (kcx + 1) * 128],
                                    rhs=qT[:, hd, q0:q0 + NQ], start=True, stop=True),
                                    r=[kTb[kh], qTb[hd, q0 // 512]], w=[self.psb[bk]])
                                return bk
                            sbk_next = s_mm(0)
                            for kcx in range(NKC):
                                sbk = sbk_next
                                k_ = pi % 3
                                pi += 1
                                P.op("act", lambda: nc.scalar.activation(out=pT[k_][:, 0:NQ], in_=self.ps[sbk][:, 0:NQ],
                                                                         func=AF.Exp), r=[self.psb[sbk]], w=[pTb[k_]])
                                if kcx + 1 < NKC:
                                    sbk_next = s_mm(kcx + 1)
                                P.op("pe", lambda: nc.tensor.matmul(
                                    self.ps[ob][:, 0:NQ], lhsT=vtok[:, sq * NKC + kcx, kh * 128:(kh + 1) * 128],
                                    rhs=pT[k_][:, 0:NQ], start=(kcx == 0), stop=(kcx == NKC - 1)),
                                    r=[vtokb, pTb[k_]], w=[self.psb[ob]], inc=False)
                                P.op("pe", lambda: nc.tensor.matmul(
                                    self.ps[db][:, 0:NQ], lhsT=self.ones_b[:, :], rhs=pT[k_][:, 0:NQ],
                                    start=(kcx == 0), stop=(kcx == NKC - 1)),
                                    r=[self.onesb, pTb[k_]], w=[self.psb[db], self.psb[ob]])
                            t1 = self.newtmp()
                            P.op("dve", lambda: nc.vector.reciprocal(out=self.tmp[t1][:, 0:NQ], in_=self.ps[db][:, 0:NQ]),
                                 r=[self.psb[db]], w=[self.tmpb[t1]])
                            P.op("dve", lambda: nc.vector.tensor_tensor(
                                out=cat[:, 8 + hd, q0:q0 + NQ], in0=self.ps[ob][:, 0:NQ], in1=self.tmp[t1][:, 0:NQ],
                                op=ALU.mult), r=[self.psb[ob], self.tmpb[t1]], w=[catb[8 + hd, q0 // 512]])
                            self.bank_release(ob)
                            self.bank_release(db)
                self.fence()
            if self.cfg.get("even_stop", 9) < 3:
                return
            self.mix_out_residual(I["mix_w_out"][0], KC, lambda tt: (lambda kc: cat[:, kc, tt * 512:(tt + 1) * 512]),
                                  lambda tt: (lambda kc: catb[kc, tt]), T, l, r, uy, uyb, win, winb, wsem, tag)
            self.fence()

    def odd_mixer(self, T, r, uy, uyb, win, winb, wsem):
        nc, P, I, O = self.nc, self.P, self.I, self.O
        l, s = 1, 1
        nseq = 2 if r == 0 else 1
        L = T // nseq
        NT = T // 512
        NTB = T // 128
        NCH = L // 128
        uyh = uy[:, :].bitcast(BF16)
        w_in = I["ssm_w_in"][0]
        ident = self.cst[:, C_ID * 128:(C_ID + 1) * 128]
        cU = self.cst[:, C_U * 128:(C_U + 1) * 128]
        cLi = self.cst[:, C_LI * 128:(C_LI + 1) * 128]
        cLs = self.cst[:, C_LS * 128:(C_LS + 1) * 128]
        cUs = self.cst[:, C_US * 128:(C_US + 1) * 128]
        tag = "o%d" % r
        ygscr = self.ygscr
        with ExitStack() as es:
            ub = bufs(KC, NT)
            with ExitStack() as e1:
                h = self.sb(e1, "h_" + tag, [128, KC, T], F32)
                hb = bufs(KC, NT)
                self.h_load(h, hb, T, None)
                for tt in range(NT):
                    self.modulate(h, hb, tt, l, s, r, lambda c: uyh[:, c * T + tt * 512:c * T + (tt + 1) * 512],
                                  lambda c: ub[c, tt])
                self.fence()
            dt = self.sb(es, "dt" + tag, [128, NTB, 128], F32)
            dtA = self.sb(es, "dtA" + tag, [128, NTB, 128], F32)
            ea = self.sb(es, "ea" + tag, [128, NTB, 128], F32)
            dtd = self.sb(es, "dtd" + tag, [128, NTB, 128], F32)
            etot = self.sb(es, "etot" + tag, [128, NTB, 128], F32)
            dtb_ = Buf()
            ssq = self.sb(es, "ssq" + tag, [128, NTB, 8], F32)
            ssqb = Buf()
            rrow = self.sb(es, "rrow" + tag, [128, T], F32)
            rrowb = Buf()

            def u_blk(kc, tb):
                return uyh[:, kc * T + tb * 128:kc * T + (tb + 1) * 128]

            srcw = w_in[:, 10240:10368].rearrange("(k p) n -> p k n", p=128)
            P.dma("pool", win[0][:, :, 0:128], srcw, wsem[0], w=[winb[0]])
            for tb in range(NTB):
                bk = self.bank()
                for kc in range(KC):
                    P.op("pe", lambda: nc.tensor.matmul(self.ps[bk][:, 0:128], lhsT=u_blk(kc, tb), rhs=win[0][:, kc, 0:128],
                                                        start=(kc == 0), stop=(kc == KC - 1)),
                         r=[winb[0], ub[kc, tb // 4]], w=[self.psb[bk]], inc=(kc == KC - 1))
                t1 = self.newtmp()
                xb = self.tmp[t1][:, 0:128]
                P.op("dve", lambda: nc.vector.tensor_tensor(out=xb, in0=self.ps[bk][:, 0:128], in1=self.dtb[:, :], op=ALU.add),
                     r=[self.psb[bk], self.smallb], w=[self.tmpb[t1]])
                t2 = self.newtmp()
                ab = self.tmp[t2][:, 0:128]
                P.op("act", lambda: nc.scalar.activation(out=ab, in_=xb, func=AF.Abs),
                     r=[self.tmpb[t1]], w=[self.tmpb[t2]])
                P.op("act", lambda: nc.scalar.activation(out=ab, in_=ab, func=AF.Exp, scale=-1.0),
                     r=[self.tmpb[t2]], w=[self.tmpb[t2]])
                P.op("act", lambda: nc.scalar.activation(out=ab, in_=ab, func=AF.Ln, bias=self.onec[:, 0:1]),
                     r=[self.tmpb[t2], self.smallb], w=[self.tmpb[t2]])
                P.op("dve", lambda: nc.vector.scalar_tensor_tensor(out=dt[:, tb, :], in0=xb, scalar=0.0, in1=ab,
                                                                   op0=ALU.max, op1=ALU.add),
                     r=[self.tmpb[t1], self.tmpb[t2]], w=[dtb_])
                P.op("dve", lambda: nc.vector.tensor_tensor(out=dtA[:, tb, :], in0=dt[:, tb, :], in1=self.Aneg[:, :], op=ALU.mult),
                     r=[dtb_, self.smallb], w=[dtb_])
                bk2 = self.bank()
                P.op("pe", lambda: nc.tensor.matmul(self.ps[bk2][:, 0:64], lhsT=cU, rhs=dtA[:, tb, 0:64], start=True, stop=True),
                     r=[dtb_, self.cstb], w=[self.psb[bk2]], inc=False)
                P.op("pe", lambda: nc.tensor.matmul(self.ps[bk2][:, 64:128], lhsT=cLi, rhs=dtA[:, tb, 64:128], start=True, stop=True),
                     r=[dtb_, self.cstb], w=[self.psb[bk2]], inc=False)
                P.op("pe", lambda: nc.tensor.matmul(self.ps[bk2][:, 128:256], lhsT=self.ones_f[:, :], rhs=dtA[:, tb, :],
                                                    start=True, stop=True),
                     r=[dtb_, self.onesb], w=[self.psb[bk2]])
                P.op("act", lambda: nc.scalar.activation(out=ea[:, tb, :], in_=self.ps[bk2][:, 0:128], func=AF.Exp),
                     r=[self.psb[bk2]], w=[dtb_])
                P.op("act", lambda: nc.scalar.activation(out=etot[:, tb, :], in_=self.ps[bk2][:, 128:256], func=AF.Exp),
                     r=[self.psb[bk2]], w=[dtb_])
                t3 = self.newtmp()
                ac = self.tmp[t3][:, 0:128]
                P.op("act", lambda: nc.scalar.copy(out=ac, in_=self.ps[bk2][:, 0:128]), r=[self.psb[bk2]], w=[self.tmpb[t3]])
                P.op("dve", lambda: nc.vector.tensor_tensor(out=ac, in0=self.ps[bk2][:, 128:256], in1=ac, op=ALU.subtract),
                     r=[self.psb[bk2], self.tmpb[t3]], w=[self.tmpb[t3]])
                P.op("act", lambda: nc.scalar.activation(out=ac, in_=ac, func=AF.Exp), r=[self.tmpb[t3]], w=[self.tmpb[t3]])
                P.op("dve", lambda: nc.vector.tensor_tensor(out=dtd[:, tb, :], in0=ac, in1=dt[:, tb, :], op=ALU.mult),
                     r=[self.tmpb[t3], dtb_], w=[dtb_])
            with ExitStack() as e2:
                sb = self.sb
                x_tok = sb(e2, "xtok" + tag, [128, NTB, 512], F32)
                xtokb = Buf()
                zs = sb(e2, "zs" + tag, [128, NTB, 512], BF16)
                zsb = Buf()
                yacc = sb(e2, "yacc" + tag, [128, NTB, 512], F32)
                yaccb = bufs(NTB)
                BT = sb(e2, "BT" + tag, [128, T], BF16)
                CT = sb(e2, "CT" + tag, [128, T], BF16)
                bcb = bufs(2)
                Btok = sb(e2, "Btok" + tag, [128, NTB, 128], BF16)
                Btokb = Buf()
                xc = sb(e2, "xc" + tag, [128, nseq, L + 2], F32)
                xcb = Buf()
                cva = sb(e2, "cva" + tag, [128, nseq, L], F32)
                cvb = sb(e2, "cvb" + tag, [128, nseq, L], F32)
                cvab = Buf()
                cvbb = Buf()
                S = [sb(e2, "S%d%s" % (d, tag), [128, 512], F32) for d in range(2)]
                Sb16 = [sb(e2, "Sb%d%s" % (d, tag), [128, 512], BF16) for d in range(2)]
                Sbuf = bufs(2)
                rb = [sb(e2, "rb%d%s" % (i, tag), [128, 8, 128], F32) for i in range(2)]
                rbb = bufs(2)
                dec = [sb(e2, "dec%d%s" % (i, tag), [128, 4, 128], F32) for i in range(2)]
                decb = bufs(2)
                wm = [sb(e2, "wm%d%s" % (i, tag), [128, 8, 128], BF16) for i in range(2)]
                wmb = bufs(2)
                cbm = [sb(e2, "cbm%d%s" % (i, tag), [128, 128], F32) for i in range(2)]
                cbmb = bufs(2)
                xdt = [sb(e2, "xdt%d%s" % (i, tag), [128, 512], BF16) for i in range(2)]
                xdtb = bufs(2)
                xdd = [sb(e2, "xdd%d%s" % (i, tag), [128, 512], BF16) for i in range(2)]
                xddb = bufs(2)
                ygst = sb(e2, "ygst" + tag, [128, 4, T], BF16)
                ygstb = Buf()
                stst = sb(e2, "stst" + tag, [128, 4, 128], F32)
                ststb = Buf()
                P.op("dve", lambda: nc.vector.memset(xc[:, :, :], 0.0), w=[xcb])
                it = 0
                nload = [0]

                def wload(col0, ncol, dcol=0, k=None):
                    if k is None:
                        k = nload[0] % 2
                        nload[0] += 1
                    srcw_ = w_in[:, col0:col0 + ncol].rearrange("(k p) n -> p k n", p=128)
                    P.dma("pool", win[k][:, :, dcol:dcol + ncol], srcw_, wsem[k], w=[winb[k]])
                    return k

                def fm_chunk(k, sub, cidx, kind, dst, dstb):
                    for tt in range(NT):
                        bk = self.bank()
                        for kc in range(KC):
                            P.op("pe", lambda: nc.tensor.matmul(
                                self.ps[bk][:, :], lhsT=win[k][:, kc, sub * 128:(sub + 1) * 128],
                                rhs=uyh[:, kc * T + tt * 512:kc * T + (tt + 1) * 512], start=(kc == 0), stop=(kc == KC - 1)),
                                r=[winb[k], ub[kc, tt]], w=[self.psb[bk]], inc=(kc == KC - 1))
                        if r == 0:
                            P.op("act", lambda: nc.scalar.copy(out=xc[:, :, 1:1 + L],
                                                               in_=self.ps[bk][:, :].rearrange("p (s t) -> p s t", s=2)),
                                 r=[self.psb[bk]], w=[xcb])
                        else:
                            P.op("act", lambda: nc.scalar.copy(out=xc[:, 0, 1 + tt * 512:1 + (tt + 1) * 512], in_=self.ps[bk][:, :]),
                                 r=[self.psb[bk]], w=[xcb])
                    w0 = self.convw[:, cidx, 0:1]
                    w1 = self.convw[:, cidx, 1:2]
                    w2 = self.convw[:, cidx, 2:3]
                    bcol = self.vecT[:, 10 + cidx:11 + cidx]
                    P.op("dve", lambda: nc.vector.tensor_scalar(out=cva[:, :, :], in0=xc[:, :, 0:L], scalar1=w0, scalar2=None,
                                                                op0=ALU.mult), r=[xcb, self.smallb], w=[cvab])
                    P.op("dve", lambda: nc.vector.scalar_tensor_tensor(out=cvb[:, :, :], in0=xc[:, :, 1:L + 1], scalar=w1,
                                                                       in1=cva[:, :, :], op0=ALU.mult, op1=ALU.add),
                         r=[xcb, cvab, self.smallb], w=[cvbb])
                    P.op("dve", lambda: nc.vector.scalar_tensor_tensor(out=cva[:, :, :], in0=xc[:, :, 2:L + 2], scalar=w2,
                                                                       in1=cvb[:, :, :], op0=ALU.mult, op1=ALU.add),
                         r=[xcb, cvbb, self.smallb], w=[cvab])
                    cflat = cva[:, :, :].rearrange("p s t -> p (s t)")
                    vflat = cvb[:, :, :].rearrange("p s t -> p (s t)")
                    P.op("act", lambda: nc.scalar.activation(out=vflat, in_=cflat, func=AF.Silu, bias=bcol),
                         r=[cvab, self.smallb], w=[cvbb])
                    if kind in ("B", "C"):
                        P.op("act", lambda: nc.scalar.copy(out=dst[:, :], in_=vflat), r=[cvbb], w=[dstb])
                    if kind in ("x", "B"):
                        for t4 in range(NTB // 4):
                            bk = self.bank()
                            for q in range(4):
                                tb = t4 * 4 + q
                                P.op("pe", lambda: nc.tensor.transpose(self.ps[bk][:, q * 128:(q + 1) * 128],
                                                                       vflat[:, tb * 128:(tb + 1) * 128], ident),
                                     r=[cvbb, self.cstb], w=[self.psb[bk]], inc=(q == 3))
                            pv = self.ps[bk][:, :].rearrange("p (q f) -> p q f", q=4)
                            if kind == "x":
                                P.op("dve", lambda: nc.vector.tensor_copy(out=x_tok[:, t4 * 4:(t4 + 1) * 4, dst * 128:(dst + 1) * 128],
                                                                          in_=pv), r=[self.psb[bk]], w=[xtokb])
                            else:
                                P.op("dve", lambda: nc.vector.tensor_copy(out=Btok[:, t4 * 4:(t4 + 1) * 4, :], in_=pv),
                                     r=[self.psb[bk]], w=[Btokb])

                for g in range(NG):
                    for half in range(2):
                        k = wload(g * 512 + half * 256, 256)
                        for tb in range(NTB):
                            bk = self.bank()
                            for kc in range(KC):
                                P.op("pe", lambda: nc.tensor.matmul(self.ps[bk][:, 0:256], lhsT=u_blk(kc, tb), rhs=win[k][:, kc, :],
                                                                    start=(kc == 0), stop=(kc == KC - 1)),
                                     r=[winb[k], ub[kc, tb // 4]], w=[self.psb[bk]], inc=(kc == KC - 1))
                            P.op("act", lambda: nc.scalar.activation(out=zs[:, tb, half * 256:(half + 1) * 256],
                                                                     in_=self.ps[bk][:, 0:256], func=AF.Silu),
                                 r=[self.psb[bk]], w=[zsb])
                    for half in range(2):
                        k = wload(4096 + g * 512 + half * 256, 256)
                        for sub in range(2):
                            ch = half * 2 + sub
                            fm_chunk(k, sub, g * 4 + ch, "x", ch, None)
                    k = wload(8192 + g * 128, 128, 0)
                    wload(9216 + g * 128, 128, 128, k=k)
                    fm_chunk(k, 0, 32 + g, "B", BT, bcb[0])
                    fm_chunk(k, 1, 40 + g, "C", CT, bcb[1])
                    for tb in range(NTB):
                        P.op("dve", lambda: nc.vector.tensor_tensor(
                            out=yacc[:, tb, :].rearrange("p (h q) -> p h q", h=8),
                            in0=x_tok[:, tb, :].rearrange("p (h q) -> p h q", h=8),
                            in1=self.Dsum[:, g * 8:(g + 1) * 8].unsqueeze(2).broadcast_to([128, 8, 64]), op=ALU.mult),
                            r=[xtokb, self.smallb], w=[yaccb[tb]])
                    for sq in range(nseq):
                        for d in range(2):
                            Mk = cU if d == 0 else cLi
                            Ml = cLs if d == 0 else cUs
                            hcol = d * 64 + g * 8
                            if r == 1:
                                P.dma("sp", stst[:, :, :], I["state"][d, g * 512:(g + 1) * 512, :].rearrange("(q p) n -> p q n", p=128),
                                      None, w=[ststb])
                                bk = self.bank()
                                for q in range(4):
                                    P.op("pe", lambda: nc.tensor.transpose(self.ps[bk][:, q * 128:(q + 1) * 128], stst[:, q, :], ident),
                                         r=[ststb, self.cstb], w=[self.psb[bk]], inc=(q == 3))
                                P.op("dve", lambda: nc.vector.tensor_copy(out=S[d][:, :], in_=self.ps[bk][:, :]),
                                     r=[self.psb[bk]], w=[Sbuf[d]])
                                P.op("act", lambda: nc.scalar.copy(out=Sb16[d][:, :], in_=self.ps[bk][:, :]),
                                     r=[self.psb[bk]], w=[Sbuf[d]])
                            order = range(NCH) if d == 0 else range(NCH - 1, -1, -1)
                            for ci, c in enumerate(order):
                                tb = sq * NCH + c
                                first = (ci == 0 and r == 0)
                                tok = slice(tb * 128, (tb + 1) * 128)
                                i2 = it % 2
                                it += 1
                                bk = self.bank()
                                P.op("pe", lambda: nc.tensor.matmul(self.ps[bk][:, 0:128], lhsT=BT[:, tok], rhs=CT[:, tok],
                                                                    start=True, stop=True), r=[bcb], w=[self.psb[bk]])
                                P.op("dve", lambda: nc.vector.tensor_tensor(out=cbm[i2][:, :], in0=self.ps[bk][:, 0:128], in1=Mk,
                                                                            op=ALU.mult), r=[self.psb[bk], self.cstb], w=[cbmb[i2]])
                                P.op("dve", lambda: nc.vector.tensor_tensor(
                                    out=rb[i2][:, :, :], in0=Mk.unsqueeze(1).broadcast_to([128, 8, 128]),
                                    in1=dtA[:, tb, hcol:hcol + 8].unsqueeze(2).broadcast_to([128, 8, 128]), op=ALU.mult),
                                    r=[self.cstb, dtb_], w=[rbb[i2]])
                                for hf in range(2):
                                    bk = self.bank()
                                    P.op("pe", lambda: nc.tensor.matmul(
                                        self.ps[bk][:, :], lhsT=Ml, rhs=rb[i2][:, hf * 4:(hf + 1) * 4, :].rearrange("p h i -> p (h i)"),
                                        start=True, stop=True), r=[rbb[i2], self.cstb], w=[self.psb[bk]])
                                    P.op("act", lambda: nc.scalar.activation(out=dec[hf][:, :, :].rearrange("p h i -> p (h i)"),
                                                                             in_=self.ps[bk][:, :], func=AF.Exp),
                                         r=[self.psb[bk]], w=[decb[hf]])
                                    P.op("dve", lambda: nc.vector.tensor_tensor(
                                        out=wm[i2][:, hf * 4:(hf + 1) * 4, :], in0=dec[hf][:, :, :],
                                        in1=cbm[i2][:, :].unsqueeze(1).broadcast_to([128, 4, 128]), op=ALU.mult),
                                        r=[decb[hf], cbmb[i2]], w=[wmb[i2]])
                                xv = x_tok[:, tb, :].rearrange("p (h q) -> p h q", h=8)
                                P.op("dve", lambda: nc.vector.tensor_tensor(
                                    out=xdt[i2][:, :].rearrange("p (h q) -> p h q", h=8), in0=xv,
                                    in1=dt[:, tb, hcol:hcol + 8].unsqueeze(2).broadcast_to([128, 8, 64]), op=ALU.mult),
                                    r=[xtokb, dtb_], w=[xdtb[i2]])
                                P.op("dve", lambda: nc.vector.tensor_tensor(
                                    out=xdd[i2][:, :].rearrange("p (h q) -> p h q", h=8), in0=xv,
                                    in1=dtd[:, tb, hcol:hcol + 8].unsqueeze(2).broadcast_to([128, 8, 64]), op=ALU.mult),
                                    r=[xtokb, dtb_], w=[xddb[i2]])
                                yi = self.bank()
                                for hh in range(8):
                                    P.op("pe", lambda: nc.tensor.matmul(self.ps[yi][:, hh * 64:(hh + 1) * 64], lhsT=wm[i2][:, hh, :],
                                                                        rhs=xdt[i2][:, hh * 64:(hh + 1) * 64], start=True, stop=True),
                                         r=[wmb[i2], xdtb[i2]], w=[self.psb[yi]], inc=(hh == 7))
                                if not first:
                                    ysb = self.bank()
                                    P.op("pe", lambda: nc.tensor.matmul(self.ps[ysb][:, :], lhsT=CT[:, tok], rhs=Sb16[d][:, :],
                                                                        start=True, stop=True), r=[bcb[1], Sbuf[d]], w=[self.psb[ysb]])
                                    t1 = self.newtmp()
                                    P.op("dve", lambda: nc.vector.tensor_tensor(
                                        out=self.tmp[t1][:, :].rearrange("p (h q) -> p h q", h=8),
                                        in0=self.ps[ysb][:, :].rearrange("p (h q) -> p h q", h=8),
                                        in1=ea[:, tb, hcol:hcol + 8].unsqueeze(2).broadcast_to([128, 8, 64]), op=ALU.mult),
                                        r=[self.psb[ysb], dtb_], w=[self.tmpb[t1]])
                                    P.op("dve", lambda: nc.vector.tensor_tensor(out=self.tmp[t1][:, :], in0=self.tmp[t1][:, :],
                                                                                in1=self.ps[yi][:, :], op=ALU.add),
                                         r=[self.tmpb[t1], self.psb[yi]], w=[self.tmpb[t1]])
                                    P.op("dve", lambda: nc.vector.tensor_tensor(out=yacc[:, tb, :], in0=yacc[:, tb, :],
                                                                                in1=self.tmp[t1][:, :], op=ALU.add),
                                         r=[self.tmpb[t1], yaccb[tb]], w=[yaccb[tb]])
                                else:
                                    P.op("dve", lambda: nc.vector.tensor_tensor(out=yacc[:, tb, :], in0=yacc[:, tb, :],
                                                                                in1=self.ps[yi][:, :], op=ALU.add),
                                         r=[self.psb[yi], yaccb[tb]], w=[yaccb[tb]])
                                su = self.bank()
                                P.op("pe", lambda: nc.tensor.matmul(self.ps[su][:, :], lhsT=Btok[:, tb, :], rhs=xdd[i2][:, :],
                                                                    start=True, stop=True), r=[Btokb, xddb[i2]], w=[self.psb[su]])
                                if first:
                                    P.op("dve", lambda: nc.vector.tensor_copy(out=S[d][:, :], in_=self.ps[su][:, :]),
                                         r=[self.psb[su]], w=[Sbuf[d]])
                                else:
                                    P.op("dve", lambda: nc.vector.tensor_tensor(
                                        out=S[d][:, :].rearrange("p (h q) -> p h q", h=8),
                                        in0=S[d][:, :].rearrange("p (h q) -> p h q", h=8),
                                        in1=etot[:, tb, hcol:hcol + 8].unsqueeze(2).broadcast_to([128, 8, 64]), op=ALU.mult),
                                        r=[Sbuf[d], dtb_], w=[Sbuf[d]])
                                    P.op("dve", lambda: nc.vector.tensor_tensor(out=S[d][:, :], in0=S[d][:, :], in1=self.ps[su][:, :],
                                                                                op=ALU.add), r=[Sbuf[d], self.psb[su]], w=[Sbuf[d]])
                                P.op("act", lambda: nc.scalar.copy(out=Sb16[d][:, :], in_=S[d][:, :]), r=[Sbuf[d]], w=[Sbuf[d]])
                            if r == 0:
                                bk = self.bank()
                                for q in range(4):
                                    P.op("pe", lambda: nc.tensor.transpose(self.ps[bk][:, q * 128:(q + 1) * 128],
                                                                           S[d][:, q * 128:(q + 1) * 128], ident),
                                         r=[Sbuf[d], self.cstb], w=[self.psb[bk]], inc=(q == 3))
                                P.op("dve", lambda: nc.vector.tensor_copy(out=stst[:, :, :],
                                                                          in_=self.ps[bk][:, :].rearrange("p (q n) -> p q n", q=4)),
                                     r=[self.psb[bk]], w=[ststb])
                                P.dma("sp", O["new_ssm"][sq, d, g * 512:(g + 1) * 512, :].rearrange("(q p) n -> p q n", p=128),
                                      stst[:, :, :], None, r=[ststb])
                    for tb in range(NTB):
                        P.op("dve", lambda: nc.vector.tensor_tensor(out=yacc[:, tb, :], in0=yacc[:, tb, :], in1=zs[:, tb, :],
                                                                    op=ALU.mult), r=[yaccb[tb], zsb], w=[yaccb[tb]])
                        t1 = self.newtmp()
                        P.op("act", lambda: nc.scalar.activation(out=self.tmp[t1][:, :], in_=yacc[:, tb, :], func=AF.Square,
                                                                 accum_out=ssq[:, tb, g:g + 1]),
                             r=[yaccb[tb]], w=[self.tmpb[t1], ssqb])
                    for ch in range(4):
                        for t4 in range(NTB // 4):
                            bk = self.bank()
                            for q in range(4):
                                tb = t4 * 4 + q
                                P.op("pe", lambda: nc.tensor.transpose(self.ps[bk][:, q * 128:(q + 1) * 128],
                                                                       yacc[:, tb, ch * 128:(ch + 1) * 128], ident),
                                     r=[yaccb[tb], self.cstb], w=[self.psb[bk]], inc=(q == 3))
                            P.op("act", lambda: nc.scalar.activation(out=ygst[:, ch, t4 * 512:(t4 + 1) * 512], in_=self.ps[bk][:, :],
                                                                     func=AF.Copy, scale=self.vecT[:, 58 + g * 4 + ch:59 + g * 4 + ch]),
                                 r=[self.psb[bk], self.smallb], w=[ygstb])
                    P.dma("sp", ygscr[:, g * 4:(g + 1) * 4, 0:T], ygst[:, :, :], None, r=[ygstb])
                self.fence()
            t1 = self.newtmp()
            rt = self.tmp[t1][:, 0:NTB]
            P.op("dve", lambda: nc.vector.tensor_reduce(out=rt, in_=ssq[:, :, :], axis=AX.X, op=ALU.add),
                 r=[ssqb], w=[self.tmpb[t1]])
            P.op("act", lambda: nc.scalar.activation(out=rt, in_=rt, func=AF.Sqrt, scale=1.0 / D_INNER, bias=self.epsc[:, 0:1]),
                 r=[self.tmpb[t1], self.smallb], w=[self.tmpb[t1]])
            P.op("dve", lambda: nc.vector.reciprocal(out=rt, in_=rt), r=[self.tmpb[t1]], w=[self.tmpb[t1]])
            for t4 in range(NTB // 4):
                t2 = self.newtmp()
                for q in range(4):
                    tb = t4 * 4 + q
                    P.op("dve", lambda: nc.vector.tensor_scalar(out=self.tmp[t2][:, q * 128:(q + 1) * 128], in0=ident,
                                                                scalar1=rt[:, tb:tb + 1], scalar2=None, op0=ALU.mult),
                         r=[self.tmpb[t1], self.cstb], w=[self.tmpb[t2]])
                bk = self.bank()
                P.op("pe", lambda: nc.tensor.matmul(self.ps[bk][:, :], lhsT=self.ones_f[:, :], rhs=self.tmp[t2][:, :],
                                                    start=True, stop=True), r=[self.tmpb[t2], self.onesb], w=[self.psb[bk]])
                P.op("act", lambda: nc.scalar.copy(out=rrow[:, t4 * 512:(t4 + 1) * 512], in_=self.ps[bk][:, :]),
                     r=[self.psb[bk]], w=[rrowb])
            self.fence()
            with ExitStack() as e3:
                ygt = self.sb(e3, "ygt" + tag, [128, 32, 512], BF16)
                ygtb_holder = [None]

                def rhs_of_tile(tt):
                    b_ = Buf()
                    ygtb_holder[0] = b_
                    P.dma("sp", ygt[:, :, :], ygscr[:, :, tt * 512:(tt + 1) * 512], None, w=[b_])
                    return lambda kc: ygt[:, kc, :]

                def rhsb_of_tile(tt):
                    return lambda kc: ygtb_holder[0]

                self.mix_out_residual(I["ssm_w_out"][0], 32, rhs_of_tile, rhsb_of_tile, T, l, r, uy, uyb, win, winb, wsem, tag,
                                      post_scale=lambda tt: (rrow[:, tt * 512:(tt + 1) * 512], rrowb))
                self.fence()


_CONSTS = None


def make_in_maps(inp):
    global _CONSTS
    if _CONSTS is None:
        _CONSTS = _host_consts()
    cst, rope, pc_ctx, pc_smp = _CONSTS
    f = lambda a: np.ascontiguousarray(np.asarray(a, dtype=np.float32))
    shared = {k: f(inp[k]) for k in ("ada_w", "ada_b", "norm_g", "ffn_w_in", "ffn_w_out", "mix_w_in", "pool_w",
                                     "pool_scale", "qk_norm_g", "mix_w_out", "ssm_w_in", "ssm_conv_w", "ssm_conv_b",
                                     "ssm_dt_bias", "ssm_A_log", "ssm_D", "ssm_norm_g", "ssm_w_out")}
    shared.update(cst=cst, rope=rope, pcnt_ctx=pc_ctx, pcnt_smp=pc_smp)
    xp = f(inp["x_prompt"])
    xs = f(inp["x_sample"])
    ck = f(inp["cache_k"])
    cv = f(inp["cache_v"])
    st = f(inp["state_ssm"])
    c = f(inp["c"])
    cc = f(inp["c_ctx"])
    maps = []
    for i in range(N_CORES):
        m = dict(shared)
        m["x_ctx"] = xp[2 * i:2 * i + 2].reshape(2 * L_CTX, D)
        m["x_smp"] = xs[i]
        m["cache_k"] = ck[i, 0].reshape(PAST, 256)
        m["cache_v"] = cv[i, 0].reshape(PAST, 256)
        m["state"] = st[i, 0].reshape(2, NH * HP, NS)
        m["cond"] = np.stack([cc, c[i]], axis=0)
        maps.append(m)
    return maps


def run(inp, cfg=None, n_cores=N_CORES):
    cfg = cfg or {}
    b = Builder(cfg)
    nc = b.build()
    maps = make_in_maps(inp)[:n_cores]
    res = run_bass_kernel_spmd(nc, maps, core_ids=list(range(n_cores)))
    return res.results


def kernel(**inputs):
    rs = run(inputs)
    y_prompt = np.stack([r["y_ctx"] for r in rs]).reshape(16, L_CTX, D)
    y_sample = np.stack([r["y_smp"] for r in rs]).reshape(8, L_SMP, D)
    new_k = np.stack([r["new_k"] for r in rs]).reshape(16, 1, L_CTX, 2, 128)
    new_v = np.stack([r["new_v"] for r in rs]).reshape(16, 1, L_CTX, 2, 128)
    new_ssm = np.stack([r["new_ssm"] for r in rs]).reshape(16, 1, 2, NH, HP, NS)
    return (y_prompt.astype(np.float32), y_sample.astype(np.float32), new_k.astype(np.float32),
            new_v.astype(np.float32), new_ssm.astype(np.float32))
```
